# Optimizing a Trainium2 kernel written in Bass

```python
import jax, jax.numpy as jnp
from jax import lax
import numpy as np

D_MODEL = 1024
BATCH = 2
SEQ = 8192
DEPTH = 1
DEC_BATCH = 128
DEC_SEQ = 1
PAST_LEN = 16384
PAGE_SIZE = 128

D_MIX = D_MODEL
D_RWKV = D_MIX // 2
D_ATTN = D_MIX - D_RWKV
HEAD_DIM = 64
H_RWKV = D_RWKV // HEAD_DIM
N_Q_HEADS = D_ATTN // HEAD_DIM
N_KV_HEADS = max(1, N_Q_HEADS // 4)
GQA_GROUP = N_Q_HEADS // N_KV_HEADS
D_KV = N_KV_HEADS * HEAD_DIM
D_DECAY_LORA = 32
D_AAA_LORA = 32
D_GATE_LORA = 96
D_SHIFT = 3 * D_RWKV + D_DECAY_LORA + D_AAA_LORA + D_GATE_LORA
D_IN_PROJ = D_SHIFT + D_ATTN + 2 * D_KV
WINDOW = 128
ATTN_BLOCK = 128
ROPE_DIM = HEAD_DIM // 4
ROPE_THETA = 500000.0
ATTN_SCALE = HEAD_DIM ** -0.5
D_FF = 4 * D_MODEL
RMS_EPS = 1e-6
LNX_EPS = 64e-5
NEG_INF = -1e30

kernel_name = 'hymba_rwkv7_swa_sink_decode_step'


def rmsnorm(x, w):
    xf = x.astype(jnp.float32)
    y = xf * lax.rsqrt(jnp.mean(xf * xf, axis=-1, keepdims=True) + RMS_EPS)
    return (y * w.astype(jnp.float32)).astype(x.dtype)


def partial_rope(x, pos):
    half = ROPE_DIM // 2
    inv_freq = jnp.power(ROPE_THETA, -jnp.arange(half, dtype=jnp.float32) * (2.0 / ROPE_DIM))
    ang = pos[:, None] * inv_freq[None, :]
    cos = jnp.cos(ang)[None, :, None, :]
    sin = jnp.sin(ang)[None, :, None, :]
    xf = x.astype(jnp.float32)
    x1 = xf[..., :half]
    x2 = xf[..., half:ROPE_DIM]
    out = jnp.concatenate([x1 * cos - x2 * sin, x2 * cos + x1 * sin, xf[..., ROPE_DIM:]], axis=-1)
    return out.astype(x.dtype)


def wkv_step(S, inp):
    r, w, k, v, a, b = inp
    sa = jnp.einsum('bhij,bhj->bhi', S, a)
    S = S * w[:, :, None, :] + sa[..., None] * b[:, :, None, :] + v[..., None] * k[:, :, None, :]
    y = jnp.einsum('bhij,bhj->bhi', S, r)
    return S, y


def rwkv7_mix(p, shift_prev, wkv_prev, prm):
    B, T, _ = p.shape
    f32 = jnp.float32
    pf = p.astype(f32)
    prev = jnp.concatenate([shift_prev.astype(f32), pf[:, :-1]], axis=1)
    xs = pf + (prev - pf) * prm['mu_shift'].astype(f32)
    c0, c1, c2 = D_RWKV, 2 * D_RWKV, 3 * D_RWKV
    c3 = c2 + D_DECAY_LORA
    c4 = c3 + D_AAA_LORA
    r = xs[..., :c0]
    k = xs[..., c0:c1]
    v = xs[..., c1:c2]
    wl = xs[..., c2:c3]
    al = xs[..., c3:c4]
    gl = xs[..., c4:]
    w_log = -jax.nn.softplus(-(prm['w0'].astype(f32) + jnp.tanh(wl) @ prm['w_decay_up'].astype(f32))) - 0.5
    decay = jnp.exp(-jnp.exp(w_log))
    a = jax.nn.sigmoid(prm['a0'].astype(f32) + al @ prm['w_a_up'].astype(f32))
    g = jax.nn.sigmoid(gl) @ prm['w_g_up'].astype(f32)
    heads = lambda t: t.reshape(B, T, H_RWKV, HEAD_DIM)
    kk = heads(k * prm['k_k'].astype(f32))
    kk = kk / jnp.maximum(jnp.sqrt(jnp.sum(kk * kk, axis=-1, keepdims=True)), 1e-12)
    k = heads(k * (1.0 + (a - 1.0) * prm['k_a'].astype(f32)))
    r = heads(r)
    v = heads(v)
    decay = heads(decay)
    a = heads(a)
    seq = tuple(jnp.moveaxis(t, 1, 0) for t in (r, decay, k, v, -kk, kk * a))
    wkv_new, ys = lax.scan(wkv_step, wkv_prev.astype(f32), seq)
    y = jnp.moveaxis(ys, 0, 1)
    mean = jnp.mean(y, axis=-1, keepdims=True)
    var = jnp.mean(jnp.square(y - mean), axis=-1, keepdims=True)
    y = ((y - mean) * lax.rsqrt(var + LNX_EPS)).reshape(B, T, D_RWKV)
    y = y * prm['ln_x_w'].astype(f32) + prm['ln_x_b'].astype(f32)
    bonus = jnp.sum(r * k * prm['r_k'].astype(f32), axis=-1, keepdims=True) * v
    y = (y + bonus.reshape(B, T, D_RWKV)) * g
    return y.astype(p.dtype), wkv_new, p[:, -1:]


def sink_attention(qb, kb, vb, mask, sinks):
    s = jnp.einsum('bnqhgd,bnkhd->bnhgqk', qb, kb, preferred_element_type=jnp.float32) * ATTN_SCALE
    s = jnp.where(mask[None, :, None, None], s, NEG_INF)
    sink = sinks.astype(jnp.float32).reshape(N_KV_HEADS, GQA_GROUP)[None, None, :, :, None, None]
    sink = jnp.broadcast_to(sink, s.shape[:-1] + (1,))
    p = jax.nn.softmax(jnp.concatenate([s, sink], axis=-1), axis=-1)[..., :-1]
    return jnp.einsum('bnhgqk,bnkhd->bnqhgd', p.astype(vb.dtype), vb)


def band_attention_prompt(q, k, v, sinks):
    B, T = q.shape[:2]
    nb = T // ATTN_BLOCK
    qb = q.reshape(B, nb, ATTN_BLOCK, N_KV_HEADS, GQA_GROUP, HEAD_DIM)
    kb = k.reshape(B, nb, ATTN_BLOCK, N_KV_HEADS, HEAD_DIM)
    vb = v.reshape(B, nb, ATTN_BLOCK, N_KV_HEADS, HEAD_DIM)
    kb2 = jnp.concatenate([jnp.concatenate([jnp.zeros_like(kb[:, :1]), kb[:, :-1]], axis=1), kb], axis=2)
    vb2 = jnp.concatenate([jnp.concatenate([jnp.zeros_like(vb[:, :1]), vb[:, :-1]], axis=1), vb], axis=2)
    qpos = jnp.arange(T).reshape(nb, ATTN_BLOCK)
    kpos = (jnp.arange(nb) * ATTN_BLOCK - ATTN_BLOCK)[:, None] + jnp.arange(2 * ATTN_BLOCK)[None, :]
    dq = qpos[:, :, None] - kpos[:, None, :]
    mask = (dq >= 0) & (dq < WINDOW) & (kpos[:, None, :] >= 0)
    o = sink_attention(qb, kb2, vb2, mask, sinks)
    return o.reshape(B, T, D_ATTN)


def window_attention_cached(q, k, v, k_past, v_past, sinks):
    B, T = q.shape[:2]
    W = k_past.shape[1]
    k_all = jnp.concatenate([k_past.astype(k.dtype), k], axis=1)
    v_all = jnp.concatenate([v_past.astype(v.dtype), v], axis=1)
    qpos = jnp.arange(T)
    kpos = jnp.arange(W + T) - W
    dq = qpos[:, None] - kpos[None, :]
    mask = ((dq >= 0) & (dq < WINDOW))[None]
    qb = q.reshape(B, 1, T, N_KV_HEADS, GQA_GROUP, HEAD_DIM)
    o = sink_attention(qb, k_all[:, None], v_all[:, None], mask, sinks)
    return o.reshape(B, T, D_ATTN), k_all[:, -W:], v_all[:, -W:]


def hybrid_layer(x, pos0, shift_prev, wkv_prev, k_past, v_past, prm):
    B, T, _ = x.shape
    h = rmsnorm(x, prm['norm_mix_w'])
    proj = h @ prm['w_in']
    o0 = D_SHIFT
    o1 = o0 + D_ATTN
    o2 = o1 + D_KV
    p_rwkv = proj[..., :o0]
    q = proj[..., o0:o1].reshape(B, T, N_Q_HEADS, HEAD_DIM)
    k = proj[..., o1:o2].reshape(B, T, N_KV_HEADS, HEAD_DIM)
    v = proj[..., o2:].reshape(B, T, N_KV_HEADS, HEAD_DIM)
    y_r, wkv_new, shift_new = rwkv7_mix(p_rwkv, shift_prev, wkv_prev, prm)
    pos = jnp.arange(T, dtype=jnp.float32) + pos0
    q = partial_rope(rmsnorm(q, prm['q_norm_w']), pos)
    k = partial_rope(rmsnorm(k, prm['k_norm_w']), pos)
    if k_past is None:
        y_a = band_attention_prompt(q, k, v, prm['sinks'])
        k_win, v_win = k[:, -WINDOW:], v[:, -WINDOW:]
    else:
        y_a, k_win, v_win = window_attention_cached(q, k, v, k_past, v_past, prm['sinks'])
    mix = jnp.concatenate([y_r, y_a.astype(y_r.dtype)], axis=-1)
    x = x + mix @ prm['w_out']
    hf = rmsnorm(x, prm['norm_ffn_w'])
    x = x + jnp.square(jax.nn.relu(hf @ prm['w_ffn_up'])) @ prm['w_ffn_down']
    return x, wkv_new, shift_new, k_win, v_win


def setup_inputs(seed: int = 0) -> dict:
    key = jax.random.key(seed)
    ks = jax.random.split(key, 26)
    f32 = jnp.float32
    nrm = lambda kk, shape, s: s * jax.random.normal(kk, shape, f32)
    L = DEPTH
    win_buf = min(WINDOW, PAST_LEN)
    return {
        'x_prompt': nrm(ks[0], (BATCH, SEQ, D_MODEL), 1.0),
        'x_sample': nrm(ks[1], (DEC_BATCH, DEC_SEQ, D_MODEL), 1.0),
        'state_wkv': nrm(ks[2], (L, DEC_BATCH, H_RWKV, HEAD_DIM, HEAD_DIM), 0.1),
        'state_shift': nrm(ks[3], (L, DEC_BATCH, 1, D_SHIFT), 1.0),
        'cache_k_win': nrm(ks[4], (L, DEC_BATCH, win_buf, N_KV_HEADS, HEAD_DIM), 1.0),
        'cache_v_win': nrm(ks[5], (L, DEC_BATCH, win_buf, N_KV_HEADS, HEAD_DIM), 1.0),
        'norm_mix_w': 1.0 + nrm(ks[6], (L, D_MODEL), 0.02),
        'w_in': nrm(ks[7], (L, D_MODEL, D_IN_PROJ), D_MODEL ** -0.5),
        'mu_shift': jax.random.uniform(ks[8], (L, D_SHIFT), f32),
        'w0': -1.0 + nrm(ks[9], (L, D_RWKV), 0.5),
        'w_decay_up': nrm(ks[10], (L, D_DECAY_LORA, D_RWKV), 0.1),
        'a0': nrm(ks[11], (L, D_RWKV), 0.1),
        'w_a_up': nrm(ks[12], (L, D_AAA_LORA, D_RWKV), 0.5 * D_AAA_LORA ** -0.5),
        'w_g_up': nrm(ks[13], (L, D_GATE_LORA, D_RWKV), D_GATE_LORA ** -0.5),
        'k_k': 0.85 + nrm(ks[14], (L, D_RWKV), 0.05),
        'k_a': 1.0 + nrm(ks[15], (L, D_RWKV), 0.05),
        'r_k': nrm(ks[16], (L, H_RWKV, HEAD_DIM), 0.1),
        'ln_x_w': 1.0 + nrm(ks[17], (L, D_RWKV), 0.02),
        'ln_x_b': nrm(ks[18], (L, D_RWKV), 0.02),
        'q_norm_w': 1.0 + nrm(ks[19], (L, HEAD_DIM), 0.02),
        'k_norm_w': 1.0 + nrm(ks[20], (L, HEAD_DIM), 0.02),
        'sinks': nrm(ks[21], (L, N_Q_HEADS), 0.5),
        'w_out': nrm(ks[22], (L, D_MIX, D_MODEL), D_MIX ** -0.5),
        'norm_ffn_w': 1.0 + nrm(ks[23], (L, D_MODEL), 0.02),
        'w_ffn_up': nrm(ks[24], (L, D_MODEL, D_FF), D_MODEL ** -0.5),
        'w_ffn_down': nrm(ks[25], (L, D_FF, D_MODEL), D_FF ** -0.5),
    }


def reference(x_prompt, x_sample, state_wkv, state_shift, cache_k_win, cache_v_win,
              norm_mix_w, w_in, mu_shift, w0, w_decay_up, a0, w_a_up, w_g_up, k_k, k_a, r_k,
              ln_x_w, ln_x_b, q_norm_w, k_norm_w, sinks, w_out, norm_ffn_w, w_ffn_up, w_ffn_down):
    yp = x_prompt
    ys = x_sample
    B = x_prompt.shape[0]
    wkv_p_l, sh_p_l, kw_p_l, vw_p_l = [], [], [], []
    wkv_s_l, sh_s_l, kw_s_l, vw_s_l = [], [], [], []
    for l in range(DEPTH):
        prm = dict(norm_mix_w=norm_mix_w[l], w_in=w_in[l], mu_shift=mu_shift[l], w0=w0[l],
                   w_decay_up=w_decay_up[l], a0=a0[l], w_a_up=w_a_up[l], w_g_up=w_g_up[l],
                   k_k=k_k[l], k_a=k_a[l], r_k=r_k[l], ln_x_w=ln_x_w[l], ln_x_b=ln_x_b[l],
                   q_norm_w=q_norm_w[l], k_norm_w=k_norm_w[l], sinks=sinks[l], w_out=w_out[l],
                   norm_ffn_w=norm_ffn_w[l], w_ffn_up=w_ffn_up[l], w_ffn_down=w_ffn_down[l])
        shift0 = jnp.zeros((B, 1, D_SHIFT), x_prompt.dtype)
        wkv0 = jnp.zeros((B, H_RWKV, HEAD_DIM, HEAD_DIM), jnp.float32)
        yp, wkv_p, sh_p, kw_p, vw_p = hybrid_layer(yp, 0, shift0, wkv0, None, None, prm)
        ys, wkv_s, sh_s, kw_s, vw_s = hybrid_layer(ys, PAST_LEN, state_shift[l], state_wkv[l],
                                                   cache_k_win[l], cache_v_win[l], prm)
        wkv_p_l.append(wkv_p); sh_p_l.append(sh_p); kw_p_l.append(kw_p); vw_p_l.append(vw_p)
        wkv_s_l.append(wkv_s); sh_s_l.append(sh_s); kw_s_l.append(kw_s); vw_s_l.append(vw_s)
    wkv_prompt = jnp.stack(wkv_p_l, axis=0)
    shift_prompt = jnp.stack(sh_p_l, axis=0)
    k_win_prompt = jnp.stack(kw_p_l, axis=0)
    v_win_prompt = jnp.stack(vw_p_l, axis=0)
    wkv_sample = jnp.stack(wkv_s_l, axis=0)
    shift_sample = jnp.stack(sh_s_l, axis=0)
    k_win_sample = jnp.stack(kw_s_l, axis=0)
    v_win_sample = jnp.stack(vw_s_l, axis=0)
    return (yp, ys, wkv_prompt, shift_prompt, k_win_prompt, v_win_prompt,
            wkv_sample, shift_sample, k_win_sample, v_win_sample)
```

```python
import os
import numpy as np
import ml_dtypes
from contextlib import ExitStack
import concourse.bass as bass
import concourse.mybir as mybir
from concourse.bass_utils import run_bass_kernel_spmd

F32 = mybir.dt.float32
BF16 = mybir.dt.bfloat16
ALU = mybir.AluOpType
AF = mybir.ActivationFunctionType
AX = mybir.AxisListType

D = 1024
NCORE = 8
SEG = 2048
NEXT = 8192
NT = NEXT // 128
MT0 = (NEXT - SEG) // 128
DS = 1696
DIN = 2464
CDEC = -float(np.exp(-0.5))


class Sem:
    def __init__(self, sem, step):
        self.sem = sem
        self.step = step
        self.cnt = 0


class Eng:
    def __init__(self, name, e, sem, skip_self):
        self.name = name
        self.e = e
        self.S = Sem(sem, 1)
        self.seen = {}
        self.skip_self = skip_self


class Buf:
    def __init__(self, t, name):
        self.t = t
        self.name = name
        self.wr = None
        self.rd = []
        self.dsem = None
        self.psum = False

    def __getitem__(self, k):
        return self.t[k]


class KB:
    def __init__(self, nc):
        self.nc = nc
        self.es = ExitStack()
        self.nsem = 0
        self.pe = Eng("pe", nc.tensor, self.sem("pe"), True)
        self.act = Eng("act", nc.scalar, self.sem("act"), False)
        self.dve = Eng("dve", nc.vector, self.sem("dve"), False)
        self.pool = Eng("pool", nc.gpsimd, self.sem("pool"), False)
        self.sp = Eng("sp", nc.sync, self.sem("sp"), True)
        self.engs = [self.pe, self.act, self.dve, self.pool, self.sp]
        self.dsems = []
        self.ninst = 0

    def sem(self, name):
        self.nsem += 1
        return self.es.enter_context(self.nc.semaphore(f"s_{name}_{self.nsem}"))

    def sb(self, name, shape, dt, stack=None):
        t = (stack or self.es).enter_context(self.nc.sbuf_tensor(name, list(shape), dt))
        return Buf(t, name)

    def ps(self, name, shape, dt):
        t = self.es.enter_context(self.nc.psum_tensor(name, list(shape), dt))
        b = Buf(t, name)
        b.psum = True
        return b

    def dram(self, name, shape, dt, kind):
        t = self.nc.dram_tensor(name, list(shape), dt, kind=kind)
        return Buf(t.ap(), name)

    def _waits(self, eng, deps):
        best = {}
        for (s, c) in deps:
            if c > best.get(s, 0):
                best[s] = c
        for s, c in best.items():
            if eng.seen.get(s, 0) >= c:
                continue
            eng.e.wait_ge(s.sem, c * s.step)
            eng.seen[s] = c
            self.ninst += 1

    def op(self, eng, fn, r=(), w=()):
        deps = []
        for b in r:
            if b.wr is not None:
                deps.append(b.wr)
            if b.psum:
                deps.extend(d for d in b.rd if d[0] is not eng.S)
        for b in w:
            if b.wr is not None:
                deps.append(b.wr)
            deps.extend(b.rd)
        if eng.skip_self:
            deps = [d for d in deps if d[0] is not eng.S]
        self._waits(eng, deps)
        ins = fn()
        eng.S.cnt += 1
        ins.then_inc(eng.S.sem, 1)
        self.ninst += 1
        me = (eng.S, eng.S.cnt)
        for b in r:
            b.rd.append(me)
        for b in w:
            b.wr = me
            b.rd = []
        return ins

    def dma(self, out_ap, in_ap, r=(), w=(), sembuf=None, q=None, **kw):
        q = q or self.sp
        sb = sembuf
        if sb.dsem is None:
            sb.dsem = Sem(self.sem("d_" + sb.name), 16)
            self.dsems.append(sb.dsem)
        deps = []
        for b in r:
            if b.wr is not None:
                deps.append(b.wr)
        for b in w:
            if b.wr is not None:
                deps.append(b.wr)
            deps.extend(b.rd)
        self._waits(q, deps)
        ins = q.e.dma_start(out=out_ap, in_=in_ap, **kw)
        sb.dsem.cnt += 1
        ins.then_inc(sb.dsem.sem, 16)
        self.ninst += 1
        me = (sb.dsem, sb.dsem.cnt)
        for b in r:
            b.rd.append(me)
        for b in w:
            b.wr = me
            b.rd = []

    def barrier(self):
        for e in self.engs:
            deps = [(o.S, o.S.cnt) for o in self.engs if o is not e and o.S.cnt > 0]
            deps += [(d, d.cnt) for d in self.dsems if d.cnt > 0]
            self._waits(e, deps)

    def finish(self):
        deps = [(o.S, o.S.cnt) for o in self.engs if o is not self.sp and o.S.cnt > 0]
        deps += [(d, d.cnt) for d in self.dsems if d.cnt > 0]
        self._waits(self.sp, deps)


class _Stop(Exception):
    pass


def build(dbg=False, stage=None):
    try:
        return _build(dbg, stage)
    except _Stop as e:
        return e.args[0], e.args[1]


def _build(dbg=False, stage=None):
    nc = bass.Bass("TRN2", target_bir_lowering=False)
    kb = KB(nc)
    PE, ACT, DVE, POOL = kb.pe, kb.act, kb.dve, kb.pool
    pe, act, dve, pool = nc.tensor, nc.scalar, nc.vector, nc.gpsimd

    def din(name, shape, dt=F32):
        return kb.dram(name, shape, dt, "ExternalInput")

    def dout(name, shape, dt=F32):
        return kb.dram(name, shape, dt, "ExternalOutput")

    xext = din("xext", [NEXT, D])
    w_in = din("w_in", [D, DIN])
    w_out = din("w_out", [D, D])
    w_up = din("w_up", [D, 4 * D])
    w_dn = din("w_dn", [4 * D, D])
    vec = {}
    for nm, n in [("norm_mix_w", D), ("mu_shift", DS), ("w0", 512), ("a0", 512), ("k_k", 512), ("k_a", 512),
                  ("r_k", 512), ("ln_x_w", 512), ("ln_x_b", 512), ("q_norm_w", 64), ("k_norm_w", 64),
                  ("sinks", 8), ("norm_ffn_w", D)]:
        vec[nm] = din(nm, [1, n])
    w_dec = din("w_decay_up", [32, 512])
    w_a = din("w_a_up", [32, 512])
    w_g = din("w_g_up", [96, 512])
    c_ident = din("c_ident", [128, 128])
    c_sha = din("c_sha", [128, 128])
    c_shb = din("c_shb", [128, 128])
    c_msu = din("c_msu", [128, 128])
    c_miu = din("c_miu", [128, 128])
    c_msl = din("c_msl", [128, 128])
    c_mp0 = din("c_mp0", [128, 128])
    c_cos = din("c_cos", [SEG + 128, 8])
    c_sin = din("c_sin", [SEG + 128, 8])

    y_main = dout("y_main", [SEG, D])
    o_wkv = dout("o_wkv", [8, 64, 64])
    o_shift = dout("o_shift", [1, DS])
    o_kwin = dout("o_kwin", [128, 128])
    o_vwin = dout("o_vwin", [128, 128])

    es = kb.es
    stacks = []

    def ck(n):
        if stage == n:
            kb.finish()
            for st_ in reversed(stacks):
                st_.close()
            kb.es.close()
            raise _Stop(nc, kb)
    psb = [kb.ps(f"psb{i}", [128, 512], F32) for i in range(6)]
    pst = [kb.ps(f"pst{i}", [128, 1024], BF16) for i in range(2)]
    ring = {"i": 0, "t": 0}

    def PS():
        b = psb[ring["i"] % 5]
        ring["i"] += 1
        return b

    def PST():
        b = pst[ring["t"] % 2]
        ring["t"] += 1
        return b

    ring.update({"f": 0, "b": 0})

    def PSF():
        b = psb[ring["f"] % 2]
        ring["f"] += 1
        return b

    def PSB():
        b = psb[2 + ring["b"] % 4]
        ring["b"] += 1
        return b

    def PSTF():
        return pst[0]

    def PSTB():
        return pst[1]

    def sbt(name, shape, dt=F32):
        return kb.sb(name, shape, dt)

    ident_f = sbt("ident_f", [128, 128])
    msu = sbt("msu", [128, 128])
    miu = sbt("miu", [128, 128])
    msl = sbt("msl", [128, 128])
    mp0_f = sbt("mp0_f", [128, 128])
    ident = sbt("ident", [128, 128], BF16)
    sha = sbt("sha", [128, 128], BF16)
    shb = sbt("shb", [128, 128], BF16)
    amc = sbt("amc", [128, 128], BF16)
    amp = sbt("amp", [128, 128], BF16)
    amp0 = sbt("amp0", [128, 128], BF16)
    ones_f = sbt("ones_f", [128, 2])
    for dst, src in [(ident_f, c_ident), (msu, c_msu), (miu, c_miu), (msl, c_msl), (mp0_f, c_mp0)]:
        kb.dma(dst[:], src[:, :], w=[dst], sembuf=dst)
    stg = sbt("cstg", [128, 2, 128])
    kb.dma(stg[:, 0, :], c_sha[:, :], w=[stg], sembuf=stg)
    kb.dma(stg[:, 1, :], c_shb[:, :], w=[stg], sembuf=stg)
    kb.op(DVE, lambda: dve.tensor_copy(out=ident[:], in_=ident_f[:]), r=[ident_f], w=[ident])
    kb.op(DVE, lambda: dve.tensor_copy(out=sha[:], in_=stg[:, 0, :]), r=[stg], w=[sha])
    kb.op(DVE, lambda: dve.tensor_copy(out=shb[:], in_=stg[:, 1, :]), r=[stg], w=[shb])
    kb.op(DVE, lambda: dve.tensor_copy(out=amc[:], in_=miu[:]), r=[miu], w=[amc])
    kb.op(DVE, lambda: dve.tensor_copy(out=amp[:], in_=msl[:]), r=[msl], w=[amp])
    kb.op(DVE, lambda: dve.tensor_copy(out=amp0[:], in_=mp0_f[:]), r=[mp0_f], w=[amp0])
    kb.op(POOL, lambda: pool.memset(ones_f[:], 1.0), w=[ones_f])

    nfw_b = sbt("b_nfw", [128, D])
    kb.dma(nfw_b[:], vec["norm_ffn_w"][0:1, 0:D].partition_broadcast(128), w=[nfw_b], sembuf=nfw_b)
    eps_vals = {"rms": 1e-6, "lnx": 64e-5, "one": 1.0}
    eps_t = {}
    for k_, v_ in eps_vals.items():
        t_ = sbt("eps_" + k_, [128, 1])
        kb.op(POOL, lambda t_=t_, v_=v_: pool.memset(t_[:], v_), w=[t_])
        eps_t[k_] = t_

    ck(0)
    esA = ExitStack()
    stacks.append(esA)

    def sa(name, shape, dt=F32):
        return kb.sb(name, shape, dt, stack=esA)

    def bvec(name, src, n, c0=0):
        t = sa("b_" + name, [128, n])
        kb.dma(t[:], src[0:1, c0:c0 + n].partition_broadcast(128), w=[t], sembuf=t)
        return t

    nmw_b = bvec("nmw", vec["norm_mix_w"], D)
    mu_b = bvec("mu", vec["mu_shift"], 1536)
    w0_b = bvec("w0", vec["w0"], 512)
    a0_b = bvec("a0", vec["a0"], 512)
    kk_b = bvec("kk", vec["k_k"], 512)
    ka_b = bvec("ka", vec["k_a"], 512)
    rk_b = bvec("rk", vec["r_k"], 512)
    lnw_b = bvec("lnw", vec["ln_x_w"], 512)
    lnb_b = bvec("lnb", vec["ln_x_b"], 512)
    qnw_b = bvec("qnw", vec["q_norm_w"], 64)
    knw_b = bvec("knw", vec["k_norm_w"], 64)
    snk_b = bvec("snk", vec["sinks"], 8)
    esink = sa("esink", [128, 8])
    kb.op(ACT, lambda: act.activation(out=esink[:], in_=snk_b[:], func=AF.Exp), r=[snk_b], w=[esink])
    mul = sa("mul", [96, 3])
    kb.op(POOL, lambda: pool.memset(mul[:], 0.0), w=[mul])
    for g, (c0, n) in enumerate([(1536, 32), (1568, 32), (1600, 96)]):
        kb.dma(mul[0:n, g:g + 1], vec["mu_shift"][0:1, c0:c0 + n].rearrange("o n -> n o"), w=[mul], sembuf=mul,
               allow_slow_non_contiguous=True)
    mix_scr = kb.dram("d_mix", [128, 8, SEG + 128], BF16, "ExternalOutput")

    lw = sa("lw", [96, 3, 512], BF16)
    win = sa("win", [128, 8, DIN], BF16)
    xb = [sa(f"xb{i}", [128, D]) for i in range(2)]
    w_in_v = w_in.t.rearrange("(kc p) n -> kc p n", p=128)
    HW = DIN // 4
    for kc in range(8):
        for hf_ in range(4):
            st = xb[hf_ % 2]
            kb.dma(st[:, 0:HW], w_in_v[kc][:, hf_ * HW:(hf_ + 1) * HW], w=[st], sembuf=st)
            kb.op(POOL, lambda: pool.tensor_copy(out=win[:, kc, hf_ * HW:(hf_ + 1) * HW], in_=st[:, 0:HW]), r=[st], w=[win])
    for g, (src, n) in enumerate([(w_dec, 32), (w_a, 32), (w_g, 96)]):
        st = xb[g % 2]
        kb.dma(st[0:n, 0:512], src[:, :], w=[st], sembuf=st)
        kb.op(DVE, lambda: dve.tensor_copy(out=lw[0:n, g, :], in_=st[0:n, 0:512]), r=[st], w=[lw])
    sm = [sa(f"sm{i}", [128, 8]) for i in range(12)]
    h_bf = sa("h_bf", [128, D], BF16)
    hT = [sa("hT0", [128, 8, 128], BF16)] * 2
    r_sb = sa("r_sb", [128, 512])
    k_sb = sa("k_sb", [128, 512])
    v_sb = sa("v_sb", [128, 512])
    lt = [sa(f"lt{i}", [96, 3, 128]) for i in range(2)]
    lt.append(lt[1])
    th_bf = sa("th_bf", [32, 128], BF16)
    al_bf = sa("al_bf", [32, 128], BF16)
    sg_bf = sa("sg_bf", [96, 128], BF16)
    f = [sa(f"f{i}", [128, 512]) for i in range(12)]
    s_sb, asig, kk, kkn, kmod, bv, g_sb = f[0], f[1], f[2], f[3], f[4], f[5], f[6]
    t1, t2, t3, t4, t5 = f[7], f[8], f[9], f[10], f[11]
    y_sb = sa("y_sb", [128, 512])
    mix_tok2 = [sa(f"mix_tok{i}", [128, D], BF16) for i in range(2)]
    mix_tok = mix_tok2[0]
    bon2 = [sa(f"bon{i}", [128, 512]) for i in range(2)]
    g2 = [g_sb, sa("g2b", [128, 512])]
    gt1 = sa("gt1", [128, 512])
    gt2 = sa("gt2", [128, 512])
    mixTt = [sa("mixTt0", [128, 8, 128], BF16)] * 2
    q_sb = s_sb
    kv_sb = sa("kv_sb", [128, 256])
    kf = sa("kf", [128, 128])
    qr_bf = sa("qr_bf", [128, 512], BF16)
    k_bf = sa("k_bf", [128, 128], BF16)
    qT = sa("qT", [64, 8, 128], BF16)
    cs_t = [sa(f"cs{i}", [128, 2, 8]) for i in range(2)]
    rp = [sa(f"rp{i}", [128, 8, 8]) for i in range(4)]
    esA2 = ExitStack()
    stacks.append(esA2)

    def sa2(name, shape, dt=F32):
        return kb.sb(name, shape, dt, stack=esA2)

    pm = [{c0: sa2(f"pm{i}_{c0}", [128, 512], BF16) for c0 in (0, 512, 1024)} for i in range(2)]
    for pd_ in pm:
        for p_ in pd_.values():
            kb.op(POOL, lambda p_=p_: pool.memset(p_[:], 0.0), w=[p_])
    lro = sa2("lro", [96, 3, 129])
    kb.op(POOL, lambda: pool.memset(lro[:], 0.0), w=[lro])
    A_tok2 = [sa2(f"A_tok{i}", [128, 512], BF16) for i in range(2)]
    Bh2 = [sa2(f"Bh{i}", [128, 512], BF16) for i in range(2)]
    Kh2 = [sa2(f"Kh{i}", [128, 512], BF16) for i in range(2)]
    Vb2 = [sa2(f"Vb{i}", [128, 512], BF16) for i in range(2)]
    Rt2 = [sa2(f"Rt{i}", [128, 512], BF16) for i in range(2)]
    tk = [sa2(f"tk{i}", [128, 512], BF16) for i in range(2)]
    FM2 = [sa2(f"FM{i}", [128, 4, 4, 128], BF16) for i in range(2)]
    gC2 = [sa2(f"gC{i}", [64, 8]) for i in range(2)]
    Gd = sa2("Gd", [64, 8, 64])
    nAB = [[[sa2(f"n{ab}{hg}{i}", [128, 4, 128], BF16) for i in range(2)] for ab in "AB"] for hg in range(2)]
    Xh = [sa2(f"Xh{i}", [128, 4, 128], BF16) for i in range(2)]
    P2b = sa2("P2b", [128, 4, 128], BF16)
    P3b = sa2("P3b", [128, 8, 128], BF16)
    P4b = sa2("P4b", [128, 8, 128], BF16)
    M0b = sa2("M0b", [128, 512], BF16)
    Wb = sa2("Wb", [128, 512], BF16)
    U0b = sa2("U0b", [128, 512], BF16)
    PsiT = sa2("PsiT", [64, 512], BF16)
    Q_sb = sa2("Q_sb", [64, 512])
    ZTb = sa2("ZTb", [64, 8, 128], BF16)
    H32 = sa2("H32", [64, 512])
    Hb = [sa2(f"Hb{i}", [64, 512], BF16) for i in range(2)]
    kb.op(POOL, lambda: pool.memset(Hb[0][:], 0.0), w=[Hb[0]])
    kb.op(POOL, lambda: pool.memset(H32[:], 0.0), w=[H32])
    kT = [sa2(f"kT{i}", [64, 2, 128], BF16) for i in range(2)]
    Va = [sa2(f"Va{i}", [128, 2, 65], BF16) for i in range(2)]
    for v_ in Va:
        kb.op(POOL, lambda v_=v_: pool.memset(v_[:], 1.0), w=[v_])
    Eb = [sa2(f"Eb{i}", [128, 512], BF16) for i in range(2)]
    Hfin = sa2("Hfin", [64, 8, 64])

    def v3(b, n=8):
        return b[:].rearrange("p (h d) -> p h d", h=n)

    def bc(ap_small, n, d):
        return ap_small.unsqueeze(2).to_broadcast([128, n, d])

    def bch(ap_vec, n, d):
        return ap_vec.unsqueeze(1).to_broadcast([128, n, d])

    def sigmoid_chain(src_ap, rbufs, tmp, dst, shape_ap=lambda b: b[:]):
        kb.op(ACT, lambda: act.activation(out=shape_ap(tmp), in_=src_ap, func=AF.Exp, scale=-1.0), r=rbufs, w=[tmp])
        kb.op(ACT, lambda: act.activation(out=shape_ap(tmp), in_=shape_ap(tmp), func=AF.Ln, bias=eps_t["one"][0:shape_ap(tmp).shape[0], 0:1]),
              r=[tmp, eps_t["one"]], w=[tmp])
        kb.op(ACT, lambda: act.activation(out=shape_ap(dst), in_=shape_ap(tmp), func=AF.Exp, scale=-1.0), r=[tmp], w=[dst])

    def rsq(src_buf, src_ap, dst_buf, dst_ap, scale, eps_key):
        n = dst_ap.shape[0]
        kb.op(ACT, lambda: act.activation(out=dst_ap, in_=src_ap, func=AF.Ln, scale=scale, bias=eps_t[eps_key][0:n, 0:1]),
              r=[src_buf, eps_t[eps_key]], w=[dst_buf])
        kb.op(ACT, lambda: act.activation(out=dst_ap, in_=dst_ap, func=AF.Exp, scale=-0.5), r=[dst_buf], w=[dst_buf])

    def load_x(i):
        b = xb[i % 2]
        kb.dma(b[:], xext[i * 128:(i + 1) * 128, :], w=[b], sembuf=b)

    def rwkv_pointwise(main, g_sb, alloc):
        psz, psza = alloc(), alloc()
        kb.op(PE, lambda: pe.matmul(psz[:], lhsT=th_bf[:], rhs=lw[0:32, 0, :], start=True, stop=True), r=[th_bf, lw], w=[psz])
        kb.op(PE, lambda: pe.matmul(psza[:], lhsT=al_bf[:], rhs=lw[0:32, 1, :], start=True, stop=True), r=[al_bf, lw], w=[psza])
        kb.op(DVE, lambda: dve.tensor_tensor(out=t1[:], in0=psz[:], in1=w0_b[:], op=ALU.add), r=[psz, w0_b], w=[t1])
        sigmoid_chain(t1[:], [t1], t2, s_sb)
        yield
        kb.op(DVE, lambda: dve.tensor_tensor(out=t1[:], in0=psza[:], in1=a0_b[:], op=ALU.add), r=[psza, a0_b], w=[t1])
        sigmoid_chain(t1[:], [t1], t2, asig)
        yield
        if main:
            psg = alloc()
            kb.op(PE, lambda: pe.matmul(psg[:], lhsT=sg_bf[:], rhs=lw[0:96, 2, :], start=True, stop=True), r=[sg_bf, lw], w=[psg])
            kb.op(ACT, lambda: act.copy(out=g_sb[:], in_=psg[:]), r=[psg], w=[g_sb])
        yield
        n2, rn = sm[2], sm[3]
        kb.op(DVE, lambda: dve.tensor_tensor(out=kk[:], in0=k_sb[:], in1=kk_b[:], op=ALU.mult), r=[k_sb, kk_b], w=[kk])
        kb.op(POOL, lambda: pool.tensor_tensor(out=t3[:], in0=kk[:], in1=kk[:], op=ALU.mult), r=[kk], w=[t3])
        kb.op(DVE, lambda: dve.tensor_reduce(out=n2[:], in_=v3(t3), axis=AX.X, op=ALU.add), r=[t3], w=[n2])
        kb.op(DVE, lambda: dve.tensor_scalar_max(out=n2[:], in0=n2[:], scalar1=1e-24), r=[n2], w=[n2])
        yield
        kb.op(ACT, lambda: act.activation(out=rn[:], in_=n2[:], func=AF.Ln), r=[n2], w=[rn])
        kb.op(ACT, lambda: act.activation(out=rn[:], in_=rn[:], func=AF.Exp, scale=-0.5), r=[rn], w=[rn])
        kb.op(DVE, lambda: dve.tensor_tensor(out=v3(kkn), in0=v3(kk), in1=bc(rn[:], 8, 64), op=ALU.mult), r=[kk, rn], w=[kkn])
        yield
        kb.op(DVE, lambda: dve.scalar_tensor_tensor(out=t4[:], in0=asig[:], scalar=-1.0, in1=ka_b[:], op0=ALU.add, op1=ALU.mult), r=[asig, ka_b], w=[t4])
        kb.op(DVE, lambda: dve.scalar_tensor_tensor(out=kmod[:], in0=t4[:], scalar=1.0, in1=k_sb[:], op0=ALU.add, op1=ALU.mult), r=[t4, k_sb], w=[kmod])
        kb.op(POOL, lambda: pool.tensor_tensor(out=bv[:], in0=kkn[:], in1=asig[:], op=ALU.mult), r=[kkn, asig], w=[bv])

    def bonus_pre(bon):
        bs = sm[9]
        kb.op(POOL, lambda: pool.tensor_tensor(out=t3[:], in0=r_sb[:], in1=kmod[:], op=ALU.mult), r=[r_sb, kmod], w=[t3])
        kb.op(POOL, lambda: pool.tensor_tensor(out=t3[:], in0=t3[:], in1=rk_b[:], op=ALU.mult), r=[t3, rk_b], w=[t3])
        kb.op(DVE, lambda: dve.tensor_reduce(out=bs[:], in_=v3(t3), axis=AX.X, op=ALU.add), r=[t3], w=[bs])
        kb.op(DVE, lambda: dve.tensor_tensor(out=v3(bon), in0=v3(v_sb), in1=bc(bs[:], 8, 64), op=ALU.mult), r=[v_sb, bs], w=[bon])

    def groupnorm_gate2(bon, g_sb, mix_tok):
        gs, gq, gm, gv, gr = sm[4], sm[5], sm[6], sm[7], sm[8]
        kb.op(DVE, lambda: dve.tensor_reduce(out=gs[:], in_=v3(y_sb), axis=AX.X, op=ALU.add), r=[y_sb], w=[gs])
        kb.op(POOL, lambda: pool.tensor_tensor(out=gt1[:], in0=y_sb[:], in1=y_sb[:], op=ALU.mult), r=[y_sb], w=[gt1])
        kb.op(DVE, lambda: dve.tensor_reduce(out=gq[:], in_=v3(gt1), axis=AX.X, op=ALU.add), r=[gt1], w=[gq])
        kb.op(DVE, lambda: dve.tensor_scalar_mul(out=gm[:], in0=gs[:], scalar1=1.0 / 64), r=[gs], w=[gm])
        kb.op(DVE, lambda: dve.tensor_tensor(out=gv[:], in0=gm[:], in1=gm[:], op=ALU.mult), r=[gm], w=[gv])
        kb.op(DVE, lambda: dve.scalar_tensor_tensor(out=gv[:], in0=gq[:], scalar=1.0 / 64, in1=gv[:], op0=ALU.mult, op1=ALU.subtract), r=[gq, gv], w=[gv])
        rsq(gv, gv[:], gr, gr[:], 1.0, "lnx")
        kb.op(DVE, lambda: dve.tensor_tensor(out=v3(gt2), in0=v3(y_sb), in1=bc(gm[:], 8, 64), op=ALU.subtract), r=[y_sb, gm], w=[gt2])
        kb.op(DVE, lambda: dve.tensor_tensor(out=v3(gt2), in0=v3(gt2), in1=bc(gr[:], 8, 64), op=ALU.mult), r=[gt2, gr], w=[gt2])
        kb.op(POOL, lambda: pool.tensor_tensor(out=gt2[:], in0=gt2[:], in1=lnw_b[:], op=ALU.mult), r=[gt2, lnw_b], w=[gt2])
        kb.op(POOL, lambda: pool.tensor_tensor(out=gt2[:], in0=gt2[:], in1=lnb_b[:], op=ALU.add), r=[gt2, lnb_b], w=[gt2])
        kb.op(POOL, lambda: pool.tensor_tensor(out=gt2[:], in0=gt2[:], in1=bon[:], op=ALU.add), r=[gt2, bon], w=[gt2])
        kb.op(POOL, lambda: pool.tensor_tensor(out=mix_tok[:, 0:512], in0=gt2[:], in1=g_sb[:], op=ALU.mult), r=[gt2, g_sb], w=[mix_tok])

    ck(1)
    load_x(0)
    def tile_gen(i):
        par = i % 2
        A_tok, Bh, Kh, Vb, Rt, FM, gC = A_tok2[par], Bh2[par], Kh2[par], Vb2[par], Rt2[par], FM2[par], gC2[par]
        mix_tok, bon, g_sb = mix_tok2[par], bon2[par], g2[par]
        par = i % 2
        main = i >= MT0
        halo = i == MT0 - 1
        mi = i - MT0
        if i + 1 < NT:
            load_x(i + 1)
        xt = xb[par]
        ss, rstd = sm[0], sm[1]
        yield
        kb.op(ACT, lambda: act.activation(out=h_bf[:], in_=xt[:], func=AF.Square, accum_out=ss[:, 0:1]), r=[xt], w=[h_bf, ss])
        rsq(ss, ss[:, 0:1], rstd, rstd[:, 0:1], 1.0 / D, "rms")
        kb.op(DVE, lambda: dve.scalar_tensor_tensor(out=h_bf[:], in0=xt[:], scalar=rstd[:, 0:1], in1=nmw_b[:], op0=ALU.mult, op1=ALU.mult),
              r=[xt, rstd, nmw_b], w=[h_bf])
        yield
        pt = PSTF()

        def fn_tr():
            ins = None
            for kc in range(8):
                ins = pe.transpose(out=pt[:, kc * 128:(kc + 1) * 128], in_=h_bf[:, kc * 128:(kc + 1) * 128], identity=ident[:])
            return ins
        kb.op(PE, fn_tr, r=[h_bf, ident], w=[pt])
        hTc = hT[par]
        kb.op(ACT, lambda: act.copy(out=hTc[:].rearrange("p a b -> p (a b)"), in_=pt[:]), r=[pt], w=[hTc])

        yield
        def proj(c0, c1, ps, stop=True):
            def fn():
                ins = None
                for kc in range(8):
                    ins = pe.matmul(ps[:, 0:c1 - c0], lhsT=hTc[:, kc, :], rhs=win[:, kc, c0:c1], start=(kc == 0), stop=(kc == 7))
                return ins
            kb.op(PE, fn, r=[hTc, win], w=[ps])

        psl = PSF()

        def fn_lo():
            ins = None
            for g, (c0, n) in enumerate([(1536, 32), (1568, 32), (1600, 96)]):
                if g == 2 and not (main or halo):
                    continue
                for kc in range(8):
                    ins = pe.matmul(psl[0:n, g * 128:(g + 1) * 128], lhsT=win[:, kc, c0:c0 + n], rhs=hTc[:, kc, :], start=(kc == 0), stop=(kc == 7))
            return ins
        kb.op(PE, fn_lo, r=[hTc, win], w=[psl])
        ng = 3 if main else 2
        kb.op(ACT, lambda: act.copy(out=lro[0:32, 0:2, 1:129], in_=psl[0:32, 0:256].rearrange("p (g t) -> p g t", g=2)), r=[psl], w=[lro])
        if main or halo:
            kb.op(ACT, lambda: act.copy(out=lro[0:96, 2, 1:129], in_=psl[0:96, 256:384]), r=[psl], w=[lro])
        if dbg and i == NT - 1:
            pass
        kb.op(DVE, lambda: dve.tensor_tensor(out=lt[0][:, 0:ng, :], in0=lro[:, 0:ng, 0:128], in1=lro[:, 0:ng, 1:129], op=ALU.subtract), r=[lro], w=[lt[0]])
        kb.op(DVE, lambda: dve.tensor_tensor(out=lt[0][:, 0:ng, :], in0=lt[0][:, 0:ng, :], in1=mul[:, 0:ng].unsqueeze(2).to_broadcast([96, ng, 128]), op=ALU.mult),
              r=[lt[0], mul], w=[lt[0]])
        kb.op(DVE, lambda: dve.tensor_tensor(out=lt[0][:, 0:ng, :], in0=lt[0][:, 0:ng, :], in1=lro[:, 0:ng, 1:129], op=ALU.add), r=[lt[0], lro], w=[lt[0]])
        if i == NT - 1 and not os.environ.get('NO_LAST_B'):
            psx = PSF()
            proj(1536, 1696, psx)
            kb.op(ACT, lambda: act.copy(out=t4[:, 0:160], in_=psx[:, 0:160]), r=[psx], w=[t4])
            kb.dma(o_shift[0:1, 1536:1696], t4[127:128, 0:160], r=[t4], sembuf=t4)
        kb.op(POOL, lambda: pool.tensor_copy(out=lro[:, :, 0:1], in_=lro[:, :, 128:129]), r=[lro], w=[lro])
        kb.op(ACT, lambda: act.activation(out=lt[1][0:32, 0, :], in_=lt[0][0:32, 0, :], func=AF.Exp, scale=2.0), r=[lt[0]], w=[lt[1]])
        kb.op(DVE, lambda: dve.tensor_scalar_add(out=lt[1][0:32, 0, :], in0=lt[1][0:32, 0, :], scalar1=1.0), r=[lt[1]], w=[lt[1]])
        kb.op(DVE, lambda: dve.reciprocal(out=lt[1][0:32, 0, :], in_=lt[1][0:32, 0, :]), r=[lt[1]], w=[lt[1]])
        kb.op(DVE, lambda: dve.tensor_scalar(out=th_bf[:], in0=lt[1][0:32, 0, :], scalar1=-2.0, scalar2=1.0, op0=ALU.mult, op1=ALU.add), r=[lt[1]], w=[th_bf])
        kb.op(POOL, lambda: pool.tensor_copy(out=al_bf[:], in_=lt[0][0:32, 1, :]), r=[lt[0]], w=[al_bf])
        if main:
            kb.op(ACT, lambda: act.activation(out=lt[2][0:96, 2, :], in_=lt[0][0:96, 2, :], func=AF.Exp, scale=-1.0), r=[lt[0]], w=[lt[2]])
            kb.op(DVE, lambda: dve.tensor_scalar_add(out=lt[2][0:96, 2, :], in0=lt[2][0:96, 2, :], scalar1=1.0), r=[lt[2]], w=[lt[2]])
            kb.op(DVE, lambda: dve.reciprocal(out=lt[2][0:96, 2, :], in_=lt[2][0:96, 2, :]), r=[lt[2]], w=[lt[2]])
            kb.op(POOL, lambda: pool.tensor_copy(out=sg_bf[:], in_=lt[2][0:96, 2, :]), r=[lt[2]], w=[sg_bf])
        cols = ([("r", 0, r_sb)] if (main or halo) else []) + [("k", 512, k_sb), ("v", 1024, v_sb)]
        pmc, pmp = pm[par], pm[1 - par]
        for g0 in range(0, len(cols), 2):
            grp = []
            for (nm, c0, dst) in cols[g0:g0 + 2]:
                ps = PSF()
                proj(c0, c0 + 512, ps)
                grp.append((c0, dst, ps))
            for (c0, dst, ps) in grp:
                if i == NT - 1:
                    rawb, rawap = {0: (xb[0], xb[0][:, 0:512]), 512: (xb[0], xb[0][:, 512:1024]), 1024: (t5, t5[:])}[c0]
                    kb.op(ACT, lambda: act.copy(out=rawap, in_=ps[:]), r=[ps], w=[rawb])
                    kb.dma(o_shift[0:1, c0:c0 + 512], rawap[127:128, :], r=[rawb], sembuf=rawb)
                kb.op(DVE, lambda: dve.tensor_tensor(out=pmc[c0][:], in0=ps[:], in1=mu_b[:, c0:c0 + 512], op=ALU.mult),
                      r=[ps, mu_b], w=[pmc[c0]])
            for (c0, dst, ps) in grp:
                def fn_l(ps=ps, c0=c0):
                    pe.matmul(ps[:], lhsT=sha[:], rhs=pmc[c0][:], start=False, stop=False, skip_group_check=True)
                    return pe.matmul(ps[:], lhsT=shb[:], rhs=pmp[c0][:], start=False, stop=True, skip_group_check=True)
                kb.op(PE, fn_l, r=[pmc[c0], pmp[c0], sha, shb], w=[ps])
                kb.op(ACT, lambda: act.copy(out=dst[:], in_=ps[:]), r=[ps], w=[dst])
        if not main:
            pass

        yield
        if main or halo:
            ai = mi + 1
            cst = cs_t[par]
            kb.dma(cst[:, 0, :], c_cos[ai * 128:(ai + 1) * 128, :], w=[cst], sembuf=cst)
            kb.dma(cst[:, 1, :], c_sin[ai * 128:(ai + 1) * 128, :], w=[cst], sembuf=cst)
            pkv = PSF()
            proj(2208, 2464, pkv)
            kb.op(ACT, lambda: act.copy(out=kv_sb[:], in_=pkv[:, 0:256]), r=[pkv], w=[kv_sb])

            def qk_norm_rope(src_buf, src_ap3, nh, wb, out_buf, out_ap3, tmpA, tmpB):
                ssq, rq = sm[10], sm[11]
                kb.op(POOL, lambda: pool.tensor_tensor(out=tmpA, in0=src_ap3, in1=src_ap3, op=ALU.mult), r=[src_buf], w=[t4])
                kb.op(DVE, lambda: dve.tensor_reduce(out=ssq[:, 0:nh], in_=tmpA, axis=AX.X, op=ALU.add), r=[t4], w=[ssq])
                rsq(ssq, ssq[:, 0:nh], rq, rq[:, 0:nh], 1.0 / 64, "rms")
                kb.op(DVE, lambda: dve.tensor_tensor(out=tmpA, in0=src_ap3, in1=bc(rq[:, 0:nh], nh, 64), op=ALU.mult), r=[src_buf, rq], w=[t4])
                kb.op(POOL, lambda: pool.tensor_tensor(out=out_ap3, in0=tmpA, in1=bch(wb[:], nh, 64), op=ALU.mult), r=[t4, wb], w=[out_buf])
                kb.op(POOL, lambda: pool.tensor_copy(out=rp[0][:, 0:nh, :], in_=out_ap3[:, :, 0:8]), r=[out_buf], w=[rp[0]])
                kb.op(POOL, lambda: pool.tensor_copy(out=rp[1][:, 0:nh, :], in_=out_ap3[:, :, 8:16]), r=[out_buf], w=[rp[1]])
                cosb = cst[:, 0, :].unsqueeze(1).to_broadcast([128, nh, 8])
                sinb = cst[:, 1, :].unsqueeze(1).to_broadcast([128, nh, 8])
                kb.op(DVE, lambda: dve.tensor_tensor(out=rp[2][:, 0:nh, :], in0=rp[0][:, 0:nh, :], in1=cosb, op=ALU.mult), r=[rp[0], cst], w=[rp[2]])
                kb.op(DVE, lambda: dve.tensor_tensor(out=rp[3][:, 0:nh, :], in0=rp[1][:, 0:nh, :], in1=sinb, op=ALU.mult), r=[rp[1], cst], w=[rp[3]])
                kb.op(DVE, lambda: dve.tensor_tensor(out=out_ap3[:, :, 0:8], in0=rp[2][:, 0:nh, :], in1=rp[3][:, 0:nh, :], op=ALU.subtract), r=[rp[2], rp[3]], w=[out_buf])
                kb.op(DVE, lambda: dve.tensor_tensor(out=rp[2][:, 0:nh, :], in0=rp[1][:, 0:nh, :], in1=cosb, op=ALU.mult), r=[rp[1], cst], w=[rp[2]])
                kb.op(DVE, lambda: dve.tensor_tensor(out=rp[3][:, 0:nh, :], in0=rp[0][:, 0:nh, :], in1=sinb, op=ALU.mult), r=[rp[0], cst], w=[rp[3]])
                kb.op(DVE, lambda: dve.tensor_tensor(out=out_ap3[:, :, 8:16], in0=rp[2][:, 0:nh, :], in1=rp[3][:, 0:nh, :], op=ALU.add), r=[rp[2], rp[3]], w=[out_buf])

            k3 = kv_sb[:, 0:128].rearrange("p (h d) -> p h d", h=2)
            kf3 = kf[:].rearrange("p (h d) -> p h d", h=2)
            qk_norm_rope(kv_sb, k3, 2, knw_b, kf, kf3, t4[:, 0:128].rearrange("p (h d) -> p h d", h=2), None)
            kb.op(POOL, lambda: pool.tensor_copy(out=k_bf[:], in_=kf[:]), r=[kf], w=[k_bf])
            Vc, Vp = Va[par], Va[1 - par]
            kb.op(POOL, lambda: pool.tensor_copy(out=Vc[:, :, 0:64], in_=kv_sb[:, 128:256].rearrange("p (h d) -> p h d", h=2)), r=[kv_sb], w=[Vc])
            kTc, kTp = kT[par], kT[1 - par]
            pt = PSTF()

            def fn_kt():
                ins = None
                for kvh in range(2):
                    ins = pe.transpose(out=pt[0:64, kvh * 128:(kvh + 1) * 128], in_=k_bf[:, kvh * 64:(kvh + 1) * 64], identity=ident[:])
                return ins
            kb.op(PE, fn_kt, r=[k_bf, ident], w=[pt])
            kb.op(ACT, lambda: act.copy(out=kTc[:].rearrange("p a b -> p (a b)"), in_=pt[0:64, 0:256]), r=[pt], w=[kTc])
            if i == NT - 1:
                kb.dma(o_kwin[:, :], kf[:], r=[kf], sembuf=kf)
                kb.dma(o_vwin[:, :], kv_sb[:, 128:256], r=[kv_sb], sembuf=kv_sb)
        yield
        if main:
            pq = PSF()
            proj(1696, 2208, pq)
            kb.op(ACT, lambda: act.copy(out=q_sb[:], in_=pq[:]), r=[pq], w=[q_sb])
            qk_norm_rope(q_sb, v3(q_sb), 8, qnw_b, t5, v3(t5), v3(t4), None)
            kb.op(POOL, lambda: pool.tensor_copy(out=qr_bf[:], in_=t5[:]), r=[t5], w=[qr_bf])
            pt = PSTF()

            def fn_qt():
                ins = None
                for h in range(8):
                    ins = pe.transpose(out=pt[0:64, h * 128:(h + 1) * 128], in_=qr_bf[:, h * 64:(h + 1) * 64], identity=ident[:])
                return ins
            kb.op(PE, fn_qt, r=[qr_bf, ident], w=[pt])
            kb.op(ACT, lambda: act.copy(out=qT[:].rearrange("p a b -> p (a b)"), in_=pt[0:64, :]), r=[pt], w=[qT])
            yield
            mprev = amp0 if mi == 0 else amp
            for kvh in range(2):
                Es = {}
                for which, (kTx, mk) in enumerate([(kTp, mprev), (kTc, amc)]):
                    pss = PSF()
                    kb.op(PE, lambda: pe.matmul(pss[:], lhsT=kTx[:, kvh, :], rhs=qT[:, 4 * kvh:4 * kvh + 4, :].rearrange("p a b -> p (a b)"),
                                                start=True, stop=True), r=[kTx, qT], w=[pss])
                    E = Eb[which]
                    kb.op(ACT, lambda: act.activation(out=E[:], in_=pss[:], func=AF.Exp, scale=0.125), r=[pss], w=[E])
                    kb.op(POOL, lambda: pool.tensor_tensor(out=E[:].rearrange("p (j t) -> p j t", j=4), in0=E[:].rearrange("p (j t) -> p j t", j=4),
                                                           in1=mk[:].unsqueeze(1).to_broadcast([128, 4, 128]), op=ALU.mult), r=[E, mk], w=[E])
                    Es[which] = E
                po = PSF()

                def fn_o():
                    ins = None
                    for j in range(4):
                        pe.matmul(po[:, j * 65:(j + 1) * 65], lhsT=Es[0][:, j * 128:(j + 1) * 128], rhs=Vp[:, kvh, :], start=True, stop=False)
                        ins = pe.matmul(po[:, j * 65:(j + 1) * 65], lhsT=Es[1][:, j * 128:(j + 1) * 128], rhs=Vc[:, kvh, :], start=False, stop=True)
                    return ins
                kb.op(PE, fn_o, r=[Es[0], Es[1], Vp, Vc], w=[po])
                den, rden = sm[10], sm[11]
                po3 = po[:, 0:260].rearrange("p (j d) -> p j d", j=4)
                kb.op(DVE, lambda: dve.tensor_tensor(out=den[:, 0:4], in0=po3[:, :, 64], in1=esink[:, 4 * kvh:4 * kvh + 4], op=ALU.add), r=[po, esink], w=[den])
                kb.op(DVE, lambda: dve.reciprocal(out=rden[:, 0:4], in_=den[:, 0:4]), r=[den], w=[rden])
                kb.op(DVE, lambda: dve.tensor_tensor(out=mix_tok[:, 512 + 256 * kvh:512 + 256 * (kvh + 1)].rearrange("p (j d) -> p j d", j=4),
                                                     in0=po3[:, :, 0:64], in1=bc(rden[:, 0:4], 4, 64), op=ALU.mult), r=[po, rden], w=[mix_tok])
                yield
        yield
        yield from rwkv_pointwise(main, g_sb, PSF)
        if main:
            bonus_pre(bon)
        yield
        eNeg, eEx, eSuf, eL = t1, t2, t3, t5
        psl1 = PSF()
        kb.op(PE, lambda: pe.matmul(psl1[:], lhsT=miu[:], rhs=s_sb[:], start=True, stop=True), r=[miu, s_sb], w=[psl1])
        kb.op(ACT, lambda: act.activation(out=eNeg[:], in_=psl1[:], func=AF.Exp, scale=-CDEC), r=[psl1], w=[eNeg])
        if main:
            kb.op(ACT, lambda: act.activation(out=eL[:], in_=psl1[:], func=AF.Exp, scale=CDEC), r=[psl1], w=[eL])
        psl2 = PSF()
        kb.op(PE, lambda: pe.matmul(psl2[:], lhsT=msu[:], rhs=s_sb[:], start=True, stop=True), r=[msu, s_sb], w=[psl2])
        kb.op(ACT, lambda: act.activation(out=eEx[:], in_=psl2[:], func=AF.Exp, scale=CDEC), r=[psl2], w=[eEx])
        psl3 = PSF()
        kb.op(PE, lambda: pe.matmul(psl3[:], lhsT=msl[:], rhs=s_sb[:], start=True, stop=True), r=[msl, s_sb], w=[psl3])
        kb.op(ACT, lambda: act.activation(out=eSuf[:], in_=psl3[:], func=AF.Exp, scale=CDEC), r=[psl3], w=[eSuf])
        psgc = PSF()

        def fn_gc():
            ins = None
            for h in range(8):
                ins = pe.matmul(psgc[0:64, 2 * h:2 * h + 2], lhsT=s_sb[:, h * 64:(h + 1) * 64], rhs=ones_f[:, 0:2], start=True, stop=True)
            return ins
        kb.op(PE, fn_gc, r=[s_sb, ones_f], w=[psgc])
        kb.op(ACT, lambda: act.activation(out=gC[:], in_=psgc[0:64, 0:16:2], func=AF.Exp, scale=CDEC), r=[psgc], w=[gC])
        Bt_tok, Kt_tok = tk[0], tk[1]
        kb.op(DVE, lambda: dve.tensor_tensor(out=Kt_tok[:], in0=kmod[:], in1=eNeg[:], op=ALU.mult), r=[kmod, eNeg], w=[Kt_tok])
        kb.op(POOL, lambda: pool.tensor_tensor(out=Bt_tok[:], in0=bv[:], in1=eNeg[:], op=ALU.mult), r=[bv, eNeg], w=[Bt_tok])
        kb.op(DVE, lambda: dve.scalar_tensor_tensor(out=A_tok[:], in0=kkn[:], scalar=-1.0, in1=eEx[:], op0=ALU.mult, op1=ALU.mult), r=[kkn, eEx], w=[A_tok])
        yield
        kb.op(POOL, lambda: pool.tensor_tensor(out=Kh[:], in0=kmod[:], in1=eSuf[:], op=ALU.mult), r=[kmod, eSuf], w=[Kh])
        kb.op(DVE, lambda: dve.tensor_tensor(out=Bh[:], in0=bv[:], in1=eSuf[:], op=ALU.mult), r=[bv, eSuf], w=[Bh])
        kb.op(ACT, lambda: act.copy(out=Vb[:], in_=v_sb[:]), r=[v_sb], w=[Vb])
        if main:
            kb.op(DVE, lambda: dve.tensor_tensor(out=Rt[:], in0=r_sb[:], in1=eL[:], op=ALU.mult), r=[r_sb, eL], w=[Rt])
        yield
        arrs = [(0, A_tok), (1, Bt_tok), (2, Kt_tok)] + ([(3, Rt)] if main else [])
        for half in range(2):
            pt = PSTF()

            def fn_fm(half=half, pt=pt):
                ins = None
                for hb in (2 * half, 2 * half + 1):
                    for (ai, src) in arrs:
                        off = ((hb % 2) * 4 + ai) * 128
                        ins = pe.transpose(out=pt[:, off:off + 128], in_=src[:, hb * 128:(hb + 1) * 128], identity=ident[:])
                return ins
            kb.op(PE, fn_fm, r=[s for (_, s) in arrs] + [ident], w=[pt])
            if main:
                kb.op(ACT, lambda half=half, pt=pt: act.copy(out=FM[:, 2 * half:2 * half + 2, :, :].rearrange("p a b c -> p (a b c)"), in_=pt[:]), r=[pt], w=[FM])
            else:
                for hh in range(2):
                    hb = 2 * half + hh
                    kb.op(ACT, lambda hb=hb, hh=hh, pt=pt: act.copy(out=FM[:, hb, 0:3, :].rearrange("p a b -> p (a b)"), in_=pt[:, (hh * 4) * 128:(hh * 4 + 3) * 128]), r=[pt], w=[FM])

        def slot(h):
            return 4 * (h % 2) + h // 2

        def fmh(h, ai):
            base = 64 * (h % 2)
            return FM[base:base + 64, h // 2, ai, :]

        yield 'F'
        def hs_of(hg):
            return [hg + 2 * j for j in range(4)]

        def pmat(hg, ai_l, ai_r, ps):
            def fn():
                ins = None
                for j, h in enumerate(hs_of(hg)):
                    ins = pe.matmul(ps[:, j * 128:(j + 1) * 128], lhsT=fmh(h, ai_l), rhs=fmh(h, ai_r), start=True, stop=True)
                return ins
            kb.op(PE, fn, r=[FM], w=[ps])

        def mask_to(ps, mk, dst_buf, dst_ap):
            kb.op(DVE, lambda: dve.tensor_tensor(out=dst_ap, in0=ps[:].rearrange("p (j t) -> p j t", j=4),
                                                 in1=mk[:].unsqueeze(1).to_broadcast([128, 4, 128]), op=ALU.mult), r=[ps, mk], w=[dst_buf])
        for hg in range(2):
            nA, nB = nAB[hg][0], nAB[hg][1]
            p1 = PSB()
            pmat(hg, 1, 0, p1)
            mask_to(p1, msu, nA[0], nA[0][:])
            kb.op(POOL, lambda: pool.tensor_tensor(out=Xh[hg][:], in0=nA[0][:], in1=ident[:].unsqueeze(1).to_broadcast([128, 4, 128]), op=ALU.add),
                  r=[nA[0], ident], w=[Xh[hg]])
            pn = PSB()
            pmat(hg, 0, 1, pn)
            mask_to(pn, msl, nB[0], nB[0][:])
        yield

        def do_p2(hg):
                p2 = PSB()
                pmat(hg, 2, 0, p2)
                mask_to(p2, msu, P2b, P2b[:])
                psM = PSB()

                def fn_m0():
                    ins = None
                    for j, h in enumerate(hs_of(hg)):
                        ins = pe.matmul(psM[:, j * 64:(j + 1) * 64], lhsT=P2b[:, j, :], rhs=Vb[:, h * 64:(h + 1) * 64], start=True, stop=True)
                    return ins
                kb.op(PE, fn_m0, r=[P2b, Vb], w=[psM])
                kb.op(ACT, lambda: act.copy(out=M0b[:].rearrange("p (j q v) -> p j q v", j=4, q=2)[:, :, hg, :], in_=psM[:, 0:256].rearrange("p (j v) -> p j v", j=4)),
                      r=[psM], w=[M0b])

        def do_p34(hg):
            p3 = PSB()
            pmat(hg, 1, 3, p3)
            mask_to(p3, miu, P3b, P3b[:, 4 * hg:4 * hg + 4, :])
            p4 = PSB()
            pmat(hg, 2, 3, p4)
            mask_to(p4, miu, P4b, P4b[:, 4 * hg:4 * hg + 4, :])
        cur = 0
        for step in range(7):
            for hg in range(2):
                nA_, nB_ = nAB[hg][0], nAB[hg][1]
                if step > 0:
                    NTn = nB_[cur]
                    pc = PSB()

                    def fn_c():
                        ins = None
                        for j in range(4):
                            ins = pe.matmul(pc[:, j * 128:(j + 1) * 128], lhsT=NTn[:, j, :], rhs=Xh[hg][:, j, :], start=True, stop=True)
                        return ins
                    kb.op(PE, fn_c, r=[NTn, Xh[hg]], w=[pc])
                    kb.op(DVE, lambda: dve.tensor_tensor(out=Xh[hg][:], in0=pc[:].rearrange("p (j t) -> p j t", j=4),
                                                         in1=Xh[hg][:], op=ALU.add), r=[pc, Xh[hg]], w=[Xh[hg]])
                if step < 6:
                    last = step == 5
                    Np, NTp = nA_[cur], nB_[cur]
                    Np2, NTp2 = nA_[1 - cur], nB_[1 - cur]
                    if not last:
                        pa = PSB()

                        def fn_a():
                            ins = None
                            for j in range(4):
                                ins = pe.matmul(pa[:, j * 128:(j + 1) * 128], lhsT=NTp[:, j, :], rhs=Np[:, j, :], start=True, stop=True)
                            return ins
                        kb.op(PE, fn_a, r=[Np, NTp], w=[pa])
                    pb_ = PSB()

                    def fn_b():
                        ins = None
                        for j in range(4):
                            ins = pe.matmul(pb_[:, j * 128:(j + 1) * 128], lhsT=Np[:, j, :], rhs=NTp[:, j, :], start=True, stop=True)
                        return ins
                    kb.op(PE, fn_b, r=[Np, NTp], w=[pb_])
                    if not last:
                        kb.op(ACT, lambda: act.copy(out=Np2[:].rearrange("p j t -> p (j t)"), in_=pa[:]), r=[pa], w=[Np2])
                    kb.op(ACT, lambda: act.copy(out=NTp2[:].rearrange("p j t -> p (j t)"), in_=pb_[:]), r=[pb_], w=[NTp2])
            cur = 1 - cur
            if step in (0, 1):
                do_p2(step)
            elif main and step in (2, 3):
                do_p34(step - 2)
            yield
        psW, psU = PSB(), PSB()

        def fn_w():
            ins = None
            for h in range(8):
                ins = pe.matmul(psW[:, h * 64:(h + 1) * 64], lhsT=Xh[h % 2][:, h // 2, :], rhs=A_tok[:, h * 64:(h + 1) * 64], start=True, stop=True)
            return ins

        def fn_u():
            ins = None
            for h in range(8):
                ins = pe.matmul(psU[:, h * 64:(h + 1) * 64], lhsT=Xh[h % 2][:, h // 2, :], rhs=M0b[:, h * 64:(h + 1) * 64], start=True, stop=True)
            return ins
        kb.op(PE, fn_w, r=[Xh[0], Xh[1], A_tok], w=[psW])
        kb.op(PE, fn_u, r=[Xh[0], Xh[1], M0b], w=[psU])
        kb.op(ACT, lambda: act.copy(out=Wb[:], in_=psW[:]), r=[psW], w=[Wb])
        kb.op(DVE, lambda: dve.tensor_copy(out=U0b[:], in_=psU[:]), r=[psU], w=[U0b])
        yield
        psPsi, psQ = PSB(), PSB()

        def fn_psi():
            ins = None
            for h in range(8):
                ins = pe.matmul(psPsi[0:64, h * 64:(h + 1) * 64], lhsT=Wb[:, h * 64:(h + 1) * 64], rhs=Bh[:, h * 64:(h + 1) * 64], start=True, stop=True)
            return ins

        def fn_q():
            ins = None
            for h in range(8):
                pe.matmul(psQ[0:64, h * 64:(h + 1) * 64], lhsT=Bh[:, h * 64:(h + 1) * 64], rhs=U0b[:, h * 64:(h + 1) * 64], start=True, stop=False)
                ins = pe.matmul(psQ[0:64, h * 64:(h + 1) * 64], lhsT=Kh[:, h * 64:(h + 1) * 64], rhs=Vb[:, h * 64:(h + 1) * 64], start=False, stop=True)
            return ins
        kb.op(PE, fn_psi, r=[Wb, Bh], w=[psPsi])
        kb.op(PE, fn_q, r=[Bh, U0b, Kh, Vb], w=[psQ])
        kb.op(DVE, lambda: dve.tensor_tensor(out=Gd[:], in0=ident_f[0:64, 0:64].unsqueeze(1).to_broadcast([64, 8, 64]),
                                               in1=gC[:].unsqueeze(2).to_broadcast([64, 8, 64]), op=ALU.mult), r=[ident_f, gC], w=[Gd])
        kb.op(DVE, lambda: dve.tensor_tensor(out=PsiT[:], in0=psPsi[0:64, :], in1=Gd[:].rearrange("p h k -> p (h k)"), op=ALU.add), r=[psPsi, Gd], w=[PsiT])
        kb.op(ACT, lambda: act.copy(out=Q_sb[:], in_=psQ[0:64, :]), r=[psQ], w=[Q_sb])
        Hc, Hn = Hb[par], Hb[1 - par]
        if main:
            for half in range(2):
                pz = PSB()

                def fn_z(half=half, pz=pz):
                    ins = None
                    for j in range(4):
                        h = 4 * half + j
                        pe.matmul(pz[0:64, j * 128:(j + 1) * 128], lhsT=Wb[:, h * 64:(h + 1) * 64], rhs=P3b[:, slot(h), :], start=True, stop=False)
                        ins = pe.matmul(pz[0:64, j * 128:(j + 1) * 128], lhsT=Rt[:, h * 64:(h + 1) * 64], rhs=ident[:], start=False, stop=True)
                    return ins
                kb.op(PE, fn_z, r=[Wb, P3b, Rt, ident], w=[pz])
                kb.op(ACT, lambda half=half, pz=pz: act.copy(out=ZTb[:, 4 * half:4 * half + 4, :].rearrange("p j t -> p (j t)"), in_=pz[0:64, :]), r=[pz], w=[ZTb])
            psY = PSB()

            def fn_y():
                ins = None
                for h in range(8):
                    sl = slice(h * 64, (h + 1) * 64)
                    pe.matmul(psY[:, sl], lhsT=P3b[:, slot(h), :], rhs=U0b[:, sl], start=True, stop=False)
                    pe.matmul(psY[:, sl], lhsT=P4b[:, slot(h), :], rhs=Vb[:, sl], start=False, stop=False)
                    ins = pe.matmul(psY[:, sl], lhsT=ZTb[:, h, :], rhs=Hc[:, sl], start=False, stop=True)
                return ins
            kb.op(PE, fn_y, r=[P3b, U0b, P4b, Vb, ZTb, Hc], w=[psY])
            kb.op(ACT, lambda: act.copy(out=y_sb[:], in_=psY[:]), r=[psY], w=[y_sb])
        yield
        psH = PSB()

        def fn_h():
            ins = None
            for h in range(8):
                sl = slice(h * 64, (h + 1) * 64)
                ins = pe.matmul(psH[0:64, sl], lhsT=PsiT[:, sl], rhs=Hc[:, sl], start=True, stop=True)
            return ins
        kb.op(PE, fn_h, r=[PsiT, Hc], w=[psH])
        kb.op(DVE, lambda: dve.tensor_tensor(out=H32[:], in0=psH[0:64, :], in1=Q_sb[:], op=ALU.add), r=[psH, Q_sb], w=[H32])
        kb.op(ACT, lambda: act.copy(out=Hn[:], in_=H32[:]), r=[H32], w=[Hn])

        if main:
            yield
            groupnorm_gate2(bon, g_sb, mix_tok)
            yield

        if main:
            pt = PSTB()

            def fn_mt():
                ins = None
                for kc in range(8):
                    ins = pe.transpose(out=pt[:, kc * 128:(kc + 1) * 128], in_=mix_tok[:, kc * 128:(kc + 1) * 128], identity=ident[:])
                return ins
            kb.op(PE, fn_mt, r=[mix_tok, ident], w=[pt])
            mtt = mixTt[par]
            kb.op(ACT, lambda: act.copy(out=mtt[:].rearrange("p a b -> p (a b)"), in_=pt[:]), r=[pt], w=[mtt])
            kb.dma(mix_scr[:, :, mi * 128:(mi + 1) * 128], mtt[:], r=[mtt], w=[mix_scr], sembuf=mtt)

        if i == NT - 1:
            pass


    def run_tiles():
        prev = None
        for i in range(NT):
            g = tile_gen(i)
            while True:
                tag = next(g)
                for _ in range(int(os.environ.get('RB', '3'))):
                    if prev is not None:
                        try:
                            next(prev)
                        except StopIteration:
                            prev = None
                if tag == 'F':
                    break
            while prev is not None:
                try:
                    next(prev)
                except StopIteration:
                    prev = None
            prev = g
        while prev is not None:
            try:
                next(prev)
            except StopIteration:
                prev = None
    run_tiles()

    ck(19)
    pz = PS()

    def fn_ht():
        ins = None
        for h in range(8):
            ins = pe.matmul(pz[0:64, h * 64:(h + 1) * 64], lhsT=H32[:, h * 64:(h + 1) * 64], rhs=ident_f[0:64, 0:64], start=True, stop=True)
        return ins
    kb.op(PE, fn_ht, r=[H32, ident_f], w=[pz])
    kb.op(ACT, lambda: act.copy(out=Hfin[:].rearrange("p a b -> p (a b)"), in_=pz[0:64, :]), r=[pz], w=[Hfin])
    kb.dma(o_wkv.t.rearrange("h v k -> v h k"), Hfin[:], r=[Hfin], sembuf=Hfin)

    kb.barrier()
    esA2.close()
    stacks.remove(esA2)
    esS = ExitStack()
    stacks.append(esS)

    def ss_(name, shape, dt=F32):
        return kb.sb(name, shape, dt, stack=esS)
    def proj_s(c0, c1, ps, hTc):
        def fn():
            ins = None
            for kc in range(8):
                ins = pe.matmul(ps[:, 0:c1 - c0], lhsT=hTc[:, kc, :], rhs=win[:, kc, c0:c1], start=(kc == 0), stop=(kc == 7))
            return ins
        kb.op(PE, fn, r=[hTc, win], w=[ps])


    def qk_norm_rope_s(src_buf, src_ap3, nh, wb, out_buf, out_ap3, tmpA, cst):
        ssq, rq = sm[10], sm[11]
        kb.op(POOL, lambda: pool.tensor_tensor(out=tmpA, in0=src_ap3, in1=src_ap3, op=ALU.mult), r=[src_buf], w=[t4])
        kb.op(DVE, lambda: dve.tensor_reduce(out=ssq[:, 0:nh], in_=tmpA, axis=AX.X, op=ALU.add), r=[t4], w=[ssq])
        rsq(ssq, ssq[:, 0:nh], rq, rq[:, 0:nh], 1.0 / 64, "rms")
        kb.op(DVE, lambda: dve.tensor_tensor(out=tmpA, in0=src_ap3, in1=bc(rq[:, 0:nh], nh, 64), op=ALU.mult), r=[src_buf, rq], w=[t4])
        kb.op(POOL, lambda: pool.tensor_tensor(out=out_ap3, in0=tmpA, in1=bch(wb[:], nh, 64), op=ALU.mult), r=[t4, wb], w=[out_buf])
        kb.op(POOL, lambda: pool.tensor_copy(out=rp[0][:, 0:nh, :], in_=out_ap3[:, :, 0:8]), r=[out_buf], w=[rp[0]])
        kb.op(POOL, lambda: pool.tensor_copy(out=rp[1][:, 0:nh, :], in_=out_ap3[:, :, 8:16]), r=[out_buf], w=[rp[1]])
        cosb = cst[:, 0, :].unsqueeze(1).to_broadcast([128, nh, 8])
        sinb = cst[:, 1, :].unsqueeze(1).to_broadcast([128, nh, 8])
        kb.op(DVE, lambda: dve.tensor_tensor(out=rp[2][:, 0:nh, :], in0=rp[0][:, 0:nh, :], in1=cosb, op=ALU.mult), r=[rp[0], cst], w=[rp[2]])
        kb.op(DVE, lambda: dve.tensor_tensor(out=rp[3][:, 0:nh, :], in0=rp[1][:, 0:nh, :], in1=sinb, op=ALU.mult), r=[rp[1], cst], w=[rp[3]])
        kb.op(DVE, lambda: dve.tensor_tensor(out=out_ap3[:, :, 0:8], in0=rp[2][:, 0:nh, :], in1=rp[3][:, 0:nh, :], op=ALU.subtract), r=[rp[2], rp[3]], w=[out_buf])
        kb.op(DVE, lambda: dve.tensor_tensor(out=rp[2][:, 0:nh, :], in0=rp[1][:, 0:nh, :], in1=cosb, op=ALU.mult), r=[rp[1], cst], w=[rp[2]])
        kb.op(DVE, lambda: dve.tensor_tensor(out=rp[3][:, 0:nh, :], in0=rp[0][:, 0:nh, :], in1=sinb, op=ALU.mult), r=[rp[0], cst], w=[rp[3]])
        kb.op(DVE, lambda: dve.tensor_tensor(out=out_ap3[:, :, 8:16], in0=rp[2][:, 0:nh, :], in1=rp[3][:, 0:nh, :], op=ALU.add), r=[rp[2], rp[3]], w=[out_buf])


    xs_pad = din("xs_pad", [128, D])
    st_wkv = din("st_wkv", [128, 4096])
    st_shift = din("st_shift", [128, DS])
    c_k = din("cache_k", [16, 128, 128])
    c_v = din("cache_v", [16, 128, 128])
    c_cos_s = din("c_cos_s", [128, 8])
    c_sin_s = din("c_sin_s", [128, 8])
    s_wkv = dout("s_wkv", [128, 4096])
    s_shift = dout("s_shift", [16, DS])
    s_kwin = dout("s_kwin", [16, 128, 128])
    s_vwin = dout("s_vwin", [16, 128, 128])
    vscr = dout("vscr", [6, 16, 512])
    yscr = dout("yscr", [2, 16, 512])

    praw = ss_("praw", [128, DS])
    spv = ss_("spv", [128, DS])
    mul_b = ss_("mul_b", [128, 160])
    vecp = ss_("vecp", [128, 6, 64])
    y_p = ss_("y_p", [128, 64])
    sa_p = ss_("sa_p", [128, 8])
    Kst = ss_("Kst", [128, 8, 128])
    Kw_bf = ss_("Kw_bf", [128, 16, 128], BF16)
    Vaug = ss_("Vaug", [128, 16, 2, 65], BF16)
    KT_bf = ss_("KT_bf", [64, 32, 128], BF16)
    E_bf = ss_("E_bf", [128, 128], BF16)
    OT_sb = ss_("OT_sb", [65, 128])
    esk = ss_("esk", [128, 1])
    den_p = ss_("den_p", [128, 1])
    ya_p = ss_("ya_p", [128, 64])

    kb.dma(spv[:], st_shift[:, :], w=[spv], sembuf=spv)
    kb.dma(mul_b[:], vec["mu_shift"][0:1, 1536:1696].partition_broadcast(128), w=[mul_b], sembuf=mul_b)
    for t_ in range(16):
        kb.dma(esk[8 * t_:8 * t_ + 8, 0:1], vec["sinks"][0:1, 0:8].rearrange("o h -> h o"), w=[esk], sembuf=esk, allow_slow_non_contiguous=True)
    kb.op(ACT, lambda: act.activation(out=esk[:], in_=esk[:], func=AF.Exp), r=[esk], w=[esk])
    kb.op(POOL, lambda: pool.memset(Vaug[:], 1.0), w=[Vaug])

    for (dst_w, src_c) in [(s_kwin, c_k), (s_vwin, c_v)]:
        kb.dma(dst_w[:, 0:127, :], src_c[:, 1:128, :], w=[dst_w], sembuf=dst_w)
    xt = xb[0]
    hTc = hT[0]
    kb.dma(xt[:], xs_pad[:, :], w=[xt], sembuf=xt)
    ss, rstd = sm[0], sm[1]
    kb.op(ACT, lambda: act.activation(out=h_bf[:], in_=xt[:], func=AF.Square, accum_out=ss[:, 0:1]), r=[xt], w=[h_bf, ss])
    rsq(ss, ss[:, 0:1], rstd, rstd[:, 0:1], 1.0 / D, "rms")
    kb.op(DVE, lambda: dve.scalar_tensor_tensor(out=h_bf[:], in0=xt[:], scalar=rstd[:, 0:1], in1=nmw_b[:], op0=ALU.mult, op1=ALU.mult),
          r=[xt, rstd, nmw_b], w=[h_bf])
    pt = PST()

    def fn_trs():
        ins = None
        for kc in range(8):
            ins = pe.transpose(out=pt[:, kc * 128:(kc + 1) * 128], in_=h_bf[:, kc * 128:(kc + 1) * 128], identity=ident[:])
        return ins
    kb.op(PE, fn_trs, r=[h_bf, ident], w=[pt])
    kb.op(ACT, lambda: act.copy(out=hTc[:].rearrange("p a b -> p (a b)"), in_=pt[:]), r=[pt], w=[hTc])

    for (c0, dst) in [(0, r_sb), (512, k_sb), (1024, v_sb)]:
        ps = PS()
        proj_s(c0, c0 + 512, ps, hTc)
        kb.op(ACT, lambda: act.copy(out=praw[:, c0:c0 + 512], in_=ps[:]), r=[ps], w=[praw])
        kb.op(DVE, lambda: dve.tensor_tensor(out=dst[:], in0=spv[:, c0:c0 + 512], in1=praw[:, c0:c0 + 512], op=ALU.subtract), r=[spv, praw], w=[dst])
        kb.op(POOL, lambda: pool.tensor_tensor(out=dst[:], in0=dst[:], in1=mu_b[:, c0:c0 + 512], op=ALU.mult), r=[dst, mu_b], w=[dst])
        kb.op(DVE, lambda: dve.tensor_tensor(out=dst[:], in0=dst[:], in1=praw[:, c0:c0 + 512], op=ALU.add), r=[dst, praw], w=[dst])
    psx = PS()
    proj_s(1536, 1696, psx, hTc)
    kb.op(ACT, lambda: act.copy(out=praw[:, 1536:1696], in_=psx[:, 0:160]), r=[psx], w=[praw])
    xl = t1[:, 0:160]
    kb.op(DVE, lambda: dve.tensor_tensor(out=xl, in0=spv[:, 1536:1696], in1=praw[:, 1536:1696], op=ALU.subtract), r=[spv, praw], w=[t1])
    kb.op(POOL, lambda: pool.tensor_tensor(out=xl, in0=xl, in1=mul_b[:], op=ALU.mult), r=[t1, mul_b], w=[t1])
    kb.op(DVE, lambda: dve.tensor_tensor(out=xl, in0=xl, in1=praw[:, 1536:1696], op=ALU.add), r=[t1, praw], w=[t1])
    kb.dma(s_shift[:, :], praw[0:16, :], r=[praw], sembuf=praw)
    kb.op(ACT, lambda: act.activation(out=t2[:, 0:32], in_=t1[:, 0:32], func=AF.Exp, scale=2.0), r=[t1], w=[t2])
    kb.op(DVE, lambda: dve.tensor_scalar_add(out=t2[:, 0:32], in0=t2[:, 0:32], scalar1=1.0), r=[t2], w=[t2])
    kb.op(DVE, lambda: dve.reciprocal(out=t2[:, 0:32], in_=t2[:, 0:32]), r=[t2], w=[t2])
    kb.op(DVE, lambda: dve.tensor_scalar(out=qr_bf[:, 0:32], in0=t2[:, 0:32], scalar1=-2.0, scalar2=1.0, op0=ALU.mult, op1=ALU.add), r=[t2], w=[qr_bf])
    kb.op(POOL, lambda: pool.tensor_copy(out=qr_bf[:, 32:64], in_=t1[:, 32:64]), r=[t1], w=[qr_bf])
    kb.op(ACT, lambda: act.activation(out=t2[:, 64:160], in_=t1[:, 64:160], func=AF.Exp, scale=-1.0), r=[t1], w=[t2])
    kb.op(DVE, lambda: dve.tensor_scalar_add(out=t2[:, 64:160], in0=t2[:, 64:160], scalar1=1.0), r=[t2], w=[t2])
    kb.op(DVE, lambda: dve.reciprocal(out=t2[:, 64:160], in_=t2[:, 64:160]), r=[t2], w=[t2])
    kb.op(POOL, lambda: pool.tensor_copy(out=qr_bf[:, 64:160], in_=t2[:, 64:160]), r=[t2], w=[qr_bf])
    pt = PST()

    def fn_trl():
        pe.transpose(out=pt[0:32, 0:128], in_=qr_bf[:, 0:32], identity=ident[:])
        pe.transpose(out=pt[0:32, 128:256], in_=qr_bf[:, 32:64], identity=ident[:])
        return pe.transpose(out=pt[0:96, 256:384], in_=qr_bf[:, 64:160], identity=ident[:])
    kb.op(PE, fn_trl, r=[qr_bf, ident], w=[pt])
    kb.op(ACT, lambda: act.copy(out=th_bf[:], in_=pt[0:32, 0:128]), r=[pt], w=[th_bf])
    kb.op(ACT, lambda: act.copy(out=al_bf[:], in_=pt[0:32, 128:256]), r=[pt], w=[al_bf])
    kb.op(ACT, lambda: act.copy(out=sg_bf[:], in_=pt[0:96, 256:384]), r=[pt], w=[sg_bf])
    for _ in rwkv_pointwise(True, g_sb, PS):
        pass
    bonus_pre(bon2[0])
    kb.op(ACT, lambda: act.activation(out=t1[:], in_=s_sb[:], func=AF.Exp, scale=CDEC), r=[s_sb], w=[t1])
    kb.op(DVE, lambda: dve.tensor_scalar_mul(out=t2[:], in0=kkn[:], scalar1=-1.0), r=[kkn], w=[t2])
    for a_, src in enumerate([t1, t2, bv, v_sb, kmod, r_sb]):
        kb.dma(vscr[a_, :, :], src[0:16, :], r=[src], w=[vscr], sembuf=src)
    kb.dma(vecp[:], vscr.t.rearrange("a t (h d) -> (t h) a d", h=8), r=[vscr], w=[vecp], sembuf=vecp)
    Sc, Tm = t3, t4
    w_b = vecp[:, 0, :].unsqueeze(1).to_broadcast([128, 8, 64])
    a_b = vecp[:, 1, :].unsqueeze(1).to_broadcast([128, 8, 64])
    b_b = vecp[:, 2, :].unsqueeze(1).to_broadcast([128, 8, 64])
    k_b = vecp[:, 4, :].unsqueeze(1).to_broadcast([128, 8, 64])
    r_b = vecp[:, 5, :].unsqueeze(1).to_broadcast([128, 8, 64])
    chunk_bufs = [(t3, t4), (t1, t2)]
    kb.dma(t3[:], st_wkv[:, 0:512], w=[t3], sembuf=t3)
    for c_ in range(8):
        Sc, Tm = chunk_bufs[c_ % 2]
        if c_ + 1 < 8:
            nxt = chunk_bufs[(c_ + 1) % 2][0]
            kb.dma(nxt[:], st_wkv[:, (c_ + 1) * 512:(c_ + 2) * 512], w=[nxt], sembuf=nxt)
        kb.op(DVE, lambda: dve.tensor_tensor(out=v3(Tm), in0=v3(Sc), in1=a_b, op=ALU.mult), r=[Sc, vecp], w=[Tm])
        kb.op(DVE, lambda: dve.tensor_reduce(out=sa_p[:], in_=v3(Tm), axis=AX.X, op=ALU.add), r=[Tm], w=[sa_p])
        kb.op(POOL, lambda: pool.tensor_tensor(out=v3(Sc), in0=v3(Sc), in1=w_b, op=ALU.mult), r=[Sc, vecp], w=[Sc])
        kb.op(DVE, lambda: dve.tensor_tensor(out=v3(Tm), in0=bc(sa_p[:], 8, 64), in1=b_b, op=ALU.mult), r=[sa_p, vecp], w=[Tm])
        kb.op(POOL, lambda: pool.tensor_tensor(out=Sc[:], in0=Sc[:], in1=Tm[:], op=ALU.add), r=[Sc, Tm], w=[Sc])
        kb.op(DVE, lambda: dve.tensor_tensor(out=v3(Tm), in0=bc(vecp[:, 3, c_ * 8:(c_ + 1) * 8], 8, 64), in1=k_b, op=ALU.mult), r=[vecp], w=[Tm])
        kb.op(POOL, lambda: pool.tensor_tensor(out=Sc[:], in0=Sc[:], in1=Tm[:], op=ALU.add), r=[Sc, Tm], w=[Sc])
        kb.dma(s_wkv[:, c_ * 512:(c_ + 1) * 512], Sc[:], r=[Sc], sembuf=Sc)
        kb.op(DVE, lambda: dve.tensor_tensor(out=v3(Tm), in0=v3(Sc), in1=r_b, op=ALU.mult), r=[Sc, vecp], w=[Tm])
        kb.op(DVE, lambda: dve.tensor_reduce(out=y_p[:, c_ * 8:(c_ + 1) * 8], in_=v3(Tm), axis=AX.X, op=ALU.add), r=[Tm], w=[y_p])
    kb.dma(yscr.t[0].rearrange("t (h d) -> (t h) d", h=8), y_p[:], r=[y_p], w=[yscr], sembuf=y_p)
    kb.op(POOL, lambda: pool.memset(y_sb[:], 0.0), w=[y_sb])
    kb.dma(y_sb[0:16, :], yscr[0, :, :], r=[yscr], w=[y_sb], sembuf=y_sb)
    groupnorm_gate2(bon2[0], g_sb, mix_tok)
    cst = cs_t[0]
    kb.dma(cst[:, 0, :], c_cos_s[:, :], w=[cst], sembuf=cst)
    kb.dma(cst[:, 1, :], c_sin_s[:, :], w=[cst], sembuf=cst)
    pkv = PS()
    proj_s(2208, 2464, pkv, hTc)
    kb.op(ACT, lambda: act.copy(out=kv_sb[:], in_=pkv[:, 0:256]), r=[pkv], w=[kv_sb])
    k3 = kv_sb[:, 0:128].rearrange("p (h d) -> p h d", h=2)
    kf3 = kf[:].rearrange("p (h d) -> p h d", h=2)
    qk_norm_rope_s(kv_sb, k3, 2, knw_b, kf, kf3, t4[:, 0:128].rearrange("p (h d) -> p h d", h=2), cst)
    pq = PS()
    proj_s(1696, 2208, pq, hTc)
    kb.op(ACT, lambda: act.copy(out=q_sb[:], in_=pq[:]), r=[pq], w=[q_sb])
    qk_norm_rope_s(q_sb, v3(q_sb), 8, qnw_b, t5, v3(t5), v3(t4), cst)
    kb.op(POOL, lambda: pool.tensor_copy(out=qr_bf[:], in_=t5[:]), r=[t5], w=[qr_bf])
    pt = PST()

    def fn_qts():
        ins = None
        for h in range(8):
            ins = pe.transpose(out=pt[0:64, h * 128:(h + 1) * 128], in_=qr_bf[:, h * 64:(h + 1) * 64], identity=ident[:])
        return ins
    kb.op(PE, fn_qts, r=[qr_bf, ident], w=[pt])
    kb.op(ACT, lambda: act.copy(out=qT[:].rearrange("p a b -> p (a b)"), in_=pt[0:64, :]), r=[pt], w=[qT])
    for (dst_w, src_c, new_buf, new_ap) in [(s_kwin, c_k, kf, kf[0:16, :]), (s_vwin, c_v, kv_sb, kv_sb[0:16, 128:256])]:
        kb.dma(dst_w[:, 127, :], new_ap, r=[new_buf], w=[dst_w], sembuf=new_buf)
    for hf_ in range(2):
        kb.dma(Kst[:], s_kwin.t[8 * hf_:8 * hf_ + 8].rearrange("t k c -> k t c"), r=[s_kwin], w=[Kst], sembuf=Kst)
        kb.op(DVE, lambda: dve.tensor_copy(out=Kw_bf[:, 8 * hf_:8 * hf_ + 8, :], in_=Kst[:]), r=[Kst], w=[Kw_bf])
    for hf_ in range(2):
        kb.dma(Kst[:], s_vwin.t[8 * hf_:8 * hf_ + 8].rearrange("t k c -> k t c"), r=[s_vwin], w=[Kst], sembuf=Kst)
        kb.op(DVE, lambda: dve.tensor_copy(out=Vaug[:, 8 * hf_:8 * hf_ + 8, :, 0:64], in_=Kst[:].rearrange("p t (h d) -> p t h d", h=2)), r=[Kst], w=[Vaug])
    for g4 in range(4):
        pt = PST()

        def fn_kts():
            ins = None
            for j in range(8):
                t_, kvh = divmod(8 * g4 + j, 2)
                ins = pe.transpose(out=pt[0:64, j * 128:(j + 1) * 128], in_=Kw_bf[:, t_, kvh * 64:(kvh + 1) * 64], identity=ident[:])
            return ins
        kb.op(PE, fn_kts, r=[Kw_bf, ident], w=[pt])
        kb.op(ACT, lambda: act.copy(out=KT_bf[:, 8 * g4:8 * g4 + 8, :].rearrange("p a b -> p (a b)"), in_=pt[0:64, :]), r=[pt], w=[KT_bf])
    psS = PS()

    def fn_sc():
        ins = None
        for t_ in range(16):
            for kvh in range(2):
                ins = pe.matmul(psS[:, t_ * 8 + 4 * kvh:t_ * 8 + 4 * kvh + 4], lhsT=KT_bf[:, 2 * t_ + kvh, :], rhs=qT[:, 4 * kvh:4 * kvh + 4, t_], start=True, stop=True)
        return ins
    kb.op(PE, fn_sc, r=[KT_bf, qT], w=[psS])
    kb.op(ACT, lambda: act.activation(out=E_bf[:], in_=psS[:, 0:128], func=AF.Exp, scale=0.125), r=[psS], w=[E_bf])
    psO = PS()

    def fn_os():
        ins = None
        for t_ in range(16):
            for kvh in range(2):
                c0_ = t_ * 8 + 4 * kvh
                ins = pe.matmul(psO[0:65, c0_:c0_ + 4], lhsT=Vaug[:, t_, kvh, :], rhs=E_bf[:, c0_:c0_ + 4], start=True, stop=True)
        return ins
    kb.op(PE, fn_os, r=[Vaug, E_bf], w=[psO])
    kb.op(ACT, lambda: act.copy(out=OT_sb[:], in_=psO[0:65, 0:128]), r=[psO], w=[OT_sb])
    psO2 = PS()
    kb.op(PE, lambda: pe.matmul(psO2[:, 0:65], lhsT=OT_sb[:], rhs=ident_f[0:65, 0:65], start=True, stop=True), r=[OT_sb, ident_f], w=[psO2])
    kb.op(DVE, lambda: dve.tensor_tensor(out=den_p[:], in0=psO2[:, 64:65], in1=esk[:], op=ALU.add), r=[psO2, esk], w=[den_p])
    kb.op(DVE, lambda: dve.reciprocal(out=den_p[:], in_=den_p[:]), r=[den_p], w=[den_p])
    kb.op(DVE, lambda: dve.tensor_scalar_mul(out=ya_p[:], in0=psO2[:, 0:64], scalar1=den_p[:, 0:1]), r=[psO2, den_p], w=[ya_p])
    kb.dma(yscr.t[1].rearrange("t (h d) -> (t h) d", h=8), ya_p[:], r=[ya_p], w=[yscr], sembuf=ya_p)
    kb.op(POOL, lambda: pool.memset(t3[:], 0.0), w=[t3])
    kb.dma(t3[0:16, :], yscr[1, :, :], r=[yscr], w=[t3], sembuf=t3)
    kb.op(POOL, lambda: pool.tensor_copy(out=mix_tok[:, 512:1024], in_=t3[:]), r=[t3], w=[mix_tok])
    pt = PST()

    def fn_mts():
        ins = None
        for kc in range(8):
            ins = pe.transpose(out=pt[:, kc * 128:(kc + 1) * 128], in_=mix_tok[:, kc * 128:(kc + 1) * 128], identity=ident[:])
        return ins
    kb.op(PE, fn_mts, r=[mix_tok, ident], w=[pt])
    mtt = mixTt[0]
    kb.op(ACT, lambda: act.copy(out=mtt[:].rearrange("p a b -> p (a b)"), in_=pt[:]), r=[pt], w=[mtt])
    kb.dma(mix_scr[:, :, SEG:SEG + 128], mtt[:], r=[mtt], w=[mix_scr], sembuf=mtt)
    ck(50)
    kb.barrier()
    esS.close()
    stacks.remove(esS)
    esA.close()
    stacks.remove(esA)
    esC = ExitStack()
    stacks.append(esC)

    def sc(name, shape, dt=F32):
        return kb.sb(name, shape, dt, stack=esC)

    TS = 256
    NSUP = SEG // TS
    wout = sc("wout", [128, 8, D], BF16)
    wup = sc("wup", [128, 8, 4 * D], BF16)
    wdn = sc("wdn", [128, 32, D], BF16)
    stgc = [sc(f"stgc{i}", [128, D]) for i in range(2)]
    xr = [sc(f"xr{i}", [128, D]) for i in range(2)]
    x1s = sc("x1s", [128, TS // 128, D])
    mixs = sc("mixs", [128, 8, TS], BF16)
    hfb = sc("hfb", [128, D], BF16)
    hfT = sc("hfT", [128, 8, TS], BF16)
    actb = sc("actb", [128, 32, TS], BF16)
    rl = [sc(f"rl{i}", [128, TS]) for i in range(2)]
    smc = [sc(f"smc{i}", [128, 1]) for i in range(2)]
    wparts = {"wout": [], "wup": [], "wdn": []}
    stage_bufs = stgc + xr
    cast_engs = [(DVE, dve), (POOL, pool), (DVE, dve), (ACT, act)]
    nld = [0]

    def load_piece(dst, src_v, kc, c0):
        st = stage_bufs[nld[0] % 4]
        eng_, e_ = cast_engs[nld[0] % 4]
        nld[0] += 1
        part = Buf(dst.t, f"{dst.name}_{kc}_{c0}")
        wparts[dst.name].append(part)
        kb.dma(st[:], src_v[kc][:, c0:c0 + D], w=[st], sembuf=st)
        if eng_ is ACT:
            kb.op(ACT, lambda: act.copy(out=dst[:, kc, c0:c0 + D], in_=st[:]), r=[st], w=[part])
        else:
            kb.op(eng_, lambda: e_.tensor_copy(out=dst[:, kc, c0:c0 + D], in_=st[:]), r=[st], w=[part])
    wo_v = w_out.t.rearrange("(kc p) n -> kc p n", p=128)
    wu_v = w_up.t.rearrange("(kc p) n -> kc p n", p=128)
    wd_v = w_dn.t.rearrange("(kc p) n -> kc p n", p=128)
    for kc in range(8):
        load_piece(wout, wo_v, kc, 0)
    for kc in range(8):
        for c0 in range(0, 4 * D, D):
            load_piece(wup, wu_v, kc, c0)
    wdn_todo = list(range(32))
    o_ys = dout("ys", [16, D])
    jobs = [(su * TS, TS // 128, None) for su in range(NSUP)] + [(SEG, 1, "s")]
    for (col0, ntl, kind) in jobs:
        ncols = ntl * 128
        kb.dma(mixs[:, :, 0:ncols], mix_scr[:, :, col0:col0 + ncols], r=[mix_scr], w=[mixs], sembuf=mixs)
        for tt in range(ntl):
            gt = col0 // 128 + tt
            xrb = xr[gt % 2]
            if kind is None:
                kb.dma(xrb[:], xext[(MT0 + gt) * 128:(MT0 + gt + 1) * 128, :], w=[xrb], sembuf=xrb)
            else:
                kb.dma(xrb[:], xs_pad[:, :], w=[xrb], sembuf=xrb)
            for half in range(2):
                po = PS()

                def fn_wo():
                    ins = None
                    for kc in range(8):
                        ins = pe.matmul(po[:], lhsT=mixs[:, kc, tt * 128:(tt + 1) * 128], rhs=wout[:, kc, half * 512:(half + 1) * 512], start=(kc == 0), stop=(kc == 7))
                    return ins
                kb.op(PE, fn_wo, r=[mixs] + wparts['wout'], w=[po])
                kb.op(DVE, lambda: dve.tensor_tensor(out=x1s[:, tt, half * 512:(half + 1) * 512], in0=po[:], in1=xrb[:, half * 512:(half + 1) * 512], op=ALU.add),
                      r=[po, xrb], w=[x1s])
            ss, rstd = smc[0], smc[1]
            kb.op(ACT, lambda: act.activation(out=hfb[:], in_=x1s[:, tt, :], func=AF.Square, accum_out=ss[:, 0:1]), r=[x1s], w=[hfb, ss])
            rsq(ss, ss[:, 0:1], rstd, rstd[:, 0:1], 1.0 / D, "rms")
            kb.op(DVE, lambda: dve.scalar_tensor_tensor(out=hfb[:], in0=x1s[:, tt, :], scalar=rstd[:, 0:1], in1=nfw_b[:], op0=ALU.mult, op1=ALU.mult),
                  r=[x1s, rstd, nfw_b], w=[hfb])
            pt = PST()

            def fn_trc():
                ins = None
                for kc in range(8):
                    ins = pe.transpose(out=pt[:, kc * 128:(kc + 1) * 128], in_=hfb[:, kc * 128:(kc + 1) * 128], identity=ident[:])
                return ins
            kb.op(PE, fn_trc, r=[hfb, ident], w=[pt])
            kb.op(ACT, lambda: act.copy(out=hfT[:, :, tt * 128:(tt + 1) * 128], in_=pt[:].rearrange("p (a b) -> p a b", a=8)), r=[pt], w=[hfT])
        for fch in range(32):
            pu = PS()

            def fn_up():
                ins = None
                for kc in range(8):
                    ins = pe.matmul(pu[:, 0:ncols], lhsT=wup[:, kc, fch * 128:(fch + 1) * 128], rhs=hfT[:, kc, 0:ncols], start=(kc == 0), stop=(kc == 7))
                return ins
            kb.op(PE, fn_up, r=[hfT] + wparts['wup'], w=[pu])
            rlb = rl[fch % 2]
            kb.op(ACT, lambda: act.activation(out=rlb[:, 0:ncols], in_=pu[:, 0:ncols], func=AF.Relu), r=[pu], w=[rlb])
            kb.op(POOL, lambda: pool.tensor_tensor(out=actb[:, fch, 0:ncols], in0=rlb[:, 0:ncols], in1=rlb[:, 0:ncols], op=ALU.mult), r=[rlb], w=[actb])
            if wdn_todo:
                load_piece(wdn, wd_v, wdn_todo.pop(0), 0)
        for tt in range(ntl):
            gt = col0 // 128 + tt
            yo = stgc[gt % 2]
            for half in range(2):
                pd = PS()

                def fn_dn():
                    ins = None
                    for fch in range(32):
                        ins = pe.matmul(pd[:], lhsT=actb[:, fch, tt * 128:(tt + 1) * 128], rhs=wdn[:, fch, half * 512:(half + 1) * 512], start=(fch == 0), stop=(fch == 31))
                    return ins
                kb.op(PE, fn_dn, r=[actb] + wparts['wdn'], w=[pd])
                kb.op(DVE, lambda: dve.tensor_tensor(out=yo[:, half * 512:(half + 1) * 512], in0=pd[:], in1=x1s[:, tt, half * 512:(half + 1) * 512], op=ALU.add),
                      r=[pd, x1s], w=[yo])
            if kind is None:
                kb.dma(y_main[gt * 128:(gt + 1) * 128, :], yo[:], r=[yo], sembuf=yo)
            else:
                kb.dma(o_ys[:, :], yo[0:16, :], r=[yo], sembuf=yo)

    kb.finish()
    esC.close()
    kb.es.close()
    return nc, kb


def _consts():
    s = np.arange(128)[:, None]
    t = np.arange(128)[None, :]
    c = {}
    c["c_ident"] = (s == t).astype(np.float32)
    c["c_sha"] = ((t == s + 1).astype(np.float32) - (t == s).astype(np.float32))
    c["c_shb"] = ((s == 127) & (t == 0)).astype(np.float32)
    c["c_msu"] = (s < t).astype(np.float32)
    c["c_miu"] = (s <= t).astype(np.float32)
    c["c_msl"] = (s > t).astype(np.float32)
    return c


def _rope_tables(pos):
    half = 8
    inv = np.power(np.float32(500000.0), -np.arange(half, dtype=np.float32) * np.float32(2.0 / 16))
    ang = pos.astype(np.float32)[:, None] * inv[None, :]
    return np.cos(ang).astype(np.float32), np.sin(ang).astype(np.float32)


def _sample_maps(c, xsm, swkv, ssh, ckw, cvw, cos_s, sin_s):
    sl = slice(16 * c, 16 * (c + 1))
    xs_pad = np.zeros((128, D), np.float32)
    xs_pad[:16] = xsm[sl]
    sp = np.zeros((128, DS), np.float32)
    sp[:16] = ssh[sl]
    return {"xs_pad": xs_pad, "st_wkv": np.ascontiguousarray(swkv[sl]).reshape(128, 4096), "st_shift": sp,
            "cache_k": np.ascontiguousarray(ckw[sl]), "cache_v": np.ascontiguousarray(cvw[sl]),
            "c_cos_s": cos_s, "c_sin_s": sin_s}


_CACHE = {}


def kernel(**inp):
    f32 = lambda a: np.ascontiguousarray(np.asarray(a, dtype=np.float32))
    if "nc" not in _CACHE:
        _CACHE["nc"] = build()
    nc, kb = _CACHE["nc"]
    xp = f32(inp["x_prompt"])
    consts = _consts()
    shared = {
        "w_in": f32(inp["w_in"][0]), "w_out": f32(inp["w_out"][0]), "w_up": f32(inp["w_ffn_up"][0]), "w_dn": f32(inp["w_ffn_down"][0]),
        "w_decay_up": f32(inp["w_decay_up"][0]), "w_a_up": f32(inp["w_a_up"][0]), "w_g_up": f32(inp["w_g_up"][0]),
    }
    for nm in ["norm_mix_w", "mu_shift", "w0", "a0", "k_k", "k_a", "ln_x_w", "ln_x_b", "q_norm_w", "k_norm_w", "sinks", "norm_ffn_w"]:
        shared[nm] = f32(inp[nm]).reshape(1, -1)
    shared["r_k"] = f32(inp["r_k"]).reshape(1, -1)
    shared.update(consts)
    in_maps = []
    xsm = f32(inp["x_sample"]).reshape(128, D)
    swkv = f32(inp["state_wkv"]).reshape(128, 8 * 64 * 64)
    ssh = f32(inp["state_shift"]).reshape(128, DS)
    ckw = f32(inp["cache_k_win"]).reshape(128, 128, 128)
    cvw = f32(inp["cache_v_win"]).reshape(128, 128, 128)
    cos_s, sin_s = _rope_tables(np.full((128,), 16384))
    for c in range(NCORE):
        b, j = c // 4, c % 4
        xe = np.zeros((NEXT, D), np.float32)
        n = SEG * (j + 1)
        xe[NEXT - n:] = xp[b, :n]
        pos = np.arange(SEG * j - 128, SEG * (j + 1))
        cos, sin = _rope_tables(pos)
        m = dict(shared)
        m["xext"] = xe
        m["c_cos"] = cos
        m["c_sin"] = sin
        m["c_mp0"] = consts["c_msl"] * (0.0 if j == 0 else 1.0)
        m.update(_sample_maps(c, xsm, swkv, ssh, ckw, cvw, cos_s, sin_s))
        in_maps.append(m)
    res = run_bass_kernel_spmd(nc, in_maps, core_ids=list(range(NCORE)))
    R = res.results
    y_prompt = np.zeros((2, 8192, D), np.float32)
    for c in range(len(R)):
        b, j = c // 4, c % 4
        y_prompt[b, j * SEG:(j + 1) * SEG] = np.asarray(R[c]["y_main"], dtype=np.float32)
    last = [min(3, len(R) - 1), min(7, len(R) - 1)]
    wkv_p = np.stack([np.asarray(R[c]["o_wkv"], dtype=np.float32) for c in last])[None]
    sh_p = np.stack([np.asarray(R[c]["o_shift"], dtype=np.float32).reshape(1, DS) for c in last])[None]
    kw_p = np.stack([np.asarray(R[c]["o_kwin"], dtype=np.float32).reshape(128, 2, 64) for c in last])[None]
    vw_p = np.stack([np.asarray(R[c]["o_vwin"], dtype=np.float32).reshape(128, 2, 64) for c in last])[None]
    nr = len(R)
    y_s = np.concatenate([np.asarray(R[c]["ys"], dtype=np.float32) for c in range(nr)], 0).reshape(16 * nr, 1, D)
    wkv_s = np.concatenate([np.asarray(R[c]["s_wkv"], dtype=np.float32).reshape(16, 8, 64, 64) for c in range(nr)], 0)[None]
    sh_s = np.concatenate([np.asarray(R[c]["s_shift"], dtype=np.float32).reshape(16, 1, DS) for c in range(nr)], 0)[None]
    kw_s = np.concatenate([np.asarray(R[c]["s_kwin"], dtype=np.float32).reshape(16, 128, 2, 64) for c in range(nr)], 0)[None]
    vw_s = np.concatenate([np.asarray(R[c]["s_vwin"], dtype=np.float32).reshape(16, 128, 2, 64) for c in range(nr)], 0)[None]
    return (y_prompt, y_s, wkv_p, sh_p, kw_p, vw_p, wkv_s, sh_s, kw_s, vw_s)
```

```python
import os
import numpy as np
import ml_dtypes
from contextlib import ExitStack
import concourse.bass as bass
import concourse.mybir as mybir
from concourse.bass_utils import run_bass_kernel_spmd

F32 = mybir.dt.float32
BF16 = mybir.dt.bfloat16
ALU = mybir.AluOpType
AF = mybir.ActivationFunctionType
AX = mybir.AxisListType

D = 1024
NCORE = 8
SEG = 2048
NEXT = 8192
NT = NEXT // 128
MT0 = (NEXT - SEG) // 128
DS = 1696
DIN = 2464
CDEC = -float(np.exp(-0.5))


class Sem:
    def __init__(self, sem, step):
        self.sem = sem
        self.step = step
        self.cnt = 0


class Eng:
    def __init__(self, name, e, sem, skip_self):
        self.name = name
        self.e = e
        self.S = Sem(sem, 1)
        self.seen = {}
        self.skip_self = skip_self


class Buf:
    def __init__(self, t, name):
        self.t = t
        self.name = name
        self.wr = None
        self.rd = []
        self.dsem = None
        self.psum = False

    def __getitem__(self, k):
        return self.t[k]


class KB:
    def __init__(self, nc):
        self.nc = nc
        self.es = ExitStack()
        self.nsem = 0
        self.pe = Eng("pe", nc.tensor, self.sem("pe"), True)
        self.act = Eng("act", nc.scalar, self.sem("act"), False)
        self.dve = Eng("dve", nc.vector, self.sem("dve"), False)
        self.pool = Eng("pool", nc.gpsimd, self.sem("pool"), False)
        self.sp = Eng("sp", nc.sync, self.sem("sp"), True)
        self.engs = [self.pe, self.act, self.dve, self.pool, self.sp]
        self.dsems = []
        self.ninst = 0

    def sem(self, name):
        self.nsem += 1
        return self.es.enter_context(self.nc.semaphore(f"s_{name}_{self.nsem}"))

    def sb(self, name, shape, dt, stack=None):
        t = (stack or self.es).enter_context(self.nc.sbuf_tensor(name, list(shape), dt))
        return Buf(t, name)

    def ps(self, name, shape, dt):
        t = self.es.enter_context(self.nc.psum_tensor(name, list(shape), dt))
        b = Buf(t, name)
        b.psum = True
        return b

    def dram(self, name, shape, dt, kind):
        t = self.nc.dram_tensor(name, list(shape), dt, kind=kind)
        return Buf(t.ap(), name)

    def _waits(self, eng, deps):
        best = {}
        for (s, c) in deps:
            if c > best.get(s, 0):
                best[s] = c
        for s, c in best.items():
            if eng.seen.get(s, 0) >= c:
                continue
            eng.e.wait_ge(s.sem, c * s.step)
            eng.seen[s] = c
            self.ninst += 1

    def op(self, eng, fn, r=(), w=()):
        deps = []
        for b in r:
            if b.wr is not None:
                deps.append(b.wr)
            if b.psum:
                deps.extend(d for d in b.rd if d[0] is not eng.S)
        for b in w:
            if b.wr is not None:
                deps.append(b.wr)
            deps.extend(b.rd)
        if eng.skip_self:
            deps = [d for d in deps if d[0] is not eng.S]
        self._waits(eng, deps)
        ins = fn()
        eng.S.cnt += 1
        ins.then_inc(eng.S.sem, 1)
        self.ninst += 1
        me = (eng.S, eng.S.cnt)
        for b in r:
            b.rd.append(me)
        for b in w:
            b.wr = me
            b.rd = []
        return ins

    def dma(self, out_ap, in_ap, r=(), w=(), sembuf=None, q=None, **kw):
        q = q or self.sp
        sb = sembuf
        if sb.dsem is None:
            sb.dsem = Sem(self.sem("d_" + sb.name), 16)
            self.dsems.append(sb.dsem)
        deps = []
        for b in r:
            if b.wr is not None:
                deps.append(b.wr)
        for b in w:
            if b.wr is not None:
                deps.append(b.wr)
            deps.extend(b.rd)
        self._waits(q, deps)
        ins = q.e.dma_start(out=out_ap, in_=in_ap, **kw)
        sb.dsem.cnt += 1
        ins.then_inc(sb.dsem.sem, 16)
        self.ninst += 1
        me = (sb.dsem, sb.dsem.cnt)
        for b in r:
            b.rd.append(me)
        for b in w:
            b.wr = me
            b.rd = []

    def barrier(self):
        for e in self.engs:
            deps = [(o.S, o.S.cnt) for o in self.engs if o is not e and o.S.cnt > 0]
            deps += [(d, d.cnt) for d in self.dsems if d.cnt > 0]
            self._waits(e, deps)

    def finish(self):
        deps = [(o.S, o.S.cnt) for o in self.engs if o is not self.sp and o.S.cnt > 0]
        deps += [(d, d.cnt) for d in self.dsems if d.cnt > 0]
        self._waits(self.sp, deps)


class _Stop(Exception):
    pass


def build(dbg=False, stage=None):
    try:
        return _build(dbg, stage)
    except _Stop as e:
        return e.args[0], e.args[1]


def _build(dbg=False, stage=None):
    nc = bass.Bass("TRN2", target_bir_lowering=False)
    kb = KB(nc)
    PE, ACT, DVE, POOL = kb.pe, kb.act, kb.dve, kb.pool
    pe, act, dve, pool = nc.tensor, nc.scalar, nc.vector, nc.gpsimd

    def din(name, shape, dt=F32):
        return kb.dram(name, shape, dt, "ExternalInput")

    def dout(name, shape, dt=F32):
        return kb.dram(name, shape, dt, "ExternalOutput")

    xext = din("xext", [NEXT, D])
    w_in = din("w_in", [D, DIN])
    w_out = din("w_out", [D, D])
    w_up = din("w_up", [D, 4 * D])
    w_dn = din("w_dn", [4 * D, D])
    vec = {}
    for nm, n in [("norm_mix_w", D), ("mu_shift", DS), ("w0", 512), ("a0", 512), ("k_k", 512), ("k_a", 512),
                  ("r_k", 512), ("ln_x_w", 512), ("ln_x_b", 512), ("q_norm_w", 64), ("k_norm_w", 64),
                  ("sinks", 8), ("norm_ffn_w", D)]:
        vec[nm] = din(nm, [1, n])
    w_dec = din("w_decay_up", [32, 512])
    w_a = din("w_a_up", [32, 512])
    w_g = din("w_g_up", [96, 512])
    c_ident = din("c_ident", [128, 128])
    c_sha = din("c_sha", [128, 128])
    c_shb = din("c_shb", [128, 128])
    c_msu = din("c_msu", [128, 128])
    c_miu = din("c_miu", [128, 128])
    c_msl = din("c_msl", [128, 128])
    c_mp0 = din("c_mp0", [128, 128])
    c_cos = din("c_cos", [SEG + 128, 8])
    c_sin = din("c_sin", [SEG + 128, 8])

    y_main = dout("y_main", [SEG, D])
    o_wkv = dout("o_wkv", [8, 64, 64])
    o_shift = dout("o_shift", [1, DS])
    o_kwin = dout("o_kwin", [128, 128])
    o_vwin = dout("o_vwin", [128, 128])

    es = kb.es
    stacks = []

    def ck(n):
        if stage == n:
            kb.finish()
            for st_ in reversed(stacks):
                st_.close()
            kb.es.close()
            raise _Stop(nc, kb)
    psb = [kb.ps(f"psb{i}", [128, 512], F32) for i in range(6)]
    pst = [kb.ps(f"pst{i}", [128, 1024], BF16) for i in range(2)]
    ring = {"i": 0, "t": 0}

    def PS():
        b = psb[ring["i"] % 5]
        ring["i"] += 1
        return b

    def PST():
        b = pst[ring["t"] % 2]
        ring["t"] += 1
        return b

    ring.update({"f": 0, "b": 0})

    def PSF():
        b = psb[ring["f"] % 2]
        ring["f"] += 1
        return b

    def PSB():
        b = psb[2 + ring["b"] % 4]
        ring["b"] += 1
        return b

    def PSTF():
        return pst[0]

    def PSTB():
        return pst[1]

    def sbt(name, shape, dt=F32):
        return kb.sb(name, shape, dt)

    ident_f = sbt("ident_f", [128, 128])
    msu = sbt("msu", [128, 128])
    miu = sbt("miu", [128, 128])
    msl = sbt("msl", [128, 128])
    mp0_f = sbt("mp0_f", [128, 128])
    ident = sbt("ident", [128, 128], BF16)
    sha = sbt("sha", [128, 128], BF16)
    shb = sbt("shb", [128, 128], BF16)
    amc = sbt("amc", [128, 128], BF16)
    amp = sbt("amp", [128, 128], BF16)
    amp0 = sbt("amp0", [128, 128], BF16)
    ones_f = sbt("ones_f", [128, 2])
    for dst, src in [(ident_f, c_ident), (msu, c_msu), (miu, c_miu), (msl, c_msl), (mp0_f, c_mp0)]:
        kb.dma(dst[:], src[:, :], w=[dst], sembuf=dst)
    stg = sbt("cstg", [128, 2, 128])
    kb.dma(stg[:, 0, :], c_sha[:, :], w=[stg], sembuf=stg)
    kb.dma(stg[:, 1, :], c_shb[:, :], w=[stg], sembuf=stg)
    kb.op(DVE, lambda: dve.tensor_copy(out=ident[:], in_=ident_f[:]), r=[ident_f], w=[ident])
    kb.op(DVE, lambda: dve.tensor_copy(out=sha[:], in_=stg[:, 0, :]), r=[stg], w=[sha])
    kb.op(DVE, lambda: dve.tensor_copy(out=shb[:], in_=stg[:, 1, :]), r=[stg], w=[shb])
    kb.op(DVE, lambda: dve.tensor_copy(out=amc[:], in_=miu[:]), r=[miu], w=[amc])
    kb.op(DVE, lambda: dve.tensor_copy(out=amp[:], in_=msl[:]), r=[msl], w=[amp])
    kb.op(DVE, lambda: dve.tensor_copy(out=amp0[:], in_=mp0_f[:]), r=[mp0_f], w=[amp0])
    kb.op(POOL, lambda: pool.memset(ones_f[:], 1.0), w=[ones_f])

    nfw_b = sbt("b_nfw", [128, D])
    kb.dma(nfw_b[:], vec["norm_ffn_w"][0:1, 0:D].partition_broadcast(128), w=[nfw_b], sembuf=nfw_b)
    eps_vals = {"rms": 1e-6, "lnx": 64e-5, "one": 1.0}
    eps_t = {}
    for k_, v_ in eps_vals.items():
        t_ = sbt("eps_" + k_, [128, 1])
        kb.op(POOL, lambda t_=t_, v_=v_: pool.memset(t_[:], v_), w=[t_])
        eps_t[k_] = t_

    ck(0)
    esA = ExitStack()
    stacks.append(esA)

    def sa(name, shape, dt=F32):
        return kb.sb(name, shape, dt, stack=esA)

    def bvec(name, src, n, c0=0):
        t = sa("b_" + name, [128, n])
        kb.dma(t[:], src[0:1, c0:c0 + n].partition_broadcast(128), w=[t], sembuf=t)
        return t

    nmw_b = bvec("nmw", vec["norm_mix_w"], D)
    mu_b = bvec("mu", vec["mu_shift"], 1536)
    w0_b = bvec("w0", vec["w0"], 512)
    a0_b = bvec("a0", vec["a0"], 512)
    kk_b = bvec("kk", vec["k_k"], 512)
    ka_b = bvec("ka", vec["k_a"], 512)
    rk_b = bvec("rk", vec["r_k"], 512)
    lnw_b = bvec("lnw", vec["ln_x_w"], 512)
    lnb_b = bvec("lnb", vec["ln_x_b"], 512)
    qnw_b = bvec("qnw", vec["q_norm_w"], 64)
    knw_b = bvec("knw", vec["k_norm_w"], 64)
    snk_b = bvec("snk", vec["sinks"], 8)
    esink = sa("esink", [128, 8])
    kb.op(ACT, lambda: act.activation(out=esink[:], in_=snk_b[:], func=AF.Exp), r=[snk_b], w=[esink])
    mul = sa("mul", [96, 3])
    kb.op(POOL, lambda: pool.memset(mul[:], 0.0), w=[mul])
    for g, (c0, n) in enumerate([(1536, 32), (1568, 32), (1600, 96)]):
        kb.dma(mul[0:n, g:g + 1], vec["mu_shift"][0:1, c0:c0 + n].rearrange("o n -> n o"), w=[mul], sembuf=mul,
               allow_slow_non_contiguous=True)
    mix_scr = kb.dram("d_mix", [128, 8, SEG + 128], BF16, "ExternalOutput")

    lw = sa("lw", [96, 3, 512], BF16)
    win = sa("win", [128, 8, DIN], BF16)
    xb = [sa(f"xb{i}", [128, D]) for i in range(2)]
    w_in_v = w_in.t.rearrange("(kc p) n -> kc p n", p=128)
    HW = DIN // 4
    for kc in range(8):
        for hf_ in range(4):
            st = xb[hf_ % 2]
            kb.dma(st[:, 0:HW], w_in_v[kc][:, hf_ * HW:(hf_ + 1) * HW], w=[st], sembuf=st)
            kb.op(POOL, lambda: pool.tensor_copy(out=win[:, kc, hf_ * HW:(hf_ + 1) * HW], in_=st[:, 0:HW]), r=[st], w=[win])
    for g, (src, n) in enumerate([(w_dec, 32), (w_a, 32), (w_g, 96)]):
        st = xb[g % 2]
        kb.dma(st[0:n, 0:512], src[:, :], w=[st], sembuf=st)
        kb.op(DVE, lambda: dve.tensor_copy(out=lw[0:n, g, :], in_=st[0:n, 0:512]), r=[st], w=[lw])
    sm = [sa(f"sm{i}", [128, 8]) for i in range(12)]
    h_bf = sa("h_bf", [128, D], BF16)
    hT = [sa("hT0", [128, 8, 128], BF16)] * 2
    r_sb = sa("r_sb", [128, 512])
    k_sb = sa("k_sb", [128, 512])
    v_sb = sa("v_sb", [128, 512])
    lt = [sa(f"lt{i}", [96, 3, 128]) for i in range(2)]
    lt.append(lt[1])
    th_bf = sa("th_bf", [32, 128], BF16)
    al_bf = sa("al_bf", [32, 128], BF16)
    sg_bf = sa("sg_bf", [96, 128], BF16)
    f = [sa(f"f{i}", [128, 512]) for i in range(12)]
    s_sb, asig, kk, kkn, kmod, bv, g_sb = f[0], f[1], f[2], f[3], f[4], f[5], f[6]
    t1, t2, t3, t4, t5 = f[7], f[8], f[9], f[10], f[11]
    y_sb = sa("y_sb", [128, 512])
    mix_tok2 = [sa(f"mix_tok{i}", [128, D], BF16) for i in range(2)]
    mix_tok = mix_tok2[0]
    bon2 = [sa(f"bon{i}", [128, 512]) for i in range(2)]
    g2 = [g_sb, sa("g2b", [128, 512])]
    gt1 = sa("gt1", [128, 512])
    gt2 = sa("gt2", [128, 512])
    mixTt = [sa("mixTt0", [128, 8, 128], BF16)] * 2
    q_sb = s_sb
    kv_sb = sa("kv_sb", [128, 256])
    kf = sa("kf", [128, 128])
    qr_bf = sa("qr_bf", [128, 512], BF16)
    k_bf = sa("k_bf", [128, 128], BF16)
    qT = sa("qT", [64, 8, 128], BF16)
    cs_t = [sa(f"cs{i}", [128, 2, 8]) for i in range(2)]
    rp = [sa(f"rp{i}", [128, 8, 8]) for i in range(4)]
    esA2 = ExitStack()
    stacks.append(esA2)

    def sa2(name, shape, dt=F32):
        return kb.sb(name, shape, dt, stack=esA2)

    pm = [{c0: sa2(f"pm{i}_{c0}", [128, 512], BF16) for c0 in (0, 512, 1024)} for i in range(2)]
    for pd_ in pm:
        for p_ in pd_.values():
            kb.op(POOL, lambda p_=p_: pool.memset(p_[:], 0.0), w=[p_])
    lro = sa2("lro", [96, 3, 129])
    kb.op(POOL, lambda: pool.memset(lro[:], 0.0), w=[lro])
    A_tok2 = [sa2(f"A_tok{i}", [128, 512], BF16) for i in range(2)]
    Bh2 = [sa2(f"Bh{i}", [128, 512], BF16) for i in range(2)]
    Kh2 = [sa2(f"Kh{i}", [128, 512], BF16) for i in range(2)]
    Vb2 = [sa2(f"Vb{i}", [128, 512], BF16) for i in range(2)]
    Rt2 = [sa2(f"Rt{i}", [128, 512], BF16) for i in range(2)]
    tk = [sa2(f"tk{i}", [128, 512], BF16) for i in range(2)]
    FM2 = [sa2(f"FM{i}", [128, 4, 4, 128], BF16) for i in range(2)]
    gC2 = [sa2(f"gC{i}", [64, 8]) for i in range(2)]
    Gd = sa2("Gd", [64, 8, 64])
    nAB = [[[sa2(f"n{ab}{hg}{i}", [128, 4, 128], BF16) for i in range(2)] for ab in "AB"] for hg in range(2)]
    Xh = [sa2(f"Xh{i}", [128, 4, 128], BF16) for i in range(2)]
    P2b = sa2("P2b", [128, 4, 128], BF16)
    P3b = sa2("P3b", [128, 8, 128], BF16)
    P4b = sa2("P4b", [128, 8, 128], BF16)
    M0b = sa2("M0b", [128, 512], BF16)
    Wb = sa2("Wb", [128, 512], BF16)
    U0b = sa2("U0b", [128, 512], BF16)
    PsiT = sa2("PsiT", [64, 512], BF16)
    Q_sb = sa2("Q_sb", [64, 512])
    ZTb = sa2("ZTb", [64, 8, 128], BF16)
    H32 = sa2("H32", [64, 512])
    Hb = [sa2(f"Hb{i}", [64, 512], BF16) for i in range(2)]
    kb.op(POOL, lambda: pool.memset(Hb[0][:], 0.0), w=[Hb[0]])
    kb.op(POOL, lambda: pool.memset(H32[:], 0.0), w=[H32])
    kT = [sa2(f"kT{i}", [64, 2, 128], BF16) for i in range(2)]
    Va = [sa2(f"Va{i}", [128, 2, 65], BF16) for i in range(2)]
    for v_ in Va:
        kb.op(POOL, lambda v_=v_: pool.memset(v_[:], 1.0), w=[v_])
    Eb = [sa2(f"Eb{i}", [128, 512], BF16) for i in range(2)]
    Hfin = sa2("Hfin", [64, 8, 64])

    def v3(b, n=8):
        return b[:].rearrange("p (h d) -> p h d", h=n)

    def bc(ap_small, n, d):
        return ap_small.unsqueeze(2).to_broadcast([128, n, d])

    def bch(ap_vec, n, d):
        return ap_vec.unsqueeze(1).to_broadcast([128, n, d])

    def sigmoid_chain(src_ap, rbufs, tmp, dst, shape_ap=lambda b: b[:]):
        kb.op(ACT, lambda: act.activation(out=shape_ap(tmp), in_=src_ap, func=AF.Exp, scale=-1.0), r=rbufs, w=[tmp])
        kb.op(ACT, lambda: act.activation(out=shape_ap(tmp), in_=shape_ap(tmp), func=AF.Ln, bias=eps_t["one"][0:shape_ap(tmp).shape[0], 0:1]),
              r=[tmp, eps_t["one"]], w=[tmp])
        kb.op(ACT, lambda: act.activation(out=shape_ap(dst), in_=shape_ap(tmp), func=AF.Exp, scale=-1.0), r=[tmp], w=[dst])

    def rsq(src_buf, src_ap, dst_buf, dst_ap, scale, eps_key):
        n = dst_ap.shape[0]
        kb.op(ACT, lambda: act.activation(out=dst_ap, in_=src_ap, func=AF.Ln, scale=scale, bias=eps_t[eps_key][0:n, 0:1]),
              r=[src_buf, eps_t[eps_key]], w=[dst_buf])
        kb.op(ACT, lambda: act.activation(out=dst_ap, in_=dst_ap, func=AF.Exp, scale=-0.5), r=[dst_buf], w=[dst_buf])

    def load_x(i):
        b = xb[i % 2]
        kb.dma(b[:], xext[i * 128:(i + 1) * 128, :], w=[b], sembuf=b)

    def rwkv_pointwise(main, g_sb, alloc):
        psz, psza = alloc(), alloc()
        kb.op(PE, lambda: pe.matmul(psz[:], lhsT=th_bf[:], rhs=lw[0:32, 0, :], start=True, stop=True), r=[th_bf, lw], w=[psz])
        kb.op(PE, lambda: pe.matmul(psza[:], lhsT=al_bf[:], rhs=lw[0:32, 1, :], start=True, stop=True), r=[al_bf, lw], w=[psza])
        kb.op(DVE, lambda: dve.tensor_tensor(out=t1[:], in0=psz[:], in1=w0_b[:], op=ALU.add), r=[psz, w0_b], w=[t1])
        sigmoid_chain(t1[:], [t1], t2, s_sb)
        yield
        kb.op(DVE, lambda: dve.tensor_tensor(out=t1[:], in0=psza[:], in1=a0_b[:], op=ALU.add), r=[psza, a0_b], w=[t1])
        sigmoid_chain(t1[:], [t1], t2, asig)
        yield
        if main:
            psg = alloc()
            kb.op(PE, lambda: pe.matmul(psg[:], lhsT=sg_bf[:], rhs=lw[0:96, 2, :], start=True, stop=True), r=[sg_bf, lw], w=[psg])
            kb.op(ACT, lambda: act.copy(out=g_sb[:], in_=psg[:]), r=[psg], w=[g_sb])
        yield
        n2, rn = sm[2], sm[3]
        kb.op(DVE, lambda: dve.tensor_tensor(out=kk[:], in0=k_sb[:], in1=kk_b[:], op=ALU.mult), r=[k_sb, kk_b], w=[kk])
        kb.op(POOL, lambda: pool.tensor_tensor(out=t3[:], in0=kk[:], in1=kk[:], op=ALU.mult), r=[kk], w=[t3])
        kb.op(DVE, lambda: dve.tensor_reduce(out=n2[:], in_=v3(t3), axis=AX.X, op=ALU.add), r=[t3], w=[n2])
        kb.op(DVE, lambda: dve.tensor_scalar_max(out=n2[:], in0=n2[:], scalar1=1e-24), r=[n2], w=[n2])
        yield
        kb.op(ACT, lambda: act.activation(out=rn[:], in_=n2[:], func=AF.Ln), r=[n2], w=[rn])
        kb.op(ACT, lambda: act.activation(out=rn[:], in_=rn[:], func=AF.Exp, scale=-0.5), r=[rn], w=[rn])
        kb.op(DVE, lambda: dve.tensor_tensor(out=v3(kkn), in0=v3(kk), in1=bc(rn[:], 8, 64), op=ALU.mult), r=[kk, rn], w=[kkn])
        yield
        kb.op(DVE, lambda: dve.scalar_tensor_tensor(out=t4[:], in0=asig[:], scalar=-1.0, in1=ka_b[:], op0=ALU.add, op1=ALU.mult), r=[asig, ka_b], w=[t4])
        kb.op(DVE, lambda: dve.scalar_tensor_tensor(out=kmod[:], in0=t4[:], scalar=1.0, in1=k_sb[:], op0=ALU.add, op1=ALU.mult), r=[t4, k_sb], w=[kmod])
        kb.op(POOL, lambda: pool.tensor_tensor(out=bv[:], in0=kkn[:], in1=asig[:], op=ALU.mult), r=[kkn, asig], w=[bv])

    def bonus_pre(bon):
        bs = sm[9]
        kb.op(POOL, lambda: pool.tensor_tensor(out=t3[:], in0=r_sb[:], in1=kmod[:], op=ALU.mult), r=[r_sb, kmod], w=[t3])
        kb.op(POOL, lambda: pool.tensor_tensor(out=t3[:], in0=t3[:], in1=rk_b[:], op=ALU.mult), r=[t3, rk_b], w=[t3])
        kb.op(DVE, lambda: dve.tensor_reduce(out=bs[:], in_=v3(t3), axis=AX.X, op=ALU.add), r=[t3], w=[bs])
        kb.op(DVE, lambda: dve.tensor_tensor(out=v3(bon), in0=v3(v_sb), in1=bc(bs[:], 8, 64), op=ALU.mult), r=[v_sb, bs], w=[bon])

    def groupnorm_gate2(bon, g_sb, mix_tok):
        gs, gq, gm, gv, gr = sm[4], sm[5], sm[6], sm[7], sm[8]
        kb.op(DVE, lambda: dve.tensor_reduce(out=gs[:], in_=v3(y_sb), axis=AX.X, op=ALU.add), r=[y_sb], w=[gs])
        kb.op(POOL, lambda: pool.tensor_tensor(out=gt1[:], in0=y_sb[:], in1=y_sb[:], op=ALU.mult), r=[y_sb], w=[gt1])
        kb.op(DVE, lambda: dve.tensor_reduce(out=gq[:], in_=v3(gt1), axis=AX.X, op=ALU.add), r=[gt1], w=[gq])
        kb.op(DVE, lambda: dve.tensor_scalar_mul(out=gm[:], in0=gs[:], scalar1=1.0 / 64), r=[gs], w=[gm])
        kb.op(DVE, lambda: dve.tensor_tensor(out=gv[:], in0=gm[:], in1=gm[:], op=ALU.mult), r=[gm], w=[gv])
        kb.op(DVE, lambda: dve.scalar_tensor_tensor(out=gv[:], in0=gq[:], scalar=1.0 / 64, in1=gv[:], op0=ALU.mult, op1=ALU.subtract), r=[gq, gv], w=[gv])
        rsq(gv, gv[:], gr, gr[:], 1.0, "lnx")
        kb.op(DVE, lambda: dve.tensor_tensor(out=v3(gt2), in0=v3(y_sb), in1=bc(gm[:], 8, 64), op=ALU.subtract), r=[y_sb, gm], w=[gt2])
        kb.op(DVE, lambda: dve.tensor_tensor(out=v3(gt2), in0=v3(gt2), in1=bc(gr[:], 8, 64), op=ALU.mult), r=[gt2, gr], w=[gt2])
        kb.op(POOL, lambda: pool.tensor_tensor(out=gt2[:], in0=gt2[:], in1=lnw_b[:], op=ALU.mult), r=[gt2, lnw_b], w=[gt2])
        kb.op(POOL, lambda: pool.tensor_tensor(out=gt2[:], in0=gt2[:], in1=lnb_b[:], op=ALU.add), r=[gt2, lnb_b], w=[gt2])
        kb.op(POOL, lambda: pool.tensor_tensor(out=gt2[:], in0=gt2[:], in1=bon[:], op=ALU.add), r=[gt2, bon], w=[gt2])
        kb.op(POOL, lambda: pool.tensor_tensor(out=mix_tok[:, 0:512], in0=gt2[:], in1=g_sb[:], op=ALU.mult), r=[gt2, g_sb], w=[mix_tok])

    ck(1)
    load_x(0)
    def tile_gen(i):
        par = i % 2
        A_tok, Bh, Kh, Vb, Rt, FM, gC = A_tok2[par], Bh2[par], Kh2[par], Vb2[par], Rt2[par], FM2[par], gC2[par]
        mix_tok, bon, g_sb = mix_tok2[par], bon2[par], g2[par]
        par = i % 2
        main = i >= MT0
        halo = i == MT0 - 1
        mi = i - MT0
        if i + 1 < NT:
            load_x(i + 1)
        xt = xb[par]
        ss, rstd = sm[0], sm[1]
        yield
        kb.op(ACT, lambda: act.activation(out=h_bf[:], in_=xt[:], func=AF.Square, accum_out=ss[:, 0:1]), r=[xt], w=[h_bf, ss])
        rsq(ss, ss[:, 0:1], rstd, rstd[:, 0:1], 1.0 / D, "rms")
        kb.op(DVE, lambda: dve.scalar_tensor_tensor(out=h_bf[:], in0=xt[:], scalar=rstd[:, 0:1], in1=nmw_b[:], op0=ALU.mult, op1=ALU.mult),
              r=[xt, rstd, nmw_b], w=[h_bf])
        yield
        pt = PSTF()

        def fn_tr():
            ins = None
            for kc in range(8):
                ins = pe.transpose(out=pt[:, kc * 128:(kc + 1) * 128], in_=h_bf[:, kc * 128:(kc + 1) * 128], identity=ident[:])
            return ins
        kb.op(PE, fn_tr, r=[h_bf, ident], w=[pt])
        hTc = hT[par]
        kb.op(ACT, lambda: act.copy(out=hTc[:].rearrange("p a b -> p (a b)"), in_=pt[:]), r=[pt], w=[hTc])

        yield
        def proj(c0, c1, ps, stop=True):
            def fn():
                ins = None
                for kc in range(8):
                    ins = pe.matmul(ps[:, 0:c1 - c0], lhsT=hTc[:, kc, :], rhs=win[:, kc, c0:c1], start=(kc == 0), stop=(kc == 7))
                return ins
            kb.op(PE, fn, r=[hTc, win], w=[ps])

        psl = PSF()

        def fn_lo():
            ins = None
            for g, (c0, n) in enumerate([(1536, 32), (1568, 32), (1600, 96)]):
                if g == 2 and not (main or halo):
                    continue
                for kc in range(8):
                    ins = pe.matmul(psl[0:n, g * 128:(g + 1) * 128], lhsT=win[:, kc, c0:c0 + n], rhs=hTc[:, kc, :], start=(kc == 0), stop=(kc == 7))
            return ins
        kb.op(PE, fn_lo, r=[hTc, win], w=[psl])
        ng = 3 if main else 2
        kb.op(ACT, lambda: act.copy(out=lro[0:32, 0:2, 1:129], in_=psl[0:32, 0:256].rearrange("p (g t) -> p g t", g=2)), r=[psl], w=[lro])
        if main or halo:
            kb.op(ACT, lambda: act.copy(out=lro[0:96, 2, 1:129], in_=psl[0:96, 256:384]), r=[psl], w=[lro])
        if dbg and i == NT - 1:
            pass
        kb.op(DVE, lambda: dve.tensor_tensor(out=lt[0][:, 0:ng, :], in0=lro[:, 0:ng, 0:128], in1=lro[:, 0:ng, 1:129], op=ALU.subtract), r=[lro], w=[lt[0]])
        kb.op(DVE, lambda: dve.tensor_tensor(out=lt[0][:, 0:ng, :], in0=lt[0][:, 0:ng, :], in1=mul[:, 0:ng].unsqueeze(2).to_broadcast([96, ng, 128]), op=ALU.mult),
              r=[lt[0], mul], w=[lt[0]])
        kb.op(DVE, lambda: dve.tensor_tensor(out=lt[0][:, 0:ng, :], in0=lt[0][:, 0:ng, :], in1=lro[:, 0:ng, 1:129], op=ALU.add), r=[lt[0], lro], w=[lt[0]])
        if i == NT - 1 and not os.environ.get('NO_LAST_B'):
            psx = PSF()
            proj(1536, 1696, psx)
            kb.op(ACT, lambda: act.copy(out=t4[:, 0:160], in_=psx[:, 0:160]), r=[psx], w=[t4])
            kb.dma(o_shift[0:1, 1536:1696], t4[127:128, 0:160], r=[t4], sembuf=t4)
        kb.op(POOL, lambda: pool.tensor_copy(out=lro[:, :, 0:1], in_=lro[:, :, 128:129]), r=[lro], w=[lro])
        kb.op(ACT, lambda: act.activation(out=lt[1][0:32, 0, :], in_=lt[0][0:32, 0, :], func=AF.Exp, scale=2.0), r=[lt[0]], w=[lt[1]])
        kb.op(DVE, lambda: dve.tensor_scalar_add(out=lt[1][0:32, 0, :], in0=lt[1][0:32, 0, :], scalar1=1.0), r=[lt[1]], w=[lt[1]])
        kb.op(DVE, lambda: dve.reciprocal(out=lt[1][0:32, 0, :], in_=lt[1][0:32, 0, :]), r=[lt[1]], w=[lt[1]])
        kb.op(DVE, lambda: dve.tensor_scalar(out=th_bf[:], in0=lt[1][0:32, 0, :], scalar1=-2.0, scalar2=1.0, op0=ALU.mult, op1=ALU.add), r=[lt[1]], w=[th_bf])
        kb.op(POOL, lambda: pool.tensor_copy(out=al_bf[:], in_=lt[0][0:32, 1, :]), r=[lt[0]], w=[al_bf])
        if main:
            kb.op(ACT, lambda: act.activation(out=lt[2][0:96, 2, :], in_=lt[0][0:96, 2, :], func=AF.Exp, scale=-1.0), r=[lt[0]], w=[lt[2]])
            kb.op(DVE, lambda: dve.tensor_scalar_add(out=lt[2][0:96, 2, :], in0=lt[2][0:96, 2, :], scalar1=1.0), r=[lt[2]], w=[lt[2]])
            kb.op(DVE, lambda: dve.reciprocal(out=lt[2][0:96, 2, :], in_=lt[2][0:96, 2, :]), r=[lt[2]], w=[lt[2]])
            kb.op(POOL, lambda: pool.tensor_copy(out=sg_bf[:], in_=lt[2][0:96, 2, :]), r=[lt[2]], w=[sg_bf])
        cols = ([("r", 0, r_sb)] if (main or halo) else []) + [("k", 512, k_sb), ("v", 1024, v_sb)]
        pmc, pmp = pm[par], pm[1 - par]
        for g0 in range(0, len(cols), 2):
            grp = []
            for (nm, c0, dst) in cols[g0:g0 + 2]:
                ps = PSF()
                proj(c0, c0 + 512, ps)
                grp.append((c0, dst, ps))
            for (c0, dst, ps) in grp:
                if i == NT - 1:
                    rawb, rawap = {0: (xb[0], xb[0][:, 0:512]), 512: (xb[0], xb[0][:, 512:1024]), 1024: (t5, t5[:])}[c0]
                    kb.op(ACT, lambda: act.copy(out=rawap, in_=ps[:]), r=[ps], w=[rawb])
                    kb.dma(o_shift[0:1, c0:c0 + 512], rawap[127:128, :], r=[rawb], sembuf=rawb)
                kb.op(DVE, lambda: dve.tensor_tensor(out=pmc[c0][:], in0=ps[:], in1=mu_b[:, c0:c0 + 512], op=ALU.mult),
                      r=[ps, mu_b], w=[pmc[c0]])
            for (c0, dst, ps) in grp:
                def fn_l(ps=ps, c0=c0):
                    pe.matmul(ps[:], lhsT=sha[:], rhs=pmc[c0][:], start=False, stop=False, skip_group_check=True)
                    return pe.matmul(ps[:], lhsT=shb[:], rhs=pmp[c0][:], start=False, stop=True, skip_group_check=True)
                kb.op(PE, fn_l, r=[pmc[c0], pmp[c0], sha, shb], w=[ps])
                kb.op(ACT, lambda: act.copy(out=dst[:], in_=ps[:]), r=[ps], w=[dst])
        if not main:
            pass

        yield
        if main or halo:
            ai = mi + 1
            cst = cs_t[par]
            kb.dma(cst[:, 0, :], c_cos[ai * 128:(ai + 1) * 128, :], w=[cst], sembuf=cst)
            kb.dma(cst[:, 1, :], c_sin[ai * 128:(ai + 1) * 128, :], w=[cst], sembuf=cst)
            pkv = PSF()
            proj(2208, 2464, pkv)
            kb.op(ACT, lambda: act.copy(out=kv_sb[:], in_=pkv[:, 0:256]), r=[pkv], w=[kv_sb])

            def qk_norm_rope(src_buf, src_ap3, nh, wb, out_buf, out_ap3, tmpA, tmpB):
                ssq, rq = sm[10], sm[11]
                kb.op(POOL, lambda: pool.tensor_tensor(out=tmpA, in0=src_ap3, in1=src_ap3, op=ALU.mult), r=[src_buf], w=[t4])
                kb.op(DVE, lambda: dve.tensor_reduce(out=ssq[:, 0:nh], in_=tmpA, axis=AX.X, op=ALU.add), r=[t4], w=[ssq])
                rsq(ssq, ssq[:, 0:nh], rq, rq[:, 0:nh], 1.0 / 64, "rms")
                kb.op(DVE, lambda: dve.tensor_tensor(out=tmpA, in0=src_ap3, in1=bc(rq[:, 0:nh], nh, 64), op=ALU.mult), r=[src_buf, rq], w=[t4])
                kb.op(POOL, lambda: pool.tensor_tensor(out=out_ap3, in0=tmpA, in1=bch(wb[:], nh, 64), op=ALU.mult), r=[t4, wb], w=[out_buf])
                kb.op(POOL, lambda: pool.tensor_copy(out=rp[0][:, 0:nh, :], in_=out_ap3[:, :, 0:8]), r=[out_buf], w=[rp[0]])
                kb.op(POOL, lambda: pool.tensor_copy(out=rp[1][:, 0:nh, :], in_=out_ap3[:, :, 8:16]), r=[out_buf], w=[rp[1]])
                cosb = cst[:, 0, :].unsqueeze(1).to_broadcast([128, nh, 8])
                sinb = cst[:, 1, :].unsqueeze(1).to_broadcast([128, nh, 8])
                kb.op(DVE, lambda: dve.tensor_tensor(out=rp[2][:, 0:nh, :], in0=rp[0][:, 0:nh, :], in1=cosb, op=ALU.mult), r=[rp[0], cst], w=[rp[2]])
                kb.op(DVE, lambda: dve.tensor_tensor(out=rp[3][:, 0:nh, :], in0=rp[1][:, 0:nh, :], in1=sinb, op=ALU.mult), r=[rp[1], cst], w=[rp[3]])
                kb.op(DVE, lambda: dve.tensor_tensor(out=out_ap3[:, :, 0:8], in0=rp[2][:, 0:nh, :], in1=rp[3][:, 0:nh, :], op=ALU.subtract), r=[rp[2], rp[3]], w=[out_buf])
                kb.op(DVE, lambda: dve.tensor_tensor(out=rp[2][:, 0:nh, :], in0=rp[1][:, 0:nh, :], in1=cosb, op=ALU.mult), r=[rp[1], cst], w=[rp[2]])
                kb.op(DVE, lambda: dve.tensor_tensor(out=rp[3][:, 0:nh, :], in0=rp[0][:, 0:nh, :], in1=sinb, op=ALU.mult), r=[rp[0], cst], w=[rp[3]])
                kb.op(DVE, lambda: dve.tensor_tensor(out=out_ap3[:, :, 8:16], in0=rp[2][:, 0:nh, :], in1=rp[3][:, 0:nh, :], op=ALU.add), r=[rp[2], rp[3]], w=[out_buf])

            k3 = kv_sb[:, 0:128].rearrange("p (h d) -> p h d", h=2)
            kf3 = kf[:].rearrange("p (h d) -> p h d", h=2)
            qk_norm_rope(kv_sb, k3, 2, knw_b, kf, kf3, t4[:, 0:128].rearrange("p (h d) -> p h d", h=2), None)
            kb.op(POOL, lambda: pool.tensor_copy(out=k_bf[:], in_=kf[:]), r=[kf], w=[k_bf])
            Vc, Vp = Va[par], Va[1 - par]
            kb.op(POOL, lambda: pool.tensor_copy(out=Vc[:, :, 0:64], in_=kv_sb[:, 128:256].rearrange("p (h d) -> p h d", h=2)), r=[kv_sb], w=[Vc])
            kTc, kTp = kT[par], kT[1 - par]
            pt = PSTF()

            def fn_kt():
                ins = None
                for kvh in range(2):
                    ins = pe.transpose(out=pt[0:64, kvh * 128:(kvh + 1) * 128], in_=k_bf[:, kvh * 64:(kvh + 1) * 64], identity=ident[:])
                return ins
            kb.op(PE, fn_kt, r=[k_bf, ident], w=[pt])
            kb.op(ACT, lambda: act.copy(out=kTc[:].rearrange("p a b -> p (a b)"), in_=pt[0:64, 0:256]), r=[pt], w=[kTc])
            if i == NT - 1:
                kb.dma(o_kwin[:, :], kf[:], r=[kf], sembuf=kf)
                kb.dma(o_vwin[:, :], kv_sb[:, 128:256], r=[kv_sb], sembuf=kv_sb)
        yield
        if main:
            pq = PSF()
            proj(1696, 2208, pq)
            kb.op(ACT, lambda: act.copy(out=q_sb[:], in_=pq[:]), r=[pq], w=[q_sb])
            qk_norm_rope(q_sb, v3(q_sb), 8, qnw_b, t5, v3(t5), v3(t4), None)
            kb.op(POOL, lambda: pool.tensor_copy(out=qr_bf[:], in_=t5[:]), r=[t5], w=[qr_bf])
            pt = PSTF()

            def fn_qt():
                ins = None
                for h in range(8):
                    ins = pe.transpose(out=pt[0:64, h * 128:(h + 1) * 128], in_=qr_bf[:, h * 64:(h + 1) * 64], identity=ident[:])
                return ins
            kb.op(PE, fn_qt, r=[qr_bf, ident], w=[pt])
            kb.op(ACT, lambda: act.copy(out=qT[:].rearrange("p a b -> p (a b)"), in_=pt[0:64, :]), r=[pt], w=[qT])
            yield
            mprev = amp0 if mi == 0 else amp
            for kvh in range(2):
                Es = {}
                for which, (kTx, mk) in enumerate([(kTp, mprev), (kTc, amc)]):
                    pss = PSF()
                    kb.op(PE, lambda: pe.matmul(pss[:], lhsT=kTx[:, kvh, :], rhs=qT[:, 4 * kvh:4 * kvh + 4, :].rearrange("p a b -> p (a b)"),
                                                start=True, stop=True), r=[kTx, qT], w=[pss])
                    E = Eb[which]
                    kb.op(ACT, lambda: act.activation(out=E[:], in_=pss[:], func=AF.Exp, scale=0.125), r=[pss], w=[E])
                    kb.op(POOL, lambda: pool.tensor_tensor(out=E[:].rearrange("p (j t) -> p j t", j=4), in0=E[:].rearrange("p (j t) -> p j t", j=4),
                                                           in1=mk[:].unsqueeze(1).to_broadcast([128, 4, 128]), op=ALU.mult), r=[E, mk], w=[E])
                    Es[which] = E
                po = PSF()

                def fn_o():
                    ins = None
                    for j in range(4):
                        pe.matmul(po[:, j * 65:(j + 1) * 65], lhsT=Es[0][:, j * 128:(j + 1) * 128], rhs=Vp[:, kvh, :], start=True, stop=False)
                        ins = pe.matmul(po[:, j * 65:(j + 1) * 65], lhsT=Es[1][:, j * 128:(j + 1) * 128], rhs=Vc[:, kvh, :], start=False, stop=True)
                    return ins
                kb.op(PE, fn_o, r=[Es[0], Es[1], Vp, Vc], w=[po])
                den, rden = sm[10], sm[11]
                po3 = po[:, 0:260].rearrange("p (j d) -> p j d", j=4)
                kb.op(DVE, lambda: dve.tensor_tensor(out=den[:, 0:4], in0=po3[:, :, 64], in1=esink[:, 4 * kvh:4 * kvh + 4], op=ALU.add), r=[po, esink], w=[den])
                kb.op(DVE, lambda: dve.reciprocal(out=rden[:, 0:4], in_=den[:, 0:4]), r=[den], w=[rden])
                kb.op(DVE, lambda: dve.tensor_tensor(out=mix_tok[:, 512 + 256 * kvh:512 + 256 * (kvh + 1)].rearrange("p (j d) -> p j d", j=4),
                                                     in0=po3[:, :, 0:64], in1=bc(rden[:, 0:4], 4, 64), op=ALU.mult), r=[po, rden], w=[mix_tok])
                yield
        yield
        yield from rwkv_pointwise(main, g_sb, PSF)
        if main:
            bonus_pre(bon)
        yield
        eNeg, eEx, eSuf, eL = t1, t2, t3, t5
        psl1 = PSF()
        kb.op(PE, lambda: pe.matmul(psl1[:], lhsT=miu[:], rhs=s_sb[:], start=True, stop=True), r=[miu, s_sb], w=[psl1])
        kb.op(ACT, lambda: act.activation(out=eNeg[:], in_=psl1[:], func=AF.Exp, scale=-CDEC), r=[psl1], w=[eNeg])
        if main:
            kb.op(ACT, lambda: act.activation(out=eL[:], in_=psl1[:], func=AF.Exp, scale=CDEC), r=[psl1], w=[eL])
        psl2 = PSF()
        kb.op(PE, lambda: pe.matmul(psl2[:], lhsT=msu[:], rhs=s_sb[:], start=True, stop=True), r=[msu, s_sb], w=[psl2])
        kb.op(ACT, lambda: act.activation(out=eEx[:], in_=psl2[:], func=AF.Exp, scale=CDEC), r=[psl2], w=[eEx])
        psl3 = PSF()
        kb.op(PE, lambda: pe.matmul(psl3[:], lhsT=msl[:], rhs=s_sb[:], start=True, stop=True), r=[msl, s_sb], w=[psl3])
        kb.op(ACT, lambda: act.activation(out=eSuf[:], in_=psl3[:], func=AF.Exp, scale=CDEC), r=[psl3], w=[eSuf])
        psgc = PSF()

        def fn_gc():
            ins = None
            for h in range(8):
                ins = pe.matmul(psgc[0:64, 2 * h:2 * h + 2], lhsT=s_sb[:, h * 64:(h + 1) * 64], rhs=ones_f[:, 0:2], start=True, stop=True)
            return ins
        kb.op(PE, fn_gc, r=[s_sb, ones_f], w=[psgc])
        kb.op(ACT, lambda: act.activation(out=gC[:], in_=psgc[0:64, 0:16:2], func=AF.Exp, scale=CDEC), r=[psgc], w=[gC])
        Bt_tok, Kt_tok = tk[0], tk[1]
        kb.op(DVE, lambda: dve.tensor_tensor(out=Kt_tok[:], in0=kmod[:], in1=eNeg[:], op=ALU.mult), r=[kmod, eNeg], w=[Kt_tok])
        kb.op(POOL, lambda: pool.tensor_tensor(out=Bt_tok[:], in0=bv[:], in1=eNeg[:], op=ALU.mult), r=[bv, eNeg], w=[Bt_tok])
        kb.op(DVE, lambda: dve.scalar_tensor_tensor(out=A_tok[:], in0=kkn[:], scalar=-1.0, in1=eEx[:], op0=ALU.mult, op1=ALU.mult), r=[kkn, eEx], w=[A_tok])
        yield
        if main:
            kb.op(DVE, lambda: dve.tensor_tensor(out=Rt[:], in0=r_sb[:], in1=eL[:], op=ALU.mult), r=[r_sb, eL], w=[Rt])
        yield
        arrs = [(0, A_tok), (1, Bt_tok), (2, Kt_tok)] + ([(3, Rt)] if main else [])
        for half in range(2):
            pt = PSTF()

            def fn_fm(half=half, pt=pt):
                ins = None
                for hb in (2 * half, 2 * half + 1):
                    for (ai, src) in arrs:
                        off = ((hb % 2) * 4 + ai) * 128
                        ins = pe.transpose(out=pt[:, off:off + 128], in_=src[:, hb * 128:(hb + 1) * 128], identity=ident[:])
                return ins
            kb.op(PE, fn_fm, r=[s for (_, s) in arrs] + [ident], w=[pt])
            if main:
                kb.op(ACT, lambda half=half, pt=pt: act.copy(out=FM[:, 2 * half:2 * half + 2, :, :].rearrange("p a b c -> p (a b c)"), in_=pt[:]), r=[pt], w=[FM])
            else:
                for hh in range(2):
                    hb = 2 * half + hh
                    kb.op(ACT, lambda hb=hb, hh=hh, pt=pt: act.copy(out=FM[:, hb, 0:3, :].rearrange("p a b -> p (a b)"), in_=pt[:, (hh * 4) * 128:(hh * 4 + 3) * 128]), r=[pt], w=[FM])

        kb.op(POOL, lambda: pool.tensor_tensor(out=Kh[:], in0=kmod[:], in1=eSuf[:], op=ALU.mult), r=[kmod, eSuf], w=[Kh])
        kb.op(DVE, lambda: dve.tensor_tensor(out=Bh[:], in0=bv[:], in1=eSuf[:], op=ALU.mult), r=[bv, eSuf], w=[Bh])
        kb.op(ACT, lambda: act.copy(out=Vb[:], in_=v_sb[:]), r=[v_sb], w=[Vb])

        def slot(h):
            return 4 * (h % 2) + h // 2

        def fmh(h, ai):
            base = 64 * (h % 2)
            return FM[base:base + 64, h // 2, ai, :]

        yield 'F'
        def hs_of(hg):
            return [hg + 2 * j for j in range(4)]

        def pmat(hg, ai_l, ai_r, ps):
            def fn():
                ins = None
                for j, h in enumerate(hs_of(hg)):
                    ins = pe.matmul(ps[:, j * 128:(j + 1) * 128], lhsT=fmh(h, ai_l), rhs=fmh(h, ai_r), start=True, stop=True)
                return ins
            kb.op(PE, fn, r=[FM], w=[ps])

        def mask_to(ps, mk, dst_buf, dst_ap):
            kb.op(DVE, lambda: dve.tensor_tensor(out=dst_ap, in0=ps[:].rearrange("p (j t) -> p j t", j=4),
                                                 in1=mk[:].unsqueeze(1).to_broadcast([128, 4, 128]), op=ALU.mult), r=[ps, mk], w=[dst_buf])
        for hg in range(2):
            nA, nB = nAB[hg][0], nAB[hg][1]
            p1 = PSB()
            pmat(hg, 1, 0, p1)
            mask_to(p1, msu, nA[0], nA[0][:])
            kb.op(POOL, lambda: pool.tensor_tensor(out=Xh[hg][:], in0=nA[0][:], in1=ident[:].unsqueeze(1).to_broadcast([128, 4, 128]), op=ALU.add),
                  r=[nA[0], ident], w=[Xh[hg]])
            p2 = PSB()
            pmat(hg, 2, 0, p2)
            mask_to(p2, msu, P2b, P2b[:])
            psM = PSB()

            def fn_m0():
                ins = None
                for j, h in enumerate(hs_of(hg)):
                    ins = pe.matmul(psM[:, j * 64:(j + 1) * 64], lhsT=P2b[:, j, :], rhs=Vb[:, h * 64:(h + 1) * 64], start=True, stop=True)
                return ins
            kb.op(PE, fn_m0, r=[P2b, Vb], w=[psM])
            kb.op(ACT, lambda: act.copy(out=M0b[:].rearrange("p (j q v) -> p j q v", j=4, q=2)[:, :, hg, :], in_=psM[:, 0:256].rearrange("p (j v) -> p j v", j=4)),
                  r=[psM], w=[M0b])
            pn = PSB()
            pmat(hg, 0, 1, pn)
            mask_to(pn, msl, nB[0], nB[0][:])
            yield
            if main:
                p3 = PSB()
                pmat(hg, 1, 3, p3)
                mask_to(p3, miu, P3b, P3b[:, 4 * hg:4 * hg + 4, :])
                p4 = PSB()
                pmat(hg, 2, 3, p4)
                mask_to(p4, miu, P4b, P4b[:, 4 * hg:4 * hg + 4, :])
                yield
        cur = 0
        for step in range(7):
            for hg in range(2):
                nA_, nB_ = nAB[hg][0], nAB[hg][1]
                if step > 0:
                    NTn = nB_[cur]
                    pc = PSB()

                    def fn_c():
                        ins = None
                        for j in range(4):
                            ins = pe.matmul(pc[:, j * 128:(j + 1) * 128], lhsT=NTn[:, j, :], rhs=Xh[hg][:, j, :], start=True, stop=True)
                        return ins
                    kb.op(PE, fn_c, r=[NTn, Xh[hg]], w=[pc])
                    kb.op(DVE, lambda: dve.tensor_tensor(out=Xh[hg][:], in0=pc[:].rearrange("p (j t) -> p j t", j=4),
                                                         in1=Xh[hg][:], op=ALU.add), r=[pc, Xh[hg]], w=[Xh[hg]])
                if step < 6:
                    last = step == 5
                    Np, NTp = nA_[cur], nB_[cur]
                    Np2, NTp2 = nA_[1 - cur], nB_[1 - cur]
                    if not last:
                        pa = PSB()

                        def fn_a():
                            ins = None
                            for j in range(4):
                                ins = pe.matmul(pa[:, j * 128:(j + 1) * 128], lhsT=NTp[:, j, :], rhs=Np[:, j, :], start=True, stop=True)
                            return ins
                        kb.op(PE, fn_a, r=[Np, NTp], w=[pa])
                    pb_ = PSB()

                    def fn_b():
                        ins = None
                        for j in range(4):
                            ins = pe.matmul(pb_[:, j * 128:(j + 1) * 128], lhsT=Np[:, j, :], rhs=NTp[:, j, :], start=True, stop=True)
                        return ins
                    kb.op(PE, fn_b, r=[Np, NTp], w=[pb_])
                    if not last:
                        kb.op(ACT, lambda: act.copy(out=Np2[:].rearrange("p j t -> p (j t)"), in_=pa[:]), r=[pa], w=[Np2])
                    kb.op(ACT, lambda: act.copy(out=NTp2[:].rearrange("p j t -> p (j t)"), in_=pb_[:]), r=[pb_], w=[NTp2])
            cur = 1 - cur
            yield
        psW, psU = PSB(), PSB()

        def fn_w():
            ins = None
            for h in range(8):
                ins = pe.matmul(psW[:, h * 64:(h + 1) * 64], lhsT=Xh[h % 2][:, h // 2, :], rhs=A_tok[:, h * 64:(h + 1) * 64], start=True, stop=True)
            return ins

        def fn_u():
            ins = None
            for h in range(8):
                ins = pe.matmul(psU[:, h * 64:(h + 1) * 64], lhsT=Xh[h % 2][:, h // 2, :], rhs=M0b[:, h * 64:(h + 1) * 64], start=True, stop=True)
            return ins
        kb.op(PE, fn_w, r=[Xh[0], Xh[1], A_tok], w=[psW])
        kb.op(PE, fn_u, r=[Xh[0], Xh[1], M0b], w=[psU])
        kb.op(ACT, lambda: act.copy(out=Wb[:], in_=psW[:]), r=[psW], w=[Wb])
        kb.op(DVE, lambda: dve.tensor_copy(out=U0b[:], in_=psU[:]), r=[psU], w=[U0b])
        yield
        psPsi, psQ = PSB(), PSB()

        def fn_psi():
            ins = None
            for h in range(8):
                ins = pe.matmul(psPsi[0:64, h * 64:(h + 1) * 64], lhsT=Wb[:, h * 64:(h + 1) * 64], rhs=Bh[:, h * 64:(h + 1) * 64], start=True, stop=True)
            return ins

        def fn_q():
            ins = None
            for h in range(8):
                pe.matmul(psQ[0:64, h * 64:(h + 1) * 64], lhsT=Bh[:, h * 64:(h + 1) * 64], rhs=U0b[:, h * 64:(h + 1) * 64], start=True, stop=False)
                ins = pe.matmul(psQ[0:64, h * 64:(h + 1) * 64], lhsT=Kh[:, h * 64:(h + 1) * 64], rhs=Vb[:, h * 64:(h + 1) * 64], start=False, stop=True)
            return ins
        kb.op(PE, fn_psi, r=[Wb, Bh], w=[psPsi])
        kb.op(PE, fn_q, r=[Bh, U0b, Kh, Vb], w=[psQ])
        kb.op(DVE, lambda: dve.tensor_tensor(out=Gd[:], in0=ident_f[0:64, 0:64].unsqueeze(1).to_broadcast([64, 8, 64]),
                                               in1=gC[:].unsqueeze(2).to_broadcast([64, 8, 64]), op=ALU.mult), r=[ident_f, gC], w=[Gd])
        kb.op(DVE, lambda: dve.tensor_tensor(out=PsiT[:], in0=psPsi[0:64, :], in1=Gd[:].rearrange("p h k -> p (h k)"), op=ALU.add), r=[psPsi, Gd], w=[PsiT])
        kb.op(ACT, lambda: act.copy(out=Q_sb[:], in_=psQ[0:64, :]), r=[psQ], w=[Q_sb])
        Hc, Hn = Hb[par], Hb[1 - par]
        if main:
            for half in range(2):
                pz = PSB()

                def fn_z(half=half, pz=pz):
                    ins = None
                    for j in range(4):
                        h = 4 * half + j
                        pe.matmul(pz[0:64, j * 128:(j + 1) * 128], lhsT=Wb[:, h * 64:(h + 1) * 64], rhs=P3b[:, slot(h), :], start=True, stop=False)
                        ins = pe.matmul(pz[0:64, j * 128:(j + 1) * 128], lhsT=Rt[:, h * 64:(h + 1) * 64], rhs=ident[:], start=False, stop=True)
                    return ins
                kb.op(PE, fn_z, r=[Wb, P3b, Rt, ident], w=[pz])
                kb.op(ACT, lambda half=half, pz=pz: act.copy(out=ZTb[:, 4 * half:4 * half + 4, :].rearrange("p j t -> p (j t)"), in_=pz[0:64, :]), r=[pz], w=[ZTb])
            psY = PSB()

            def fn_y():
                ins = None
                for h in range(8):
                    sl = slice(h * 64, (h + 1) * 64)
                    pe.matmul(psY[:, sl], lhsT=P3b[:, slot(h), :], rhs=U0b[:, sl], start=True, stop=False)
                    pe.matmul(psY[:, sl], lhsT=P4b[:, slot(h), :], rhs=Vb[:, sl], start=False, stop=False)
                    ins = pe.matmul(psY[:, sl], lhsT=ZTb[:, h, :], rhs=Hc[:, sl], start=False, stop=True)
                return ins
            kb.op(PE, fn_y, r=[P3b, U0b, P4b, Vb, ZTb, Hc], w=[psY])
            kb.op(ACT, lambda: act.copy(out=y_sb[:], in_=psY[:]), r=[psY], w=[y_sb])
        yield
        psH = PSB()

        def fn_h():
            ins = None
            for h in range(8):
                sl = slice(h * 64, (h + 1) * 64)
                ins = pe.matmul(psH[0:64, sl], lhsT=PsiT[:, sl], rhs=Hc[:, sl], start=True, stop=True)
            return ins
        kb.op(PE, fn_h, r=[PsiT, Hc], w=[psH])
        kb.op(DVE, lambda: dve.tensor_tensor(out=H32[:], in0=psH[0:64, :], in1=Q_sb[:], op=ALU.add), r=[psH, Q_sb], w=[H32])
        kb.op(ACT, lambda: act.copy(out=Hn[:], in_=H32[:]), r=[H32], w=[Hn])

        if main:
            yield
            groupnorm_gate2(bon, g_sb, mix_tok)
            yield

        if main:
            pt = PSTB()

            def fn_mt():
                ins = None
                for kc in range(8):
                    ins = pe.transpose(out=pt[:, kc * 128:(kc + 1) * 128], in_=mix_tok[:, kc * 128:(kc + 1) * 128], identity=ident[:])
                return ins
            kb.op(PE, fn_mt, r=[mix_tok, ident], w=[pt])
            mtt = mixTt[par]
            kb.op(ACT, lambda: act.copy(out=mtt[:].rearrange("p a b -> p (a b)"), in_=pt[:]), r=[pt], w=[mtt])
            kb.dma(mix_scr[:, :, mi * 128:(mi + 1) * 128], mtt[:], r=[mtt], w=[mix_scr], sembuf=mtt)

        if i == NT - 1:
            pass


    def run_tiles():
        prev = None
        for i in range(NT):
            g = tile_gen(i)
            while True:
                tag = next(g)
                for _ in range(int(os.environ.get('RB', '3'))):
                    if prev is not None:
                        try:
                            next(prev)
                        except StopIteration:
                            prev = None
                if tag == 'F':
                    break
            while prev is not None:
                try:
                    next(prev)
                except StopIteration:
                    prev = None
            prev = g
        while prev is not None:
            try:
                next(prev)
            except StopIteration:
                prev = None
    run_tiles()

    ck(19)
    pz = PS()

    def fn_ht():
        ins = None
        for h in range(8):
            ins = pe.matmul(pz[0:64, h * 64:(h + 1) * 64], lhsT=H32[:, h * 64:(h + 1) * 64], rhs=ident_f[0:64, 0:64], start=True, stop=True)
        return ins
    kb.op(PE, fn_ht, r=[H32, ident_f], w=[pz])
    kb.op(ACT, lambda: act.copy(out=Hfin[:].rearrange("p a b -> p (a b)"), in_=pz[0:64, :]), r=[pz], w=[Hfin])
    kb.dma(o_wkv.t.rearrange("h v k -> v h k"), Hfin[:], r=[Hfin], sembuf=Hfin)

    kb.barrier()
    esA2.close()
    stacks.remove(esA2)
    esS = ExitStack()
    stacks.append(esS)

    def ss_(name, shape, dt=F32):
        return kb.sb(name, shape, dt, stack=esS)
    def proj_s(c0, c1, ps, hTc):
        def fn():
            ins = None
            for kc in range(8):
                ins = pe.matmul(ps[:, 0:c1 - c0], lhsT=hTc[:, kc, :], rhs=win[:, kc, c0:c1], start=(kc == 0), stop=(kc == 7))
            return ins
        kb.op(PE, fn, r=[hTc, win], w=[ps])


    def qk_norm_rope_s(src_buf, src_ap3, nh, wb, out_buf, out_ap3, tmpA, cst):
        ssq, rq = sm[10], sm[11]
        kb.op(POOL, lambda: pool.tensor_tensor(out=tmpA, in0=src_ap3, in1=src_ap3, op=ALU.mult), r=[src_buf], w=[t4])
        kb.op(DVE, lambda: dve.tensor_reduce(out=ssq[:, 0:nh], in_=tmpA, axis=AX.X, op=ALU.add), r=[t4], w=[ssq])
        rsq(ssq, ssq[:, 0:nh], rq, rq[:, 0:nh], 1.0 / 64, "rms")
        kb.op(DVE, lambda: dve.tensor_tensor(out=tmpA, in0=src_ap3, in1=bc(rq[:, 0:nh], nh, 64), op=ALU.mult), r=[src_buf, rq], w=[t4])
        kb.op(POOL, lambda: pool.tensor_tensor(out=out_ap3, in0=tmpA, in1=bch(wb[:], nh, 64), op=ALU.mult), r=[t4, wb], w=[out_buf])
        kb.op(POOL, lambda: pool.tensor_copy(out=rp[0][:, 0:nh, :], in_=out_ap3[:, :, 0:8]), r=[out_buf], w=[rp[0]])
        kb.op(POOL, lambda: pool.tensor_copy(out=rp[1][:, 0:nh, :], in_=out_ap3[:, :, 8:16]), r=[out_buf], w=[rp[1]])
        cosb = cst[:, 0, :].unsqueeze(1).to_broadcast([128, nh, 8])
        sinb = cst[:, 1, :].unsqueeze(1).to_broadcast([128, nh, 8])
        kb.op(DVE, lambda: dve.tensor_tensor(out=rp[2][:, 0:nh, :], in0=rp[0][:, 0:nh, :], in1=cosb, op=ALU.mult), r=[rp[0], cst], w=[rp[2]])
        kb.op(DVE, lambda: dve.tensor_tensor(out=rp[3][:, 0:nh, :], in0=rp[1][:, 0:nh, :], in1=sinb, op=ALU.mult), r=[rp[1], cst], w=[rp[3]])
        kb.op(DVE, lambda: dve.tensor_tensor(out=out_ap3[:, :, 0:8], in0=rp[2][:, 0:nh, :], in1=rp[3][:, 0:nh, :], op=ALU.subtract), r=[rp[2], rp[3]], w=[out_buf])
        kb.op(DVE, lambda: dve.tensor_tensor(out=rp[2][:, 0:nh, :], in0=rp[1][:, 0:nh, :], in1=cosb, op=ALU.mult), r=[rp[1], cst], w=[rp[2]])
        kb.op(DVE, lambda: dve.tensor_tensor(out=rp[3][:, 0:nh, :], in0=rp[0][:, 0:nh, :], in1=sinb, op=ALU.mult), r=[rp[0], cst], w=[rp[3]])
        kb.op(DVE, lambda: dve.tensor_tensor(out=out_ap3[:, :, 8:16], in0=rp[2][:, 0:nh, :], in1=rp[3][:, 0:nh, :], op=ALU.add), r=[rp[2], rp[3]], w=[out_buf])


    xs_pad = din("xs_pad", [128, D])
    st_wkv = din("st_wkv", [128, 4096])
    st_shift = din("st_shift", [128, DS])
    c_k = din("cache_k", [16, 128, 128])
    c_v = din("cache_v", [16, 128, 128])
    c_cos_s = din("c_cos_s", [128, 8])
    c_sin_s = din("c_sin_s", [128, 8])
    s_wkv = dout("s_wkv", [128, 4096])
    s_shift = dout("s_shift", [16, DS])
    s_kwin = dout("s_kwin", [16, 128, 128])
    s_vwin = dout("s_vwin", [16, 128, 128])
    vscr = dout("vscr", [6, 16, 512])
    yscr = dout("yscr", [2, 16, 512])

    praw = ss_("praw", [128, DS])
    spv = ss_("spv", [128, DS])
    mul_b = ss_("mul_b", [128, 160])
    vecp = ss_("vecp", [128, 6, 64])
    y_p = ss_("y_p", [128, 64])
    sa_p = ss_("sa_p", [128, 8])
    Kst = ss_("Kst", [128, 8, 128])
    Kw_bf = ss_("Kw_bf", [128, 16, 128], BF16)
    Vaug = ss_("Vaug", [128, 16, 2, 65], BF16)
    KT_bf = ss_("KT_bf", [64, 32, 128], BF16)
    E_bf = ss_("E_bf", [128, 128], BF16)
    OT_sb = ss_("OT_sb", [65, 128])
    esk = ss_("esk", [128, 1])
    den_p = ss_("den_p", [128, 1])
    ya_p = ss_("ya_p", [128, 64])

    kb.dma(spv[:], st_shift[:, :], w=[spv], sembuf=spv)
    kb.dma(mul_b[:], vec["mu_shift"][0:1, 1536:1696].partition_broadcast(128), w=[mul_b], sembuf=mul_b)
    for t_ in range(16):
        kb.dma(esk[8 * t_:8 * t_ + 8, 0:1], vec["sinks"][0:1, 0:8].rearrange("o h -> h o"), w=[esk], sembuf=esk, allow_slow_non_contiguous=True)
    kb.op(ACT, lambda: act.activation(out=esk[:], in_=esk[:], func=AF.Exp), r=[esk], w=[esk])
    kb.op(POOL, lambda: pool.memset(Vaug[:], 1.0), w=[Vaug])

    for (dst_w, src_c) in [(s_kwin, c_k), (s_vwin, c_v)]:
        kb.dma(dst_w[:, 0:127, :], src_c[:, 1:128, :], w=[dst_w], sembuf=dst_w)
    xt = xb[0]
    hTc = hT[0]
    kb.dma(xt[:], xs_pad[:, :], w=[xt], sembuf=xt)
    ss, rstd = sm[0], sm[1]
    kb.op(ACT, lambda: act.activation(out=h_bf[:], in_=xt[:], func=AF.Square, accum_out=ss[:, 0:1]), r=[xt], w=[h_bf, ss])
    rsq(ss, ss[:, 0:1], rstd, rstd[:, 0:1], 1.0 / D, "rms")
    kb.op(DVE, lambda: dve.scalar_tensor_tensor(out=h_bf[:], in0=xt[:], scalar=rstd[:, 0:1], in1=nmw_b[:], op0=ALU.mult, op1=ALU.mult),
          r=[xt, rstd, nmw_b], w=[h_bf])
    pt = PST()

    def fn_trs():
        ins = None
        for kc in range(8):
            ins = pe.transpose(out=pt[:, kc * 128:(kc + 1) * 128], in_=h_bf[:, kc * 128:(kc + 1) * 128], identity=ident[:])
        return ins
    kb.op(PE, fn_trs, r=[h_bf, ident], w=[pt])
    kb.op(ACT, lambda: act.copy(out=hTc[:].rearrange("p a b -> p (a b)"), in_=pt[:]), r=[pt], w=[hTc])

    for (c0, dst) in [(0, r_sb), (512, k_sb), (1024, v_sb)]:
        ps = PS()
        proj_s(c0, c0 + 512, ps, hTc)
        kb.op(ACT, lambda: act.copy(out=praw[:, c0:c0 + 512], in_=ps[:]), r=[ps], w=[praw])
        kb.op(DVE, lambda: dve.tensor_tensor(out=dst[:], in0=spv[:, c0:c0 + 512], in1=praw[:, c0:c0 + 512], op=ALU.subtract), r=[spv, praw], w=[dst])
        kb.op(POOL, lambda: pool.tensor_tensor(out=dst[:], in0=dst[:], in1=mu_b[:, c0:c0 + 512], op=ALU.mult), r=[dst, mu_b], w=[dst])
        kb.op(DVE, lambda: dve.tensor_tensor(out=dst[:], in0=dst[:], in1=praw[:, c0:c0 + 512], op=ALU.add), r=[dst, praw], w=[dst])
    psx = PS()
    proj_s(1536, 1696, psx, hTc)
    kb.op(ACT, lambda: act.copy(out=praw[:, 1536:1696], in_=psx[:, 0:160]), r=[psx], w=[praw])
    xl = t1[:, 0:160]
    kb.op(DVE, lambda: dve.tensor_tensor(out=xl, in0=spv[:, 1536:1696], in1=praw[:, 1536:1696], op=ALU.subtract), r=[spv, praw], w=[t1])
    kb.op(POOL, lambda: pool.tensor_tensor(out=xl, in0=xl, in1=mul_b[:], op=ALU.mult), r=[t1, mul_b], w=[t1])
    kb.op(DVE, lambda: dve.tensor_tensor(out=xl, in0=xl, in1=praw[:, 1536:1696], op=ALU.add), r=[t1, praw], w=[t1])
    kb.dma(s_shift[:, :], praw[0:16, :], r=[praw], sembuf=praw)
    kb.op(ACT, lambda: act.activation(out=t2[:, 0:32], in_=t1[:, 0:32], func=AF.Exp, scale=2.0), r=[t1], w=[t2])
    kb.op(DVE, lambda: dve.tensor_scalar_add(out=t2[:, 0:32], in0=t2[:, 0:32], scalar1=1.0), r=[t2], w=[t2])
    kb.op(DVE, lambda: dve.reciprocal(out=t2[:, 0:32], in_=t2[:, 0:32]), r=[t2], w=[t2])
    kb.op(DVE, lambda: dve.tensor_scalar(out=qr_bf[:, 0:32], in0=t2[:, 0:32], scalar1=-2.0, scalar2=1.0, op0=ALU.mult, op1=ALU.add), r=[t2], w=[qr_bf])
    kb.op(POOL, lambda: pool.tensor_copy(out=qr_bf[:, 32:64], in_=t1[:, 32:64]), r=[t1], w=[qr_bf])
    kb.op(ACT, lambda: act.activation(out=t2[:, 64:160], in_=t1[:, 64:160], func=AF.Exp, scale=-1.0), r=[t1], w=[t2])
    kb.op(DVE, lambda: dve.tensor_scalar_add(out=t2[:, 64:160], in0=t2[:, 64:160], scalar1=1.0), r=[t2], w=[t2])
    kb.op(DVE, lambda: dve.reciprocal(out=t2[:, 64:160], in_=t2[:, 64:160]), r=[t2], w=[t2])
    kb.op(POOL, lambda: pool.tensor_copy(out=qr_bf[:, 64:160], in_=t2[:, 64:160]), r=[t2], w=[qr_bf])
    pt = PST()

    def fn_trl():
        pe.transpose(out=pt[0:32, 0:128], in_=qr_bf[:, 0:32], identity=ident[:])
        pe.transpose(out=pt[0:32, 128:256], in_=qr_bf[:, 32:64], identity=ident[:])
        return pe.transpose(out=pt[0:96, 256:384], in_=qr_bf[:, 64:160], identity=ident[:])
    kb.op(PE, fn_trl, r=[qr_bf, ident], w=[pt])
    kb.op(ACT, lambda: act.copy(out=th_bf[:], in_=pt[0:32, 0:128]), r=[pt], w=[th_bf])
    kb.op(ACT, lambda: act.copy(out=al_bf[:], in_=pt[0:32, 128:256]), r=[pt], w=[al_bf])
    kb.op(ACT, lambda: act.copy(out=sg_bf[:], in_=pt[0:96, 256:384]), r=[pt], w=[sg_bf])
    for _ in rwkv_pointwise(True, g_sb, PS):
        pass
    bonus_pre(bon2[0])
    kb.op(ACT, lambda: act.activation(out=t1[:], in_=s_sb[:], func=AF.Exp, scale=CDEC), r=[s_sb], w=[t1])
    kb.op(DVE, lambda: dve.tensor_scalar_mul(out=t2[:], in0=kkn[:], scalar1=-1.0), r=[kkn], w=[t2])
    for a_, src in enumerate([t1, t2, bv, v_sb, kmod, r_sb]):
        kb.dma(vscr[a_, :, :], src[0:16, :], r=[src], w=[vscr], sembuf=src)
    kb.dma(vecp[:], vscr.t.rearrange("a t (h d) -> (t h) a d", h=8), r=[vscr], w=[vecp], sembuf=vecp)
    Sc, Tm = t3, t4
    w_b = vecp[:, 0, :].unsqueeze(1).to_broadcast([128, 8, 64])
    a_b = vecp[:, 1, :].unsqueeze(1).to_broadcast([128, 8, 64])
    b_b = vecp[:, 2, :].unsqueeze(1).to_broadcast([128, 8, 64])
    k_b = vecp[:, 4, :].unsqueeze(1).to_broadcast([128, 8, 64])
    r_b = vecp[:, 5, :].unsqueeze(1).to_broadcast([128, 8, 64])
    chunk_bufs = [(t3, t4), (t1, t2)]
    kb.dma(t3[:], st_wkv[:, 0:512], w=[t3], sembuf=t3)
    for c_ in range(8):
        Sc, Tm = chunk_bufs[c_ % 2]
        if c_ + 1 < 8:
            nxt = chunk_bufs[(c_ + 1) % 2][0]
            kb.dma(nxt[:], st_wkv[:, (c_ + 1) * 512:(c_ + 2) * 512], w=[nxt], sembuf=nxt)
        kb.op(DVE, lambda: dve.tensor_tensor(out=v3(Tm), in0=v3(Sc), in1=a_b, op=ALU.mult), r=[Sc, vecp], w=[Tm])
        kb.op(DVE, lambda: dve.tensor_reduce(out=sa_p[:], in_=v3(Tm), axis=AX.X, op=ALU.add), r=[Tm], w=[sa_p])
        kb.op(POOL, lambda: pool.tensor_tensor(out=v3(Sc), in0=v3(Sc), in1=w_b, op=ALU.mult), r=[Sc, vecp], w=[Sc])
        kb.op(DVE, lambda: dve.tensor_tensor(out=v3(Tm), in0=bc(sa_p[:], 8, 64), in1=b_b, op=ALU.mult), r=[sa_p, vecp], w=[Tm])
        kb.op(POOL, lambda: pool.tensor_tensor(out=Sc[:], in0=Sc[:], in1=Tm[:], op=ALU.add), r=[Sc, Tm], w=[Sc])
        kb.op(DVE, lambda: dve.tensor_tensor(out=v3(Tm), in0=bc(vecp[:, 3, c_ * 8:(c_ + 1) * 8], 8, 64), in1=k_b, op=ALU.mult), r=[vecp], w=[Tm])
        kb.op(POOL, lambda: pool.tensor_tensor(out=Sc[:], in0=Sc[:], in1=Tm[:], op=ALU.add), r=[Sc, Tm], w=[Sc])
        kb.dma(s_wkv[:, c_ * 512:(c_ + 1) * 512], Sc[:], r=[Sc], sembuf=Sc)
        kb.op(DVE, lambda: dve.tensor_tensor(out=v3(Tm), in0=v3(Sc), in1=r_b, op=ALU.mult), r=[Sc, vecp], w=[Tm])
        kb.op(DVE, lambda: dve.tensor_reduce(out=y_p[:, c_ * 8:(c_ + 1) * 8], in_=v3(Tm), axis=AX.X, op=ALU.add), r=[Tm], w=[y_p])
    kb.dma(yscr.t[0].rearrange("t (h d) -> (t h) d", h=8), y_p[:], r=[y_p], w=[yscr], sembuf=y_p)
    kb.op(POOL, lambda: pool.memset(y_sb[:], 0.0), w=[y_sb])
    kb.dma(y_sb[0:16, :], yscr[0, :, :], r=[yscr], w=[y_sb], sembuf=y_sb)
    groupnorm_gate2(bon2[0], g_sb, mix_tok)
    cst = cs_t[0]
    kb.dma(cst[:, 0, :], c_cos_s[:, :], w=[cst], sembuf=cst)
    kb.dma(cst[:, 1, :], c_sin_s[:, :], w=[cst], sembuf=cst)
    pkv = PS()
    proj_s(2208, 2464, pkv, hTc)
    kb.op(ACT, lambda: act.copy(out=kv_sb[:], in_=pkv[:, 0:256]), r=[pkv], w=[kv_sb])
    k3 = kv_sb[:, 0:128].rearrange("p (h d) -> p h d", h=2)
    kf3 = kf[:].rearrange("p (h d) -> p h d", h=2)
    qk_norm_rope_s(kv_sb, k3, 2, knw_b, kf, kf3, t4[:, 0:128].rearrange("p (h d) -> p h d", h=2), cst)
    pq = PS()
    proj_s(1696, 2208, pq, hTc)
    kb.op(ACT, lambda: act.copy(out=q_sb[:], in_=pq[:]), r=[pq], w=[q_sb])
    qk_norm_rope_s(q_sb, v3(q_sb), 8, qnw_b, t5, v3(t5), v3(t4), cst)
    kb.op(POOL, lambda: pool.tensor_copy(out=qr_bf[:], in_=t5[:]), r=[t5], w=[qr_bf])
    pt = PST()

    def fn_qts():
        ins = None
        for h in range(8):
            ins = pe.transpose(out=pt[0:64, h * 128:(h + 1) * 128], in_=qr_bf[:, h * 64:(h + 1) * 64], identity=ident[:])
        return ins
    kb.op(PE, fn_qts, r=[qr_bf, ident], w=[pt])
    kb.op(ACT, lambda: act.copy(out=qT[:].rearrange("p a b -> p (a b)"), in_=pt[0:64, :]), r=[pt], w=[qT])
    for (dst_w, src_c, new_buf, new_ap) in [(s_kwin, c_k, kf, kf[0:16, :]), (s_vwin, c_v, kv_sb, kv_sb[0:16, 128:256])]:
        kb.dma(dst_w[:, 127, :], new_ap, r=[new_buf], w=[dst_w], sembuf=new_buf)
    for hf_ in range(2):
        kb.dma(Kst[:], s_kwin.t[8 * hf_:8 * hf_ + 8].rearrange("t k c -> k t c"), r=[s_kwin], w=[Kst], sembuf=Kst)
        kb.op(DVE, lambda: dve.tensor_copy(out=Kw_bf[:, 8 * hf_:8 * hf_ + 8, :], in_=Kst[:]), r=[Kst], w=[Kw_bf])
    for hf_ in range(2):
        kb.dma(Kst[:], s_vwin.t[8 * hf_:8 * hf_ + 8].rearrange("t k c -> k t c"), r=[s_vwin], w=[Kst], sembuf=Kst)
        kb.op(DVE, lambda: dve.tensor_copy(out=Vaug[:, 8 * hf_:8 * hf_ + 8, :, 0:64], in_=Kst[:].rearrange("p t (h d) -> p t h d", h=2)), r=[Kst], w=[Vaug])
    for g4 in range(4):
        pt = PST()

        def fn_kts():
            ins = None
            for j in range(8):
                t_, kvh = divmod(8 * g4 + j, 2)
                ins = pe.transpose(out=pt[0:64, j * 128:(j + 1) * 128], in_=Kw_bf[:, t_, kvh * 64:(kvh + 1) * 64], identity=ident[:])
            return ins
        kb.op(PE, fn_kts, r=[Kw_bf, ident], w=[pt])
        kb.op(ACT, lambda: act.copy(out=KT_bf[:, 8 * g4:8 * g4 + 8, :].rearrange("p a b -> p (a b)"), in_=pt[0:64, :]), r=[pt], w=[KT_bf])
    psS = PS()

    def fn_sc():
        ins = None
        for t_ in range(16):
            for kvh in range(2):
                ins = pe.matmul(psS[:, t_ * 8 + 4 * kvh:t_ * 8 + 4 * kvh + 4], lhsT=KT_bf[:, 2 * t_ + kvh, :], rhs=qT[:, 4 * kvh:4 * kvh + 4, t_], start=True, stop=True)
        return ins
    kb.op(PE, fn_sc, r=[KT_bf, qT], w=[psS])
    kb.op(ACT, lambda: act.activation(out=E_bf[:], in_=psS[:, 0:128], func=AF.Exp, scale=0.125), r=[psS], w=[E_bf])
    psO = PS()

    def fn_os():
        ins = None
        for t_ in range(16):
            for kvh in range(2):
                c0_ = t_ * 8 + 4 * kvh
                ins = pe.matmul(psO[0:65, c0_:c0_ + 4], lhsT=Vaug[:, t_, kvh, :], rhs=E_bf[:, c0_:c0_ + 4], start=True, stop=True)
        return ins
    kb.op(PE, fn_os, r=[Vaug, E_bf], w=[psO])
    kb.op(ACT, lambda: act.copy(out=OT_sb[:], in_=psO[0:65, 0:128]), r=[psO], w=[OT_sb])
    psO2 = PS()
    kb.op(PE, lambda: pe.matmul(psO2[:, 0:65], lhsT=OT_sb[:], rhs=ident_f[0:65, 0:65], start=True, stop=True), r=[OT_sb, ident_f], w=[psO2])
    kb.op(DVE, lambda: dve.tensor_tensor(out=den_p[:], in0=psO2[:, 64:65], in1=esk[:], op=ALU.add), r=[psO2, esk], w=[den_p])
    kb.op(DVE, lambda: dve.reciprocal(out=den_p[:], in_=den_p[:]), r=[den_p], w=[den_p])
    kb.op(DVE, lambda: dve.tensor_scalar_mul(out=ya_p[:], in0=psO2[:, 0:64], scalar1=den_p[:, 0:1]), r=[psO2, den_p], w=[ya_p])
    kb.dma(yscr.t[1].rearrange("t (h d) -> (t h) d", h=8), ya_p[:], r=[ya_p], w=[yscr], sembuf=ya_p)
    kb.op(POOL, lambda: pool.memset(t3[:], 0.0), w=[t3])
    kb.dma(t3[0:16, :], yscr[1, :, :], r=[yscr], w=[t3], sembuf=t3)
    kb.op(POOL, lambda: pool.tensor_copy(out=mix_tok[:, 512:1024], in_=t3[:]), r=[t3], w=[mix_tok])
    pt = PST()

    def fn_mts():
        ins = None
        for kc in range(8):
            ins = pe.transpose(out=pt[:, kc * 128:(kc + 1) * 128], in_=mix_tok[:, kc * 128:(kc + 1) * 128], identity=ident[:])
        return ins
    kb.op(PE, fn_mts, r=[mix_tok, ident], w=[pt])
    mtt = mixTt[0]
    kb.op(ACT, lambda: act.copy(out=mtt[:].rearrange("p a b -> p (a b)"), in_=pt[:]), r=[pt], w=[mtt])
    kb.dma(mix_scr[:, :, SEG:SEG + 128], mtt[:], r=[mtt], w=[mix_scr], sembuf=mtt)
    ck(50)
    kb.barrier()
    esS.close()
    stacks.remove(esS)
    esA.close()
    stacks.remove(esA)
    esC = ExitStack()
    stacks.append(esC)

    def sc(name, shape, dt=F32):
        return kb.sb(name, shape, dt, stack=esC)

    TS = 256
    NSUP = SEG // TS
    wout = sc("wout", [128, 8, D], BF16)
    wup = sc("wup", [128, 8, 4 * D], BF16)
    wdn = sc("wdn", [128, 32, D], BF16)
    stgc = [sc(f"stgc{i}", [128, D]) for i in range(2)]
    xr = [sc(f"xr{i}", [128, D]) for i in range(2)]
    x1s = sc("x1s", [128, TS // 128, D])
    mixs = sc("mixs", [128, 8, TS], BF16)
    hfb = sc("hfb", [128, D], BF16)
    hfT = sc("hfT", [128, 8, TS], BF16)
    actb = sc("actb", [128, 32, TS], BF16)
    rl = [sc(f"rl{i}", [128, TS]) for i in range(2)]
    smc = [sc(f"smc{i}", [128, 1]) for i in range(2)]
    wparts = {"wout": [], "wup": [], "wdn": []}
    stage_bufs = stgc + xr
    cast_engs = [(DVE, dve), (POOL, pool), (DVE, dve), (ACT, act)]
    nld = [0]

    def load_piece(dst, src_v, kc, c0):
        st = stage_bufs[nld[0] % 4]
        eng_, e_ = cast_engs[nld[0] % 4]
        nld[0] += 1
        part = Buf(dst.t, f"{dst.name}_{kc}_{c0}")
        wparts[dst.name].append(part)
        kb.dma(st[:], src_v[kc][:, c0:c0 + D], w=[st], sembuf=st)
        if eng_ is ACT:
            kb.op(ACT, lambda: act.copy(out=dst[:, kc, c0:c0 + D], in_=st[:]), r=[st], w=[part])
        else:
            kb.op(eng_, lambda: e_.tensor_copy(out=dst[:, kc, c0:c0 + D], in_=st[:]), r=[st], w=[part])
    wo_v = w_out.t.rearrange("(kc p) n -> kc p n", p=128)
    wu_v = w_up.t.rearrange("(kc p) n -> kc p n", p=128)
    wd_v = w_dn.t.rearrange("(kc p) n -> kc p n", p=128)
    for kc in range(8):
        load_piece(wout, wo_v, kc, 0)
    for kc in range(8):
        for c0 in range(0, 4 * D, D):
            load_piece(wup, wu_v, kc, c0)
    wdn_todo = list(range(32))
    o_ys = dout("ys", [16, D])
    jobs = [(su * TS, TS // 128, None) for su in range(NSUP)] + [(SEG, 1, "s")]
    for (col0, ntl, kind) in jobs:
        ncols = ntl * 128
        kb.dma(mixs[:, :, 0:ncols], mix_scr[:, :, col0:col0 + ncols], r=[mix_scr], w=[mixs], sembuf=mixs)
        for tt in range(ntl):
            gt = col0 // 128 + tt
            xrb = xr[gt % 2]
            if kind is None:
                kb.dma(xrb[:], xext[(MT0 + gt) * 128:(MT0 + gt + 1) * 128, :], w=[xrb], sembuf=xrb)
            else:
                kb.dma(xrb[:], xs_pad[:, :], w=[xrb], sembuf=xrb)
            for half in range(2):
                po = PS()

                def fn_wo():
                    ins = None
                    for kc in range(8):
                        ins = pe.matmul(po[:], lhsT=mixs[:, kc, tt * 128:(tt + 1) * 128], rhs=wout[:, kc, half * 512:(half + 1) * 512], start=(kc == 0), stop=(kc == 7))
                    return ins
                kb.op(PE, fn_wo, r=[mixs] + wparts['wout'], w=[po])
                kb.op(DVE, lambda: dve.tensor_tensor(out=x1s[:, tt, half * 512:(half + 1) * 512], in0=po[:], in1=xrb[:, half * 512:(half + 1) * 512], op=ALU.add),
                      r=[po, xrb], w=[x1s])
            ss, rstd = smc[0], smc[1]
            kb.op(ACT, lambda: act.activation(out=hfb[:], in_=x1s[:, tt, :], func=AF.Square, accum_out=ss[:, 0:1]), r=[x1s], w=[hfb, ss])
            rsq(ss, ss[:, 0:1], rstd, rstd[:, 0:1], 1.0 / D, "rms")
            kb.op(DVE, lambda: dve.scalar_tensor_tensor(out=hfb[:], in0=x1s[:, tt, :], scalar=rstd[:, 0:1], in1=nfw_b[:], op0=ALU.mult, op1=ALU.mult),
                  r=[x1s, rstd, nfw_b], w=[hfb])
            pt = PST()

            def fn_trc():
                ins = None
                for kc in range(8):
                    ins = pe.transpose(out=pt[:, kc * 128:(kc + 1) * 128], in_=hfb[:, kc * 128:(kc + 1) * 128], identity=ident[:])
                return ins
            kb.op(PE, fn_trc, r=[hfb, ident], w=[pt])
            kb.op(ACT, lambda: act.copy(out=hfT[:, :, tt * 128:(tt + 1) * 128], in_=pt[:].rearrange("p (a b) -> p a b", a=8)), r=[pt], w=[hfT])
        for fch in range(32):
            pu = PS()

            def fn_up():
                ins = None
                for kc in range(8):
                    ins = pe.matmul(pu[:, 0:ncols], lhsT=wup[:, kc, fch * 128:(fch + 1) * 128], rhs=hfT[:, kc, 0:ncols], start=(kc == 0), stop=(kc == 7))
                return ins
            kb.op(PE, fn_up, r=[hfT] + wparts['wup'], w=[pu])
            rlb = rl[fch % 2]
            kb.op(ACT, lambda: act.activation(out=rlb[:, 0:ncols], in_=pu[:, 0:ncols], func=AF.Relu), r=[pu], w=[rlb])
            kb.op(POOL, lambda: pool.tensor_tensor(out=actb[:, fch, 0:ncols], in0=rlb[:, 0:ncols], in1=rlb[:, 0:ncols], op=ALU.mult), r=[rlb], w=[actb])
            if wdn_todo:
                load_piece(wdn, wd_v, wdn_todo.pop(0), 0)
        for tt in range(ntl):
            gt = col0 // 128 + tt
            yo = stgc[gt % 2]
            for half in range(2):
                pd = PS()

                def fn_dn():
                    ins = None
                    for fch in range(32):
                        ins = pe.matmul(pd[:], lhsT=actb[:, fch, tt * 128:(tt + 1) * 128], rhs=wdn[:, fch, half * 512:(half + 1) * 512], start=(fch == 0), stop=(fch == 31))
                    return ins
                kb.op(PE, fn_dn, r=[actb] + wparts['wdn'], w=[pd])
                kb.op(DVE, lambda: dve.tensor_tensor(out=yo[:, half * 512:(half + 1) * 512], in0=pd[:], in1=x1s[:, tt, half * 512:(half + 1) * 512], op=ALU.add),
                      r=[pd, x1s], w=[yo])
            if kind is None:
                kb.dma(y_main[gt * 128:(gt + 1) * 128, :], yo[:], r=[yo], sembuf=yo)
            else:
                kb.dma(o_ys[:, :], yo[0:16, :], r=[yo], sembuf=yo)

    kb.finish()
    esC.close()
    kb.es.close()
    return nc, kb


def _consts():
    s = np.arange(128)[:, None]
    t = np.arange(128)[None, :]
    c = {}
    c["c_ident"] = (s == t).astype(np.float32)
    c["c_sha"] = ((t == s + 1).astype(np.float32) - (t == s).astype(np.float32))
    c["c_shb"] = ((s == 127) & (t == 0)).astype(np.float32)
    c["c_msu"] = (s < t).astype(np.float32)
    c["c_miu"] = (s <= t).astype(np.float32)
    c["c_msl"] = (s > t).astype(np.float32)
    return c


def _rope_tables(pos):
    half = 8
    inv = np.power(np.float32(500000.0), -np.arange(half, dtype=np.float32) * np.float32(2.0 / 16))
    ang = pos.astype(np.float32)[:, None] * inv[None, :]
    return np.cos(ang).astype(np.float32), np.sin(ang).astype(np.float32)


def _sample_maps(c, xsm, swkv, ssh, ckw, cvw, cos_s, sin_s):
    sl = slice(16 * c, 16 * (c + 1))
    xs_pad = np.zeros((128, D), np.float32)
    xs_pad[:16] = xsm[sl]
    sp = np.zeros((128, DS), np.float32)
    sp[:16] = ssh[sl]
    return {"xs_pad": xs_pad, "st_wkv": np.ascontiguousarray(swkv[sl]).reshape(128, 4096), "st_shift": sp,
            "cache_k": np.ascontiguousarray(ckw[sl]), "cache_v": np.ascontiguousarray(cvw[sl]),
            "c_cos_s": cos_s, "c_sin_s": sin_s}


_CACHE = {}


def kernel(**inp):
    f32 = lambda a: np.ascontiguousarray(np.asarray(a, dtype=np.float32))
    if "nc" not in _CACHE:
        _CACHE["nc"] = build()
    nc, kb = _CACHE["nc"]
    xp = f32(inp["x_prompt"])
    consts = _consts()
    shared = {
        "w_in": f32(inp["w_in"][0]), "w_out": f32(inp["w_out"][0]), "w_up": f32(inp["w_ffn_up"][0]), "w_dn": f32(inp["w_ffn_down"][0]),
        "w_decay_up": f32(inp["w_decay_up"][0]), "w_a_up": f32(inp["w_a_up"][0]), "w_g_up": f32(inp["w_g_up"][0]),
    }
    for nm in ["norm_mix_w", "mu_shift", "w0", "a0", "k_k", "k_a", "ln_x_w", "ln_x_b", "q_norm_w", "k_norm_w", "sinks", "norm_ffn_w"]:
        shared[nm] = f32(inp[nm]).reshape(1, -1)
    shared["r_k"] = f32(inp["r_k"]).reshape(1, -1)
    shared.update(consts)
    in_maps = []
    xsm = f32(inp["x_sample"]).reshape(128, D)
    swkv = f32(inp["state_wkv"]).reshape(128, 8 * 64 * 64)
    ssh = f32(inp["state_shift"]).reshape(128, DS)
    ckw = f32(inp["cache_k_win"]).reshape(128, 128, 128)
    cvw = f32(inp["cache_v_win"]).reshape(128, 128, 128)
    cos_s, sin_s = _rope_tables(np.full((128,), 16384))
    for c in range(NCORE):
        b, j = c // 4, c % 4
        xe = np.zeros((NEXT, D), np.float32)
        n = SEG * (j + 1)
        xe[NEXT - n:] = xp[b, :n]
        pos = np.arange(SEG * j - 128, SEG * (j + 1))
        cos, sin = _rope_tables(pos)
        m = dict(shared)
        m["xext"] = xe
        m["c_cos"] = cos
        m["c_sin"] = sin
        m["c_mp0"] = consts["c_msl"] * (0.0 if j == 0 else 1.0)
        m.update(_sample_maps(c, xsm, swkv, ssh, ckw, cvw, cos_s, sin_s))
        in_maps.append(m)
    res = run_bass_kernel_spmd(nc, in_maps, core_ids=list(range(NCORE)))
    R = res.results
    y_prompt = np.zeros((2, 8192, D), np.float32)
    for c in range(len(R)):
        b, j = c // 4, c % 4
        y_prompt[b, j * SEG:(j + 1) * SEG] = np.asarray(R[c]["y_main"], dtype=np.float32)
    last = [min(3, len(R) - 1), min(7, len(R) - 1)]
    wkv_p = np.stack([np.asarray(R[c]["o_wkv"], dtype=np.float32) for c in last])[None]
    sh_p = np.stack([np.asarray(R[c]["o_shift"], dtype=np.float32).reshape(1, DS) for c in last])[None]
    kw_p = np.stack([np.asarray(R[c]["o_kwin"], dtype=np.float32).reshape(128, 2, 64) for c in last])[None]
    vw_p = np.stack([np.asarray(R[c]["o_vwin"], dtype=np.float32).reshape(128, 2, 64) for c in last])[None]
    nr = len(R)
    y_s = np.concatenate([np.asarray(R[c]["ys"], dtype=np.float32) for c in range(nr)], 0).reshape(16 * nr, 1, D)
    wkv_s = np.concatenate([np.asarray(R[c]["s_wkv"], dtype=np.float32).reshape(16, 8, 64, 64) for c in range(nr)], 0)[None]
    sh_s = np.concatenate([np.asarray(R[c]["s_shift"], dtype=np.float32).reshape(16, 1, DS) for c in range(nr)], 0)[None]
    kw_s = np.concatenate([np.asarray(R[c]["s_kwin"], dtype=np.float32).reshape(16, 128, 2, 64) for c in range(nr)], 0)[None]
    vw_s = np.concatenate([np.asarray(R[c]["s_vwin"], dtype=np.float32).reshape(16, 128, 2, 64) for c in range(nr)], 0)[None]
    return (y_prompt, y_s, wkv_p, sh_p, kw_p, vw_p, wkv_s, sh_s, kw_s, vw_s)
```

```python
import os
import numpy as np
import ml_dtypes
from contextlib import ExitStack
import concourse.bass as bass
import concourse.mybir as mybir
from concourse.bass_utils import run_bass_kernel_spmd

F32 = mybir.dt.float32
BF16 = mybir.dt.bfloat16
ALU = mybir.AluOpType
AF = mybir.ActivationFunctionType
AX = mybir.AxisListType

D = 1024
NCORE = 8
SEG = 2048
NEXT = 8192
NT = NEXT // 128
MT0 = (NEXT - SEG) // 128
DS = 1696
DIN = 2464
CDEC = -float(np.exp(-0.5))


class Sem:
    def __init__(self, sem, step):
        self.sem = sem
        self.step = step
        self.cnt = 0


class Eng:
    def __init__(self, name, e, sem, skip_self):
        self.name = name
        self.e = e
        self.S = Sem(sem, 1)
        self.seen = {}
        self.skip_self = skip_self


class Buf:
    def __init__(self, t, name):
        self.t = t
        self.name = name
        self.wr = None
        self.rd = []
        self.dsem = None
        self.psum = False

    def __getitem__(self, k):
        return self.t[k]


class KB:
    def __init__(self, nc):
        self.nc = nc
        self.es = ExitStack()
        self.nsem = 0
        self.pe = Eng("pe", nc.tensor, self.sem("pe"), True)
        self.act = Eng("act", nc.scalar, self.sem("act"), False)
        self.dve = Eng("dve", nc.vector, self.sem("dve"), False)
        self.pool = Eng("pool", nc.gpsimd, self.sem("pool"), False)
        self.sp = Eng("sp", nc.sync, self.sem("sp"), True)
        self.engs = [self.pe, self.act, self.dve, self.pool, self.sp]
        self.dsems = []
        self.ninst = 0

    def sem(self, name):
        self.nsem += 1
        return self.es.enter_context(self.nc.semaphore(f"s_{name}_{self.nsem}"))

    def sb(self, name, shape, dt, stack=None):
        t = (stack or self.es).enter_context(self.nc.sbuf_tensor(name, list(shape), dt))
        return Buf(t, name)

    def ps(self, name, shape, dt):
        t = self.es.enter_context(self.nc.psum_tensor(name, list(shape), dt))
        b = Buf(t, name)
        b.psum = True
        return b

    def dram(self, name, shape, dt, kind):
        t = self.nc.dram_tensor(name, list(shape), dt, kind=kind)
        return Buf(t.ap(), name)

    def _waits(self, eng, deps):
        best = {}
        for (s, c) in deps:
            if c > best.get(s, 0):
                best[s] = c
        for s, c in best.items():
            if eng.seen.get(s, 0) >= c:
                continue
            eng.e.wait_ge(s.sem, c * s.step)
            eng.seen[s] = c
            self.ninst += 1

    def op(self, eng, fn, r=(), w=()):
        deps = []
        for b in r:
            if b.wr is not None:
                deps.append(b.wr)
            if b.psum:
                deps.extend(d for d in b.rd if d[0] is not eng.S)
        for b in w:
            if b.wr is not None:
                deps.append(b.wr)
            deps.extend(b.rd)
        if eng.skip_self:
            deps = [d for d in deps if d[0] is not eng.S]
        self._waits(eng, deps)
        ins = fn()
        eng.S.cnt += 1
        ins.then_inc(eng.S.sem, 1)
        self.ninst += 1
        me = (eng.S, eng.S.cnt)
        for b in r:
            b.rd.append(me)
        for b in w:
            b.wr = me
            b.rd = []
        return ins

    def dma(self, out_ap, in_ap, r=(), w=(), sembuf=None, q=None, **kw):
        q = q or self.sp
        sb = sembuf
        if sb.dsem is None:
            sb.dsem = Sem(self.sem("d_" + sb.name), 16)
            self.dsems.append(sb.dsem)
        deps = []
        for b in r:
            if b.wr is not None:
                deps.append(b.wr)
        for b in w:
            if b.wr is not None:
                deps.append(b.wr)
            deps.extend(b.rd)
        self._waits(q, deps)
        ins = q.e.dma_start(out=out_ap, in_=in_ap, **kw)
        sb.dsem.cnt += 1
        ins.then_inc(sb.dsem.sem, 16)
        self.ninst += 1
        me = (sb.dsem, sb.dsem.cnt)
        for b in r:
            b.rd.append(me)
        for b in w:
            b.wr = me
            b.rd = []

    def barrier(self):
        for e in self.engs:
            deps = [(o.S, o.S.cnt) for o in self.engs if o is not e and o.S.cnt > 0]
            deps += [(d, d.cnt) for d in self.dsems if d.cnt > 0]
            self._waits(e, deps)

    def finish(self):
        deps = [(o.S, o.S.cnt) for o in self.engs if o is not self.sp and o.S.cnt > 0]
        deps += [(d, d.cnt) for d in self.dsems if d.cnt > 0]
        self._waits(self.sp, deps)


class _Stop(Exception):
    pass


def build(dbg=False, stage=None):
    try:
        return _build(dbg, stage)
    except _Stop as e:
        return e.args[0], e.args[1]


def _build(dbg=False, stage=None):
    nc = bass.Bass("TRN2", target_bir_lowering=False)
    kb = KB(nc)
    PE, ACT, DVE, POOL = kb.pe, kb.act, kb.dve, kb.pool
    pe, act, dve, pool = nc.tensor, nc.scalar, nc.vector, nc.gpsimd

    def din(name, shape, dt=F32):
        return kb.dram(name, shape, dt, "ExternalInput")

    def dout(name, shape, dt=F32):
        return kb.dram(name, shape, dt, "ExternalOutput")

    xext = din("xext", [NEXT, D])
    w_in = din("w_in", [D, DIN])
    w_out = din("w_out", [D, D])
    w_up = din("w_up", [D, 4 * D])
    w_dn = din("w_dn", [4 * D, D])
    vec = {}
    for nm, n in [("norm_mix_w", D), ("mu_shift", DS), ("w0", 512), ("a0", 512), ("k_k", 512), ("k_a", 512),
                  ("r_k", 512), ("ln_x_w", 512), ("ln_x_b", 512), ("q_norm_w", 64), ("k_norm_w", 64),
                  ("sinks", 8), ("norm_ffn_w", D)]:
        vec[nm] = din(nm, [1, n])
    w_dec = din("w_decay_up", [32, 512])
    w_a = din("w_a_up", [32, 512])
    w_g = din("w_g_up", [96, 512])
    c_ident = din("c_ident", [128, 128])
    c_sha = din("c_sha", [128, 128])
    c_shb = din("c_shb", [128, 128])
    c_msu = din("c_msu", [128, 128])
    c_miu = din("c_miu", [128, 128])
    c_msl = din("c_msl", [128, 128])
    c_mp0 = din("c_mp0", [128, 128])
    c_cos = din("c_cos", [SEG + 128, 8])
    c_sin = din("c_sin", [SEG + 128, 8])

    y_main = dout("y_main", [SEG, D])
    o_wkv = dout("o_wkv", [8, 64, 64])
    o_shift = dout("o_shift", [1, DS])
    o_kwin = dout("o_kwin", [128, 128])
    o_vwin = dout("o_vwin", [128, 128])

    es = kb.es
    stacks = []

    def ck(n):
        if stage == n:
            kb.finish()
            for st_ in reversed(stacks):
                st_.close()
            kb.es.close()
            raise _Stop(nc, kb)
    psb = [kb.ps(f"psb{i}", [128, 512], F32) for i in range(6)]
    pst = [kb.ps(f"pst{i}", [128, 1024], BF16) for i in range(2)]
    ring = {"i": 0, "t": 0}

    def PS():
        b = psb[ring["i"] % 5]
        ring["i"] += 1
        return b

    def PST():
        b = pst[ring["t"] % 2]
        ring["t"] += 1
        return b

    ring.update({"f": 0, "b": 0})

    def PSF():
        b = psb[ring["f"] % 2]
        ring["f"] += 1
        return b

    def PSB():
        b = psb[2 + ring["b"] % 4]
        ring["b"] += 1
        return b

    def PSTF():
        return pst[0]

    def PSTB():
        return pst[1]

    def sbt(name, shape, dt=F32):
        return kb.sb(name, shape, dt)

    ident_f = sbt("ident_f", [128, 128])
    msu = sbt("msu", [128, 128])
    miu = sbt("miu", [128, 128])
    msl = sbt("msl", [128, 128])
    mp0_f = sbt("mp0_f", [128, 128])
    ident = sbt("ident", [128, 128], BF16)
    sha = sbt("sha", [128, 128], BF16)
    shb = sbt("shb", [128, 128], BF16)
    amc = sbt("amc", [128, 128], BF16)
    amp = sbt("amp", [128, 128], BF16)
    amp0 = sbt("amp0", [128, 128], BF16)
    ones_f = sbt("ones_f", [128, 2])
    for dst, src in [(ident_f, c_ident), (msu, c_msu), (miu, c_miu), (msl, c_msl), (mp0_f, c_mp0)]:
        kb.dma(dst[:], src[:, :], w=[dst], sembuf=dst)
    stg = sbt("cstg", [128, 2, 128])
    kb.dma(stg[:, 0, :], c_sha[:, :], w=[stg], sembuf=stg)
    kb.dma(stg[:, 1, :], c_shb[:, :], w=[stg], sembuf=stg)
    kb.op(DVE, lambda: dve.tensor_copy(out=ident[:], in_=ident_f[:]), r=[ident_f], w=[ident])
    kb.op(DVE, lambda: dve.tensor_copy(out=sha[:], in_=stg[:, 0, :]), r=[stg], w=[sha])
    kb.op(DVE, lambda: dve.tensor_copy(out=shb[:], in_=stg[:, 1, :]), r=[stg], w=[shb])
    kb.op(DVE, lambda: dve.tensor_copy(out=amc[:], in_=miu[:]), r=[miu], w=[amc])
    kb.op(DVE, lambda: dve.tensor_copy(out=amp[:], in_=msl[:]), r=[msl], w=[amp])
    kb.op(DVE, lambda: dve.tensor_copy(out=amp0[:], in_=mp0_f[:]), r=[mp0_f], w=[amp0])
    kb.op(POOL, lambda: pool.memset(ones_f[:], 1.0), w=[ones_f])

    nfw_b = sbt("b_nfw", [128, D])
    kb.dma(nfw_b[:], vec["norm_ffn_w"][0:1, 0:D].partition_broadcast(128), w=[nfw_b], sembuf=nfw_b)
    eps_vals = {"rms": 1e-6, "lnx": 64e-5, "one": 1.0}
    eps_t = {}
    for k_, v_ in eps_vals.items():
        t_ = sbt("eps_" + k_, [128, 1])
        kb.op(POOL, lambda t_=t_, v_=v_: pool.memset(t_[:], v_), w=[t_])
        eps_t[k_] = t_

    ck(0)
    esA = ExitStack()
    stacks.append(esA)

    def sa(name, shape, dt=F32):
        return kb.sb(name, shape, dt, stack=esA)

    def bvec(name, src, n, c0=0):
        t = sa("b_" + name, [128, n])
        kb.dma(t[:], src[0:1, c0:c0 + n].partition_broadcast(128), w=[t], sembuf=t)
        return t

    nmw_b = bvec("nmw", vec["norm_mix_w"], D)
    mu_b = bvec("mu", vec["mu_shift"], 1536)
    w0_b = bvec("w0", vec["w0"], 512)
    a0_b = bvec("a0", vec["a0"], 512)
    kk_b = bvec("kk", vec["k_k"], 512)
    ka_b = bvec("ka", vec["k_a"], 512)
    rk_b = bvec("rk", vec["r_k"], 512)
    lnw_b = bvec("lnw", vec["ln_x_w"], 512)
    lnb_b = bvec("lnb", vec["ln_x_b"], 512)
    qnw_b = bvec("qnw", vec["q_norm_w"], 64)
    knw_b = bvec("knw", vec["k_norm_w"], 64)
    snk_b = bvec("snk", vec["sinks"], 8)
    esink = sa("esink", [128, 8])
    kb.op(ACT, lambda: act.activation(out=esink[:], in_=snk_b[:], func=AF.Exp), r=[snk_b], w=[esink])
    mul = sa("mul", [96, 3])
    kb.op(POOL, lambda: pool.memset(mul[:], 0.0), w=[mul])
    for g, (c0, n) in enumerate([(1536, 32), (1568, 32), (1600, 96)]):
        kb.dma(mul[0:n, g:g + 1], vec["mu_shift"][0:1, c0:c0 + n].rearrange("o n -> n o"), w=[mul], sembuf=mul,
               allow_slow_non_contiguous=True)
    mix_scr = kb.dram("d_mix", [128, 8, SEG + 128], BF16, "ExternalOutput")

    lw = sa("lw", [96, 3, 512], BF16)
    win = sa("win", [128, 8, DIN], BF16)
    xb = [sa(f"xb{i}", [128, D]) for i in range(2)]
    w_in_v = w_in.t.rearrange("(kc p) n -> kc p n", p=128)
    HW = DIN // 4
    for kc in range(8):
        for hf_ in range(4):
            st = xb[hf_ % 2]
            kb.dma(st[:, 0:HW], w_in_v[kc][:, hf_ * HW:(hf_ + 1) * HW], w=[st], sembuf=st)
            kb.op(POOL, lambda: pool.tensor_copy(out=win[:, kc, hf_ * HW:(hf_ + 1) * HW], in_=st[:, 0:HW]), r=[st], w=[win])
    for g, (src, n) in enumerate([(w_dec, 32), (w_a, 32), (w_g, 96)]):
        st = xb[g % 2]
        kb.dma(st[0:n, 0:512], src[:, :], w=[st], sembuf=st)
        kb.op(DVE, lambda: dve.tensor_copy(out=lw[0:n, g, :], in_=st[0:n, 0:512]), r=[st], w=[lw])
    sm = [sa(f"sm{i}", [128, 8]) for i in range(12)]
    h_bf = sa("h_bf", [128, D], BF16)
    hT = [sa("hT0", [128, 8, 128], BF16)] * 2
    r_sb = sa("r_sb", [128, 512])
    k_sb = sa("k_sb", [128, 512])
    v_sb = sa("v_sb", [128, 512])
    lt = [sa(f"lt{i}", [96, 3, 128]) for i in range(2)]
    lt.append(lt[1])
    th_bf = sa("th_bf", [32, 128], BF16)
    al_bf = sa("al_bf", [32, 128], BF16)
    sg_bf = sa("sg_bf", [96, 128], BF16)
    f = [sa(f"f{i}", [128, 512]) for i in range(12)]
    s_sb, asig, kk, kkn, kmod, bv, g_sb = f[0], f[1], f[2], f[3], f[4], f[5], f[6]
    t1, t2, t3, t4, t5 = f[7], f[8], f[9], f[10], f[11]
    y_sb = sa("y_sb", [128, 512])
    mix_tok2 = [sa(f"mix_tok{i}", [128, D], BF16) for i in range(2)]
    mix_tok = mix_tok2[0]
    bon2 = [sa(f"bon{i}", [128, 512]) for i in range(2)]
    g2 = [g_sb, sa("g2b", [128, 512])]
    gt1 = sa("gt1", [128, 512])
    gt2 = sa("gt2", [128, 512])
    mixTt = [sa("mixTt0", [128, 8, 128], BF16)] * 2
    q_sb = s_sb
    kv_sb = sa("kv_sb", [128, 256])
    kf = sa("kf", [128, 128])
    qr_bf = sa("qr_bf", [128, 512], BF16)
    k_bf = sa("k_bf", [128, 128], BF16)
    qT = sa("qT", [64, 8, 128], BF16)
    cs_t = [sa(f"cs{i}", [128, 2, 8]) for i in range(2)]
    rp = [sa(f"rp{i}", [128, 8, 8]) for i in range(4)]
    esA2 = ExitStack()
    stacks.append(esA2)

    def sa2(name, shape, dt=F32):
        return kb.sb(name, shape, dt, stack=esA2)

    pm = [{c0: sa2(f"pm{i}_{c0}", [128, 512], BF16) for c0 in (0, 512, 1024)} for i in range(2)]
    for pd_ in pm:
        for p_ in pd_.values():
            kb.op(POOL, lambda p_=p_: pool.memset(p_[:], 0.0), w=[p_])
    lro = sa2("lro", [96, 3, 129])
    kb.op(POOL, lambda: pool.memset(lro[:], 0.0), w=[lro])
    A_tok2 = [sa2(f"A_tok{i}", [128, 512], BF16) for i in range(2)]
    Bh2 = [sa2(f"Bh{i}", [128, 512], BF16) for i in range(2)]
    Kh2 = [sa2(f"Kh{i}", [128, 512], BF16) for i in range(2)]
    Vb2 = [sa2(f"Vb{i}", [128, 512], BF16) for i in range(2)]
    Rt2 = [sa2(f"Rt{i}", [128, 512], BF16) for i in range(2)]
    tk = [sa2(f"tk{i}", [128, 512], BF16) for i in range(2)]
    FM2 = [sa2(f"FM{i}", [128, 4, 4, 128], BF16) for i in range(2)]
    gC2 = [sa2(f"gC{i}", [64, 8]) for i in range(2)]
    Gd = sa2("Gd", [64, 8, 64])
    nAB = [[[sa2(f"n{ab}{hg}{i}", [128, 4, 128], BF16) for i in range(2)] for ab in "AB"] for hg in range(2)]
    Xh = [sa2(f"Xh{i}", [128, 4, 128], BF16) for i in range(2)]
    P2b = sa2("P2b", [128, 4, 128], BF16)
    P3b = sa2("P3b", [128, 8, 128], BF16)
    P4b = sa2("P4b", [128, 8, 128], BF16)
    M0b = sa2("M0b", [128, 512], BF16)
    Wb = sa2("Wb", [128, 512], BF16)
    U0b = sa2("U0b", [128, 512], BF16)
    PsiT = sa2("PsiT", [64, 512], BF16)
    Q_sb = sa2("Q_sb", [64, 512])
    ZTb = sa2("ZTb", [64, 8, 128], BF16)
    H32 = sa2("H32", [64, 512])
    Hb = [sa2(f"Hb{i}", [64, 512], BF16) for i in range(2)]
    kb.op(POOL, lambda: pool.memset(Hb[0][:], 0.0), w=[Hb[0]])
    kb.op(POOL, lambda: pool.memset(H32[:], 0.0), w=[H32])
    kT = [sa2(f"kT{i}", [64, 2, 128], BF16) for i in range(2)]
    Va = [sa2(f"Va{i}", [128, 2, 65], BF16) for i in range(2)]
    for v_ in Va:
        kb.op(POOL, lambda v_=v_: pool.memset(v_[:], 1.0), w=[v_])
    Eb = [sa2(f"Eb{i}", [128, 512], BF16) for i in range(2)]
    Hfin = sa2("Hfin", [64, 8, 64])

    def v3(b, n=8):
        return b[:].rearrange("p (h d) -> p h d", h=n)

    def bc(ap_small, n, d):
        return ap_small.unsqueeze(2).to_broadcast([128, n, d])

    def bch(ap_vec, n, d):
        return ap_vec.unsqueeze(1).to_broadcast([128, n, d])

    def sigmoid_chain(src_ap, rbufs, tmp, dst, shape_ap=lambda b: b[:]):
        kb.op(ACT, lambda: act.activation(out=shape_ap(tmp), in_=src_ap, func=AF.Exp, scale=-1.0), r=rbufs, w=[tmp])
        kb.op(ACT, lambda: act.activation(out=shape_ap(tmp), in_=shape_ap(tmp), func=AF.Ln, bias=eps_t["one"][0:shape_ap(tmp).shape[0], 0:1]),
              r=[tmp, eps_t["one"]], w=[tmp])
        kb.op(ACT, lambda: act.activation(out=shape_ap(dst), in_=shape_ap(tmp), func=AF.Exp, scale=-1.0), r=[tmp], w=[dst])

    def rsq(src_buf, src_ap, dst_buf, dst_ap, scale, eps_key):
        n = dst_ap.shape[0]
        kb.op(ACT, lambda: act.activation(out=dst_ap, in_=src_ap, func=AF.Ln, scale=scale, bias=eps_t[eps_key][0:n, 0:1]),
              r=[src_buf, eps_t[eps_key]], w=[dst_buf])
        kb.op(ACT, lambda: act.activation(out=dst_ap, in_=dst_ap, func=AF.Exp, scale=-0.5), r=[dst_buf], w=[dst_buf])

    def load_x(i):
        b = xb[i % 2]
        kb.dma(b[:], xext[i * 128:(i + 1) * 128, :], w=[b], sembuf=b)

    def rwkv_pointwise(main, g_sb, alloc):
        psz, psza = alloc(), alloc()
        kb.op(PE, lambda: pe.matmul(psz[:], lhsT=th_bf[:], rhs=lw[0:32, 0, :], start=True, stop=True), r=[th_bf, lw], w=[psz])
        kb.op(PE, lambda: pe.matmul(psza[:], lhsT=al_bf[:], rhs=lw[0:32, 1, :], start=True, stop=True), r=[al_bf, lw], w=[psza])
        kb.op(DVE, lambda: dve.tensor_tensor(out=t1[:], in0=psz[:], in1=w0_b[:], op=ALU.add), r=[psz, w0_b], w=[t1])
        sigmoid_chain(t1[:], [t1], t2, s_sb)
        yield
        kb.op(DVE, lambda: dve.tensor_tensor(out=t1[:], in0=psza[:], in1=a0_b[:], op=ALU.add), r=[psza, a0_b], w=[t1])
        sigmoid_chain(t1[:], [t1], t2, asig)
        yield
        if main:
            psg = alloc()
            kb.op(PE, lambda: pe.matmul(psg[:], lhsT=sg_bf[:], rhs=lw[0:96, 2, :], start=True, stop=True), r=[sg_bf, lw], w=[psg])
            kb.op(ACT, lambda: act.copy(out=g_sb[:], in_=psg[:]), r=[psg], w=[g_sb])
        yield
        n2, rn = sm[2], sm[3]
        kb.op(DVE, lambda: dve.tensor_tensor(out=kk[:], in0=k_sb[:], in1=kk_b[:], op=ALU.mult), r=[k_sb, kk_b], w=[kk])
        kb.op(POOL, lambda: pool.tensor_tensor(out=t3[:], in0=kk[:], in1=kk[:], op=ALU.mult), r=[kk], w=[t3])
        kb.op(DVE, lambda: dve.tensor_reduce(out=n2[:], in_=v3(t3), axis=AX.X, op=ALU.add), r=[t3], w=[n2])
        kb.op(DVE, lambda: dve.tensor_scalar_max(out=n2[:], in0=n2[:], scalar1=1e-24), r=[n2], w=[n2])
        yield
        kb.op(ACT, lambda: act.activation(out=rn[:], in_=n2[:], func=AF.Ln), r=[n2], w=[rn])
        kb.op(ACT, lambda: act.activation(out=rn[:], in_=rn[:], func=AF.Exp, scale=-0.5), r=[rn], w=[rn])
        kb.op(DVE, lambda: dve.tensor_tensor(out=v3(kkn), in0=v3(kk), in1=bc(rn[:], 8, 64), op=ALU.mult), r=[kk, rn], w=[kkn])
        yield
        kb.op(DVE, lambda: dve.scalar_tensor_tensor(out=t4[:], in0=asig[:], scalar=-1.0, in1=ka_b[:], op0=ALU.add, op1=ALU.mult), r=[asig, ka_b], w=[t4])
        kb.op(DVE, lambda: dve.scalar_tensor_tensor(out=kmod[:], in0=t4[:], scalar=1.0, in1=k_sb[:], op0=ALU.add, op1=ALU.mult), r=[t4, k_sb], w=[kmod])
        kb.op(POOL, lambda: pool.tensor_tensor(out=bv[:], in0=kkn[:], in1=asig[:], op=ALU.mult), r=[kkn, asig], w=[bv])

    def bonus_pre(bon):
        bs = sm[9]
        kb.op(POOL, lambda: pool.tensor_tensor(out=t3[:], in0=r_sb[:], in1=kmod[:], op=ALU.mult), r=[r_sb, kmod], w=[t3])
        kb.op(POOL, lambda: pool.tensor_tensor(out=t3[:], in0=t3[:], in1=rk_b[:], op=ALU.mult), r=[t3, rk_b], w=[t3])
        kb.op(DVE, lambda: dve.tensor_reduce(out=bs[:], in_=v3(t3), axis=AX.X, op=ALU.add), r=[t3], w=[bs])
        kb.op(DVE, lambda: dve.tensor_tensor(out=v3(bon), in0=v3(v_sb), in1=bc(bs[:], 8, 64), op=ALU.mult), r=[v_sb, bs], w=[bon])

    def groupnorm_gate2(bon, g_sb, mix_tok):
        gs, gq, gm, gv, gr = sm[4], sm[5], sm[6], sm[7], sm[8]
        kb.op(DVE, lambda: dve.tensor_reduce(out=gs[:], in_=v3(y_sb), axis=AX.X, op=ALU.add), r=[y_sb], w=[gs])
        kb.op(POOL, lambda: pool.tensor_tensor(out=gt1[:], in0=y_sb[:], in1=y_sb[:], op=ALU.mult), r=[y_sb], w=[gt1])
        kb.op(DVE, lambda: dve.tensor_reduce(out=gq[:], in_=v3(gt1), axis=AX.X, op=ALU.add), r=[gt1], w=[gq])
        kb.op(DVE, lambda: dve.tensor_scalar_mul(out=gm[:], in0=gs[:], scalar1=1.0 / 64), r=[gs], w=[gm])
        kb.op(DVE, lambda: dve.tensor_tensor(out=gv[:], in0=gm[:], in1=gm[:], op=ALU.mult), r=[gm], w=[gv])
        kb.op(DVE, lambda: dve.scalar_tensor_tensor(out=gv[:], in0=gq[:], scalar=1.0 / 64, in1=gv[:], op0=ALU.mult, op1=ALU.subtract), r=[gq, gv], w=[gv])
        rsq(gv, gv[:], gr, gr[:], 1.0, "lnx")
        kb.op(DVE, lambda: dve.tensor_tensor(out=v3(gt2), in0=v3(y_sb), in1=bc(gm[:], 8, 64), op=ALU.subtract), r=[y_sb, gm], w=[gt2])
        kb.op(DVE, lambda: dve.tensor_tensor(out=v3(gt2), in0=v3(gt2), in1=bc(gr[:], 8, 64), op=ALU.mult), r=[gt2, gr], w=[gt2])
        kb.op(POOL, lambda: pool.tensor_tensor(out=gt2[:], in0=gt2[:], in1=lnw_b[:], op=ALU.mult), r=[gt2, lnw_b], w=[gt2])
        kb.op(POOL, lambda: pool.tensor_tensor(out=gt2[:], in0=gt2[:], in1=lnb_b[:], op=ALU.add), r=[gt2, lnb_b], w=[gt2])
        kb.op(POOL, lambda: pool.tensor_tensor(out=gt2[:], in0=gt2[:], in1=bon[:], op=ALU.add), r=[gt2, bon], w=[gt2])
        kb.op(POOL, lambda: pool.tensor_tensor(out=mix_tok[:, 0:512], in0=gt2[:], in1=g_sb[:], op=ALU.mult), r=[gt2, g_sb], w=[mix_tok])

    ck(1)
    load_x(0)
    def tile_gen(i):
        par = i % 2
        A_tok, Bh, Kh, Vb, Rt, FM, gC = A_tok2[par], Bh2[par], Kh2[par], Vb2[par], Rt2[par], FM2[par], gC2[par]
        mix_tok, bon, g_sb = mix_tok2[par], bon2[par], g2[par]
        par = i % 2
        main = i >= MT0
        halo = i == MT0 - 1
        mi = i - MT0
        if i + 1 < NT:
            load_x(i + 1)
        xt = xb[par]
        ss, rstd = sm[0], sm[1]
        yield
        kb.op(ACT, lambda: act.activation(out=h_bf[:], in_=xt[:], func=AF.Square, accum_out=ss[:, 0:1]), r=[xt], w=[h_bf, ss])
        rsq(ss, ss[:, 0:1], rstd, rstd[:, 0:1], 1.0 / D, "rms")
        kb.op(DVE, lambda: dve.scalar_tensor_tensor(out=h_bf[:], in0=xt[:], scalar=rstd[:, 0:1], in1=nmw_b[:], op0=ALU.mult, op1=ALU.mult),
              r=[xt, rstd, nmw_b], w=[h_bf])
        yield
        pt = PSTF()

        def fn_tr():
            ins = None
            for kc in range(8):
                ins = pe.transpose(out=pt[:, kc * 128:(kc + 1) * 128], in_=h_bf[:, kc * 128:(kc + 1) * 128], identity=ident[:])
            return ins
        kb.op(PE, fn_tr, r=[h_bf, ident], w=[pt])
        hTc = hT[par]
        kb.op(ACT, lambda: act.copy(out=hTc[:].rearrange("p a b -> p (a b)"), in_=pt[:]), r=[pt], w=[hTc])

        yield
        def proj(c0, c1, ps, stop=True):
            def fn():
                ins = None
                for kc in range(8):
                    ins = pe.matmul(ps[:, 0:c1 - c0], lhsT=hTc[:, kc, :], rhs=win[:, kc, c0:c1], start=(kc == 0), stop=(kc == 7))
                return ins
            kb.op(PE, fn, r=[hTc, win], w=[ps])

        psl = PSF()

        def fn_lo():
            ins = None
            for g, (c0, n) in enumerate([(1536, 32), (1568, 32), (1600, 96)]):
                if g == 2 and not (main or halo):
                    continue
                for kc in range(8):
                    ins = pe.matmul(psl[0:n, g * 128:(g + 1) * 128], lhsT=win[:, kc, c0:c0 + n], rhs=hTc[:, kc, :], start=(kc == 0), stop=(kc == 7))
            return ins
        kb.op(PE, fn_lo, r=[hTc, win], w=[psl])
        ng = 3 if main else 2
        kb.op(ACT, lambda: act.copy(out=lro[0:32, 0:2, 1:129], in_=psl[0:32, 0:256].rearrange("p (g t) -> p g t", g=2)), r=[psl], w=[lro])
        if main or halo:
            kb.op(ACT, lambda: act.copy(out=lro[0:96, 2, 1:129], in_=psl[0:96, 256:384]), r=[psl], w=[lro])
        if dbg and i == NT - 1:
            pass
        kb.op(DVE, lambda: dve.tensor_tensor(out=lt[0][:, 0:ng, :], in0=lro[:, 0:ng, 0:128], in1=lro[:, 0:ng, 1:129], op=ALU.subtract), r=[lro], w=[lt[0]])
        kb.op(DVE, lambda: dve.tensor_tensor(out=lt[0][:, 0:ng, :], in0=lt[0][:, 0:ng, :], in1=mul[:, 0:ng].unsqueeze(2).to_broadcast([96, ng, 128]), op=ALU.mult),
              r=[lt[0], mul], w=[lt[0]])
        kb.op(DVE, lambda: dve.tensor_tensor(out=lt[0][:, 0:ng, :], in0=lt[0][:, 0:ng, :], in1=lro[:, 0:ng, 1:129], op=ALU.add), r=[lt[0], lro], w=[lt[0]])
        if i == NT - 1 and not os.environ.get('NO_LAST_B'):
            psx = PSF()
            proj(1536, 1696, psx)
            kb.op(ACT, lambda: act.copy(out=t4[:, 0:160], in_=psx[:, 0:160]), r=[psx], w=[t4])
            kb.dma(o_shift[0:1, 1536:1696], t4[127:128, 0:160], r=[t4], sembuf=t4)
        kb.op(POOL, lambda: pool.tensor_copy(out=lro[:, :, 0:1], in_=lro[:, :, 128:129]), r=[lro], w=[lro])
        kb.op(ACT, lambda: act.activation(out=lt[1][0:32, 0, :], in_=lt[0][0:32, 0, :], func=AF.Exp, scale=2.0), r=[lt[0]], w=[lt[1]])
        kb.op(DVE, lambda: dve.tensor_scalar_add(out=lt[1][0:32, 0, :], in0=lt[1][0:32, 0, :], scalar1=1.0), r=[lt[1]], w=[lt[1]])
        kb.op(DVE, lambda: dve.reciprocal(out=lt[1][0:32, 0, :], in_=lt[1][0:32, 0, :]), r=[lt[1]], w=[lt[1]])
        kb.op(DVE, lambda: dve.tensor_scalar(out=th_bf[:], in0=lt[1][0:32, 0, :], scalar1=-2.0, scalar2=1.0, op0=ALU.mult, op1=ALU.add), r=[lt[1]], w=[th_bf])
        kb.op(POOL, lambda: pool.tensor_copy(out=al_bf[:], in_=lt[0][0:32, 1, :]), r=[lt[0]], w=[al_bf])
        if main:
            kb.op(ACT, lambda: act.activation(out=lt[2][0:96, 2, :], in_=lt[0][0:96, 2, :], func=AF.Exp, scale=-1.0), r=[lt[0]], w=[lt[2]])
            kb.op(DVE, lambda: dve.tensor_scalar_add(out=lt[2][0:96, 2, :], in0=lt[2][0:96, 2, :], scalar1=1.0), r=[lt[2]], w=[lt[2]])
            kb.op(DVE, lambda: dve.reciprocal(out=lt[2][0:96, 2, :], in_=lt[2][0:96, 2, :]), r=[lt[2]], w=[lt[2]])
            kb.op(POOL, lambda: pool.tensor_copy(out=sg_bf[:], in_=lt[2][0:96, 2, :]), r=[lt[2]], w=[sg_bf])
        cols = [("k", 512, k_sb), ("v", 1024, v_sb)] + ([("r", 0, r_sb)] if (main or halo) else [])
        pmc, pmp = pm[par], pm[1 - par]
        for g0 in range(0, len(cols), 2):
            grp = []
            for (nm, c0, dst) in cols[g0:g0 + 2]:
                ps = PSF()
                proj(c0, c0 + 512, ps)
                grp.append((c0, dst, ps))
            for (c0, dst, ps) in grp:
                if i == NT - 1:
                    rawb, rawap = {0: (xb[0], xb[0][:, 0:512]), 512: (xb[0], xb[0][:, 512:1024]), 1024: (t5, t5[:])}[c0]
                    kb.op(ACT, lambda: act.copy(out=rawap, in_=ps[:]), r=[ps], w=[rawb])
                    kb.dma(o_shift[0:1, c0:c0 + 512], rawap[127:128, :], r=[rawb], sembuf=rawb)
                kb.op(DVE, lambda: dve.tensor_tensor(out=pmc[c0][:], in0=ps[:], in1=mu_b[:, c0:c0 + 512], op=ALU.mult),
                      r=[ps, mu_b], w=[pmc[c0]])
            for (c0, dst, ps) in grp:
                def fn_l(ps=ps, c0=c0):
                    pe.matmul(ps[:], lhsT=sha[:], rhs=pmc[c0][:], start=False, stop=False, skip_group_check=True)
                    return pe.matmul(ps[:], lhsT=shb[:], rhs=pmp[c0][:], start=False, stop=True, skip_group_check=True)
                kb.op(PE, fn_l, r=[pmc[c0], pmp[c0], sha, shb], w=[ps])
                kb.op(ACT, lambda: act.copy(out=dst[:], in_=ps[:]), r=[ps], w=[dst])
        if not main:
            pass

        yield
        if main or halo:
            ai = mi + 1
            cst = cs_t[par]
            kb.dma(cst[:, 0, :], c_cos[ai * 128:(ai + 1) * 128, :], w=[cst], sembuf=cst)
            kb.dma(cst[:, 1, :], c_sin[ai * 128:(ai + 1) * 128, :], w=[cst], sembuf=cst)
            pkv = PSF()
            proj(2208, 2464, pkv)
            kb.op(ACT, lambda: act.copy(out=kv_sb[:], in_=pkv[:, 0:256]), r=[pkv], w=[kv_sb])

            def qk_norm_rope(src_buf, src_ap3, nh, wb, out_buf, out_ap3, tmpA, tmpB):
                ssq, rq = sm[10], sm[11]
                kb.op(POOL, lambda: pool.tensor_tensor(out=tmpA, in0=src_ap3, in1=src_ap3, op=ALU.mult), r=[src_buf], w=[t4])
                kb.op(DVE, lambda: dve.tensor_reduce(out=ssq[:, 0:nh], in_=tmpA, axis=AX.X, op=ALU.add), r=[t4], w=[ssq])
                rsq(ssq, ssq[:, 0:nh], rq, rq[:, 0:nh], 1.0 / 64, "rms")
                kb.op(DVE, lambda: dve.tensor_tensor(out=tmpA, in0=src_ap3, in1=bc(rq[:, 0:nh], nh, 64), op=ALU.mult), r=[src_buf, rq], w=[t4])
                kb.op(POOL, lambda: pool.tensor_tensor(out=out_ap3, in0=tmpA, in1=bch(wb[:], nh, 64), op=ALU.mult), r=[t4, wb], w=[out_buf])
                kb.op(POOL, lambda: pool.tensor_copy(out=rp[0][:, 0:nh, :], in_=out_ap3[:, :, 0:8]), r=[out_buf], w=[rp[0]])
                kb.op(POOL, lambda: pool.tensor_copy(out=rp[1][:, 0:nh, :], in_=out_ap3[:, :, 8:16]), r=[out_buf], w=[rp[1]])
                cosb = cst[:, 0, :].unsqueeze(1).to_broadcast([128, nh, 8])
                sinb = cst[:, 1, :].unsqueeze(1).to_broadcast([128, nh, 8])
                kb.op(DVE, lambda: dve.tensor_tensor(out=rp[2][:, 0:nh, :], in0=rp[0][:, 0:nh, :], in1=cosb, op=ALU.mult), r=[rp[0], cst], w=[rp[2]])
                kb.op(DVE, lambda: dve.tensor_tensor(out=rp[3][:, 0:nh, :], in0=rp[1][:, 0:nh, :], in1=sinb, op=ALU.mult), r=[rp[1], cst], w=[rp[3]])
                kb.op(DVE, lambda: dve.tensor_tensor(out=out_ap3[:, :, 0:8], in0=rp[2][:, 0:nh, :], in1=rp[3][:, 0:nh, :], op=ALU.subtract), r=[rp[2], rp[3]], w=[out_buf])
                kb.op(DVE, lambda: dve.tensor_tensor(out=rp[2][:, 0:nh, :], in0=rp[1][:, 0:nh, :], in1=cosb, op=ALU.mult), r=[rp[1], cst], w=[rp[2]])
                kb.op(DVE, lambda: dve.tensor_tensor(out=rp[3][:, 0:nh, :], in0=rp[0][:, 0:nh, :], in1=sinb, op=ALU.mult), r=[rp[0], cst], w=[rp[3]])
                kb.op(DVE, lambda: dve.tensor_tensor(out=out_ap3[:, :, 8:16], in0=rp[2][:, 0:nh, :], in1=rp[3][:, 0:nh, :], op=ALU.add), r=[rp[2], rp[3]], w=[out_buf])

            k3 = kv_sb[:, 0:128].rearrange("p (h d) -> p h d", h=2)
            kf3 = kf[:].rearrange("p (h d) -> p h d", h=2)
            qk_norm_rope(kv_sb, k3, 2, knw_b, kf, kf3, t4[:, 0:128].rearrange("p (h d) -> p h d", h=2), None)
            kb.op(POOL, lambda: pool.tensor_copy(out=k_bf[:], in_=kf[:]), r=[kf], w=[k_bf])
            Vc, Vp = Va[par], Va[1 - par]
            kb.op(POOL, lambda: pool.tensor_copy(out=Vc[:, :, 0:64], in_=kv_sb[:, 128:256].rearrange("p (h d) -> p h d", h=2)), r=[kv_sb], w=[Vc])
            kTc, kTp = kT[par], kT[1 - par]
            pt = PSTF()

            def fn_kt():
                ins = None
                for kvh in range(2):
                    ins = pe.transpose(out=pt[0:64, kvh * 128:(kvh + 1) * 128], in_=k_bf[:, kvh * 64:(kvh + 1) * 64], identity=ident[:])
                return ins
            kb.op(PE, fn_kt, r=[k_bf, ident], w=[pt])
            kb.op(ACT, lambda: act.copy(out=kTc[:].rearrange("p a b -> p (a b)"), in_=pt[0:64, 0:256]), r=[pt], w=[kTc])
            if i == NT - 1:
                kb.dma(o_kwin[:, :], kf[:], r=[kf], sembuf=kf)
                kb.dma(o_vwin[:, :], kv_sb[:, 128:256], r=[kv_sb], sembuf=kv_sb)
        yield
        if main:
            pq = PSF()
            proj(1696, 2208, pq)
            kb.op(ACT, lambda: act.copy(out=q_sb[:], in_=pq[:]), r=[pq], w=[q_sb])
            qk_norm_rope(q_sb, v3(q_sb), 8, qnw_b, t5, v3(t5), v3(t4), None)
            kb.op(POOL, lambda: pool.tensor_copy(out=qr_bf[:], in_=t5[:]), r=[t5], w=[qr_bf])
            pt = PSTF()

            def fn_qt():
                ins = None
                for h in range(8):
                    ins = pe.transpose(out=pt[0:64, h * 128:(h + 1) * 128], in_=qr_bf[:, h * 64:(h + 1) * 64], identity=ident[:])
                return ins
            kb.op(PE, fn_qt, r=[qr_bf, ident], w=[pt])
            kb.op(ACT, lambda: act.copy(out=qT[:].rearrange("p a b -> p (a b)"), in_=pt[0:64, :]), r=[pt], w=[qT])
            yield
            mprev = amp0 if mi == 0 else amp
            for kvh in range(2):
                Es = {}
                for which, (kTx, mk) in enumerate([(kTp, mprev), (kTc, amc)]):
                    pss = PSF()
                    kb.op(PE, lambda: pe.matmul(pss[:], lhsT=kTx[:, kvh, :], rhs=qT[:, 4 * kvh:4 * kvh + 4, :].rearrange("p a b -> p (a b)"),
                                                start=True, stop=True), r=[kTx, qT], w=[pss])
                    E = Eb[which]
                    kb.op(ACT, lambda: act.activation(out=E[:], in_=pss[:], func=AF.Exp, scale=0.125), r=[pss], w=[E])
                    kb.op(POOL, lambda: pool.tensor_tensor(out=E[:].rearrange("p (j t) -> p j t", j=4), in0=E[:].rearrange("p (j t) -> p j t", j=4),
                                                           in1=mk[:].unsqueeze(1).to_broadcast([128, 4, 128]), op=ALU.mult), r=[E, mk], w=[E])
                    Es[which] = E
                po = PSF()

                def fn_o():
                    ins = None
                    for j in range(4):
                        pe.matmul(po[:, j * 65:(j + 1) * 65], lhsT=Es[0][:, j * 128:(j + 1) * 128], rhs=Vp[:, kvh, :], start=True, stop=False)
                        ins = pe.matmul(po[:, j * 65:(j + 1) * 65], lhsT=Es[1][:, j * 128:(j + 1) * 128], rhs=Vc[:, kvh, :], start=False, stop=True)
                    return ins
                kb.op(PE, fn_o, r=[Es[0], Es[1], Vp, Vc], w=[po])
                den, rden = sm[10], sm[11]
                po3 = po[:, 0:260].rearrange("p (j d) -> p j d", j=4)
                kb.op(DVE, lambda: dve.tensor_tensor(out=den[:, 0:4], in0=po3[:, :, 64], in1=esink[:, 4 * kvh:4 * kvh + 4], op=ALU.add), r=[po, esink], w=[den])
                kb.op(DVE, lambda: dve.reciprocal(out=rden[:, 0:4], in_=den[:, 0:4]), r=[den], w=[rden])
                kb.op(DVE, lambda: dve.tensor_tensor(out=mix_tok[:, 512 + 256 * kvh:512 + 256 * (kvh + 1)].rearrange("p (j d) -> p j d", j=4),
                                                     in0=po3[:, :, 0:64], in1=bc(rden[:, 0:4], 4, 64), op=ALU.mult), r=[po, rden], w=[mix_tok])
                yield
        yield
        yield from rwkv_pointwise(main, g_sb, PSF)
        if main:
            bonus_pre(bon)
        yield
        eNeg, eEx, eSuf, eL = t1, t2, t3, t5
        psl1 = PSF()
        kb.op(PE, lambda: pe.matmul(psl1[:], lhsT=miu[:], rhs=s_sb[:], start=True, stop=True), r=[miu, s_sb], w=[psl1])
        kb.op(ACT, lambda: act.activation(out=eNeg[:], in_=psl1[:], func=AF.Exp, scale=-CDEC), r=[psl1], w=[eNeg])
        if main:
            kb.op(ACT, lambda: act.activation(out=eL[:], in_=psl1[:], func=AF.Exp, scale=CDEC), r=[psl1], w=[eL])
        psl2 = PSF()
        kb.op(PE, lambda: pe.matmul(psl2[:], lhsT=msu[:], rhs=s_sb[:], start=True, stop=True), r=[msu, s_sb], w=[psl2])
        kb.op(ACT, lambda: act.activation(out=eEx[:], in_=psl2[:], func=AF.Exp, scale=CDEC), r=[psl2], w=[eEx])
        psl3 = PSF()
        kb.op(PE, lambda: pe.matmul(psl3[:], lhsT=msl[:], rhs=s_sb[:], start=True, stop=True), r=[msl, s_sb], w=[psl3])
        kb.op(ACT, lambda: act.activation(out=eSuf[:], in_=psl3[:], func=AF.Exp, scale=CDEC), r=[psl3], w=[eSuf])
        psgc = PSF()

        def fn_gc():
            ins = None
            for h in range(8):
                ins = pe.matmul(psgc[0:64, 2 * h:2 * h + 2], lhsT=s_sb[:, h * 64:(h + 1) * 64], rhs=ones_f[:, 0:2], start=True, stop=True)
            return ins
        kb.op(PE, fn_gc, r=[s_sb, ones_f], w=[psgc])
        kb.op(ACT, lambda: act.activation(out=gC[:], in_=psgc[0:64, 0:16:2], func=AF.Exp, scale=CDEC), r=[psgc], w=[gC])
        Bt_tok, Kt_tok = tk[0], tk[1]
        kb.op(DVE, lambda: dve.tensor_tensor(out=Kt_tok[:], in0=kmod[:], in1=eNeg[:], op=ALU.mult), r=[kmod, eNeg], w=[Kt_tok])
        kb.op(POOL, lambda: pool.tensor_tensor(out=Bt_tok[:], in0=bv[:], in1=eNeg[:], op=ALU.mult), r=[bv, eNeg], w=[Bt_tok])
        kb.op(DVE, lambda: dve.scalar_tensor_tensor(out=A_tok[:], in0=kkn[:], scalar=-1.0, in1=eEx[:], op0=ALU.mult, op1=ALU.mult), r=[kkn, eEx], w=[A_tok])
        yield
        kb.op(POOL, lambda: pool.tensor_tensor(out=Kh[:], in0=kmod[:], in1=eSuf[:], op=ALU.mult), r=[kmod, eSuf], w=[Kh])
        kb.op(DVE, lambda: dve.tensor_tensor(out=Bh[:], in0=bv[:], in1=eSuf[:], op=ALU.mult), r=[bv, eSuf], w=[Bh])
        kb.op(ACT, lambda: act.copy(out=Vb[:], in_=v_sb[:]), r=[v_sb], w=[Vb])
        if main:
            kb.op(DVE, lambda: dve.tensor_tensor(out=Rt[:], in0=r_sb[:], in1=eL[:], op=ALU.mult), r=[r_sb, eL], w=[Rt])
        yield
        arrs = [(0, A_tok), (1, Bt_tok), (2, Kt_tok)] + ([(3, Rt)] if main else [])
        for half in range(2):
            pt = PSTF()

            def fn_fm(half=half, pt=pt):
                ins = None
                for hb in (2 * half, 2 * half + 1):
                    for (ai, src) in arrs:
                        off = ((hb % 2) * 4 + ai) * 128
                        ins = pe.transpose(out=pt[:, off:off + 128], in_=src[:, hb * 128:(hb + 1) * 128], identity=ident[:])
                return ins
            kb.op(PE, fn_fm, r=[s for (_, s) in arrs] + [ident], w=[pt])
            if main:
                kb.op(ACT, lambda half=half, pt=pt: act.copy(out=FM[:, 2 * half:2 * half + 2, :, :].rearrange("p a b c -> p (a b c)"), in_=pt[:]), r=[pt], w=[FM])
            else:
                for hh in range(2):
                    hb = 2 * half + hh
                    kb.op(ACT, lambda hb=hb, hh=hh, pt=pt: act.copy(out=FM[:, hb, 0:3, :].rearrange("p a b -> p (a b)"), in_=pt[:, (hh * 4) * 128:(hh * 4 + 3) * 128]), r=[pt], w=[FM])

        def slot(h):
            return 4 * (h % 2) + h // 2

        def fmh(h, ai):
            base = 64 * (h % 2)
            return FM[base:base + 64, h // 2, ai, :]

        yield 'F'
        def hs_of(hg):
            return [hg + 2 * j for j in range(4)]

        def pmat(hg, ai_l, ai_r, ps):
            def fn():
                ins = None
                for j, h in enumerate(hs_of(hg)):
                    ins = pe.matmul(ps[:, j * 128:(j + 1) * 128], lhsT=fmh(h, ai_l), rhs=fmh(h, ai_r), start=True, stop=True)
                return ins
            kb.op(PE, fn, r=[FM], w=[ps])

        def mask_to(ps, mk, dst_buf, dst_ap):
            kb.op(DVE, lambda: dve.tensor_tensor(out=dst_ap, in0=ps[:].rearrange("p (j t) -> p j t", j=4),
                                                 in1=mk[:].unsqueeze(1).to_broadcast([128, 4, 128]), op=ALU.mult), r=[ps, mk], w=[dst_buf])
        for hg in range(2):
            nA, nB = nAB[hg][0], nAB[hg][1]
            p1 = PSB()
            pmat(hg, 1, 0, p1)
            mask_to(p1, msu, nA[0], nA[0][:])
            kb.op(POOL, lambda: pool.tensor_tensor(out=Xh[hg][:], in0=nA[0][:], in1=ident[:].unsqueeze(1).to_broadcast([128, 4, 128]), op=ALU.add),
                  r=[nA[0], ident], w=[Xh[hg]])
            p2 = PSB()
            pmat(hg, 2, 0, p2)
            mask_to(p2, msu, P2b, P2b[:])
            psM = PSB()

            def fn_m0():
                ins = None
                for j, h in enumerate(hs_of(hg)):
                    ins = pe.matmul(psM[:, j * 64:(j + 1) * 64], lhsT=P2b[:, j, :], rhs=Vb[:, h * 64:(h + 1) * 64], start=True, stop=True)
                return ins
            kb.op(PE, fn_m0, r=[P2b, Vb], w=[psM])
            kb.op(ACT, lambda: act.copy(out=M0b[:].rearrange("p (j q v) -> p j q v", j=4, q=2)[:, :, hg, :], in_=psM[:, 0:256].rearrange("p (j v) -> p j v", j=4)),
                  r=[psM], w=[M0b])
            pn = PSB()
            pmat(hg, 0, 1, pn)
            mask_to(pn, msl, nB[0], nB[0][:])
            yield
            if main:
                p3 = PSB()
                pmat(hg, 1, 3, p3)
                mask_to(p3, miu, P3b, P3b[:, 4 * hg:4 * hg + 4, :])
                p4 = PSB()
                pmat(hg, 2, 3, p4)
                mask_to(p4, miu, P4b, P4b[:, 4 * hg:4 * hg + 4, :])
                yield
        cur = 0
        for step in range(7):
            for hg in range(2):
                nA_, nB_ = nAB[hg][0], nAB[hg][1]
                if step > 0:
                    NTn = nB_[cur]
                    pc = PSB()

                    def fn_c():
                        ins = None
                        for j in range(4):
                            ins = pe.matmul(pc[:, j * 128:(j + 1) * 128], lhsT=NTn[:, j, :], rhs=Xh[hg][:, j, :], start=True, stop=True)
                        return ins
                    kb.op(PE, fn_c, r=[NTn, Xh[hg]], w=[pc])
                    kb.op(DVE, lambda: dve.tensor_tensor(out=Xh[hg][:], in0=pc[:].rearrange("p (j t) -> p j t", j=4),
                                                         in1=Xh[hg][:], op=ALU.add), r=[pc, Xh[hg]], w=[Xh[hg]])
                if step < 6:
                    last = step == 5
                    Np, NTp = nA_[cur], nB_[cur]
                    Np2, NTp2 = nA_[1 - cur], nB_[1 - cur]
                    if not last:
                        pa = PSB()

                        def fn_a():
                            ins = None
                            for j in range(4):
                                ins = pe.matmul(pa[:, j * 128:(j + 1) * 128], lhsT=NTp[:, j, :], rhs=Np[:, j, :], start=True, stop=True)
                            return ins
                        kb.op(PE, fn_a, r=[Np, NTp], w=[pa])
                    pb_ = PSB()

                    def fn_b():
                        ins = None
                        for j in range(4):
                            ins = pe.matmul(pb_[:, j * 128:(j + 1) * 128], lhsT=Np[:, j, :], rhs=NTp[:, j, :], start=True, stop=True)
                        return ins
                    kb.op(PE, fn_b, r=[Np, NTp], w=[pb_])
                    if not last:
                        kb.op(ACT, lambda: act.copy(out=Np2[:].rearrange("p j t -> p (j t)"), in_=pa[:]), r=[pa], w=[Np2])
                    kb.op(ACT, lambda: act.copy(out=NTp2[:].rearrange("p j t -> p (j t)"), in_=pb_[:]), r=[pb_], w=[NTp2])
            cur = 1 - cur
            yield
        psW, psU = PSB(), PSB()

        def fn_w():
            ins = None
            for h in range(8):
                ins = pe.matmul(psW[:, h * 64:(h + 1) * 64], lhsT=Xh[h % 2][:, h // 2, :], rhs=A_tok[:, h * 64:(h + 1) * 64], start=True, stop=True)
            return ins

        def fn_u():
            ins = None
            for h in range(8):
                ins = pe.matmul(psU[:, h * 64:(h + 1) * 64], lhsT=Xh[h % 2][:, h // 2, :], rhs=M0b[:, h * 64:(h + 1) * 64], start=True, stop=True)
            return ins
        kb.op(PE, fn_w, r=[Xh[0], Xh[1], A_tok], w=[psW])
        kb.op(PE, fn_u, r=[Xh[0], Xh[1], M0b], w=[psU])
        kb.op(ACT, lambda: act.copy(out=Wb[:], in_=psW[:]), r=[psW], w=[Wb])
        kb.op(DVE, lambda: dve.tensor_copy(out=U0b[:], in_=psU[:]), r=[psU], w=[U0b])
        yield
        psPsi, psQ = PSB(), PSB()

        def fn_psi():
            ins = None
            for h in range(8):
                ins = pe.matmul(psPsi[0:64, h * 64:(h + 1) * 64], lhsT=Wb[:, h * 64:(h + 1) * 64], rhs=Bh[:, h * 64:(h + 1) * 64], start=True, stop=True)
            return ins

        def fn_q():
            ins = None
            for h in range(8):
                pe.matmul(psQ[0:64, h * 64:(h + 1) * 64], lhsT=Bh[:, h * 64:(h + 1) * 64], rhs=U0b[:, h * 64:(h + 1) * 64], start=True, stop=False)
                ins = pe.matmul(psQ[0:64, h * 64:(h + 1) * 64], lhsT=Kh[:, h * 64:(h + 1) * 64], rhs=Vb[:, h * 64:(h + 1) * 64], start=False, stop=True)
            return ins
        kb.op(PE, fn_psi, r=[Wb, Bh], w=[psPsi])
        kb.op(PE, fn_q, r=[Bh, U0b, Kh, Vb], w=[psQ])
        kb.op(DVE, lambda: dve.tensor_tensor(out=Gd[:], in0=ident_f[0:64, 0:64].unsqueeze(1).to_broadcast([64, 8, 64]),
                                               in1=gC[:].unsqueeze(2).to_broadcast([64, 8, 64]), op=ALU.mult), r=[ident_f, gC], w=[Gd])
        kb.op(DVE, lambda: dve.tensor_tensor(out=PsiT[:], in0=psPsi[0:64, :], in1=Gd[:].rearrange("p h k -> p (h k)"), op=ALU.add), r=[psPsi, Gd], w=[PsiT])
        kb.op(ACT, lambda: act.copy(out=Q_sb[:], in_=psQ[0:64, :]), r=[psQ], w=[Q_sb])
        Hc, Hn = Hb[par], Hb[1 - par]
        if main:
            for half in range(2):
                pz = PSB()

                def fn_z(half=half, pz=pz):
                    ins = None
                    for j in range(4):
                        h = 4 * half + j
                        pe.matmul(pz[0:64, j * 128:(j + 1) * 128], lhsT=Wb[:, h * 64:(h + 1) * 64], rhs=P3b[:, slot(h), :], start=True, stop=False)
                        ins = pe.matmul(pz[0:64, j * 128:(j + 1) * 128], lhsT=Rt[:, h * 64:(h + 1) * 64], rhs=ident[:], start=False, stop=True)
                    return ins
                kb.op(PE, fn_z, r=[Wb, P3b, Rt, ident], w=[pz])
                kb.op(ACT, lambda half=half, pz=pz: act.copy(out=ZTb[:, 4 * half:4 * half + 4, :].rearrange("p j t -> p (j t)"), in_=pz[0:64, :]), r=[pz], w=[ZTb])
            psY = PSB()

            def fn_y():
                ins = None
                for h in range(8):
                    sl = slice(h * 64, (h + 1) * 64)
                    pe.matmul(psY[:, sl], lhsT=P3b[:, slot(h), :], rhs=U0b[:, sl], start=True, stop=False)
                    pe.matmul(psY[:, sl], lhsT=P4b[:, slot(h), :], rhs=Vb[:, sl], start=False, stop=False)
                    ins = pe.matmul(psY[:, sl], lhsT=ZTb[:, h, :], rhs=Hc[:, sl], start=False, stop=True)
                return ins
            kb.op(PE, fn_y, r=[P3b, U0b, P4b, Vb, ZTb, Hc], w=[psY])
            kb.op(ACT, lambda: act.copy(out=y_sb[:], in_=psY[:]), r=[psY], w=[y_sb])
        yield
        psH = PSB()

        def fn_h():
            ins = None
            for h in range(8):
                sl = slice(h * 64, (h + 1) * 64)
                ins = pe.matmul(psH[0:64, sl], lhsT=PsiT[:, sl], rhs=Hc[:, sl], start=True, stop=True)
            return ins
        kb.op(PE, fn_h, r=[PsiT, Hc], w=[psH])
        kb.op(DVE, lambda: dve.tensor_tensor(out=H32[:], in0=psH[0:64, :], in1=Q_sb[:], op=ALU.add), r=[psH, Q_sb], w=[H32])
        kb.op(ACT, lambda: act.copy(out=Hn[:], in_=H32[:]), r=[H32], w=[Hn])

        if main:
            yield
            groupnorm_gate2(bon, g_sb, mix_tok)
            yield

        if main:
            pt = PSTB()

            def fn_mt():
                ins = None
                for kc in range(8):
                    ins = pe.transpose(out=pt[:, kc * 128:(kc + 1) * 128], in_=mix_tok[:, kc * 128:(kc + 1) * 128], identity=ident[:])
                return ins
            kb.op(PE, fn_mt, r=[mix_tok, ident], w=[pt])
            mtt = mixTt[par]
            kb.op(ACT, lambda: act.copy(out=mtt[:].rearrange("p a b -> p (a b)"), in_=pt[:]), r=[pt], w=[mtt])
            kb.dma(mix_scr[:, :, mi * 128:(mi + 1) * 128], mtt[:], r=[mtt], w=[mix_scr], sembuf=mtt)

        if i == NT - 1:
            pass


    def run_tiles():
        prev = None
        for i in range(NT):
            g = tile_gen(i)
            while True:
                tag = next(g)
                for _ in range(int(os.environ.get('RB', '3'))):
                    if prev is not None:
                        try:
                            next(prev)
                        except StopIteration:
                            prev = None
                if tag == 'F':
                    break
            while prev is not None:
                try:
                    next(prev)
                except StopIteration:
                    prev = None
            prev = g
        while prev is not None:
            try:
                next(prev)
            except StopIteration:
                prev = None
    run_tiles()

    ck(19)
    pz = PS()

    def fn_ht():
        ins = None
        for h in range(8):
            ins = pe.matmul(pz[0:64, h * 64:(h + 1) * 64], lhsT=H32[:, h * 64:(h + 1) * 64], rhs=ident_f[0:64, 0:64], start=True, stop=True)
        return ins
    kb.op(PE, fn_ht, r=[H32, ident_f], w=[pz])
    kb.op(ACT, lambda: act.copy(out=Hfin[:].rearrange("p a b -> p (a b)"), in_=pz[0:64, :]), r=[pz], w=[Hfin])
    kb.dma(o_wkv.t.rearrange("h v k -> v h k"), Hfin[:], r=[Hfin], sembuf=Hfin)

    kb.barrier()
    esA2.close()
    stacks.remove(esA2)
    esS = ExitStack()
    stacks.append(esS)

    def ss_(name, shape, dt=F32):
        return kb.sb(name, shape, dt, stack=esS)
    def proj_s(c0, c1, ps, hTc):
        def fn():
            ins = None
            for kc in range(8):
                ins = pe.matmul(ps[:, 0:c1 - c0], lhsT=hTc[:, kc, :], rhs=win[:, kc, c0:c1], start=(kc == 0), stop=(kc == 7))
            return ins
        kb.op(PE, fn, r=[hTc, win], w=[ps])


    def qk_norm_rope_s(src_buf, src_ap3, nh, wb, out_buf, out_ap3, tmpA, cst):
        ssq, rq = sm[10], sm[11]
        kb.op(POOL, lambda: pool.tensor_tensor(out=tmpA, in0=src_ap3, in1=src_ap3, op=ALU.mult), r=[src_buf], w=[t4])
        kb.op(DVE, lambda: dve.tensor_reduce(out=ssq[:, 0:nh], in_=tmpA, axis=AX.X, op=ALU.add), r=[t4], w=[ssq])
        rsq(ssq, ssq[:, 0:nh], rq, rq[:, 0:nh], 1.0 / 64, "rms")
        kb.op(DVE, lambda: dve.tensor_tensor(out=tmpA, in0=src_ap3, in1=bc(rq[:, 0:nh], nh, 64), op=ALU.mult), r=[src_buf, rq], w=[t4])
        kb.op(POOL, lambda: pool.tensor_tensor(out=out_ap3, in0=tmpA, in1=bch(wb[:], nh, 64), op=ALU.mult), r=[t4, wb], w=[out_buf])
        kb.op(POOL, lambda: pool.tensor_copy(out=rp[0][:, 0:nh, :], in_=out_ap3[:, :, 0:8]), r=[out_buf], w=[rp[0]])
        kb.op(POOL, lambda: pool.tensor_copy(out=rp[1][:, 0:nh, :], in_=out_ap3[:, :, 8:16]), r=[out_buf], w=[rp[1]])
        cosb = cst[:, 0, :].unsqueeze(1).to_broadcast([128, nh, 8])
        sinb = cst[:, 1, :].unsqueeze(1).to_broadcast([128, nh, 8])
        kb.op(DVE, lambda: dve.tensor_tensor(out=rp[2][:, 0:nh, :], in0=rp[0][:, 0:nh, :], in1=cosb, op=ALU.mult), r=[rp[0], cst], w=[rp[2]])
        kb.op(DVE, lambda: dve.tensor_tensor(out=rp[3][:, 0:nh, :], in0=rp[1][:, 0:nh, :], in1=sinb, op=ALU.mult), r=[rp[1], cst], w=[rp[3]])
        kb.op(DVE, lambda: dve.tensor_tensor(out=out_ap3[:, :, 0:8], in0=rp[2][:, 0:nh, :], in1=rp[3][:, 0:nh, :], op=ALU.subtract), r=[rp[2], rp[3]], w=[out_buf])
        kb.op(DVE, lambda: dve.tensor_tensor(out=rp[2][:, 0:nh, :], in0=rp[1][:, 0:nh, :], in1=cosb, op=ALU.mult), r=[rp[1], cst], w=[rp[2]])
        kb.op(DVE, lambda: dve.tensor_tensor(out=rp[3][:, 0:nh, :], in0=rp[0][:, 0:nh, :], in1=sinb, op=ALU.mult), r=[rp[0], cst], w=[rp[3]])
        kb.op(DVE, lambda: dve.tensor_tensor(out=out_ap3[:, :, 8:16], in0=rp[2][:, 0:nh, :], in1=rp[3][:, 0:nh, :], op=ALU.add), r=[rp[2], rp[3]], w=[out_buf])


    xs_pad = din("xs_pad", [128, D])
    st_wkv = din("st_wkv", [128, 4096])
    st_shift = din("st_shift", [128, DS])
    c_k = din("cache_k", [16, 128, 128])
    c_v = din("cache_v", [16, 128, 128])
    c_cos_s = din("c_cos_s", [128, 8])
    c_sin_s = din("c_sin_s", [128, 8])
    s_wkv = dout("s_wkv", [128, 4096])
    s_shift = dout("s_shift", [16, DS])
    s_kwin = dout("s_kwin", [16, 128, 128])
    s_vwin = dout("s_vwin", [16, 128, 128])
    vscr = dout("vscr", [6, 16, 512])
    yscr = dout("yscr", [2, 16, 512])

    praw = ss_("praw", [128, DS])
    spv = ss_("spv", [128, DS])
    mul_b = ss_("mul_b", [128, 160])
    vecp = ss_("vecp", [128, 6, 64])
    y_p = ss_("y_p", [128, 64])
    sa_p = ss_("sa_p", [128, 8])
    Kst = ss_("Kst", [128, 8, 128])
    Kw_bf = ss_("Kw_bf", [128, 16, 128], BF16)
    Vaug = ss_("Vaug", [128, 16, 2, 65], BF16)
    KT_bf = ss_("KT_bf", [64, 32, 128], BF16)
    E_bf = ss_("E_bf", [128, 128], BF16)
    OT_sb = ss_("OT_sb", [65, 128])
    esk = ss_("esk", [128, 1])
    den_p = ss_("den_p", [128, 1])
    ya_p = ss_("ya_p", [128, 64])

    kb.dma(spv[:], st_shift[:, :], w=[spv], sembuf=spv)
    kb.dma(mul_b[:], vec["mu_shift"][0:1, 1536:1696].partition_broadcast(128), w=[mul_b], sembuf=mul_b)
    for t_ in range(16):
        kb.dma(esk[8 * t_:8 * t_ + 8, 0:1], vec["sinks"][0:1, 0:8].rearrange("o h -> h o"), w=[esk], sembuf=esk, allow_slow_non_contiguous=True)
    kb.op(ACT, lambda: act.activation(out=esk[:], in_=esk[:], func=AF.Exp), r=[esk], w=[esk])
    kb.op(POOL, lambda: pool.memset(Vaug[:], 1.0), w=[Vaug])

    for (dst_w, src_c) in [(s_kwin, c_k), (s_vwin, c_v)]:
        kb.dma(dst_w[:, 0:127, :], src_c[:, 1:128, :], w=[dst_w], sembuf=dst_w)
    xt = xb[0]
    hTc = hT[0]
    kb.dma(xt[:], xs_pad[:, :], w=[xt], sembuf=xt)
    ss, rstd = sm[0], sm[1]
    kb.op(ACT, lambda: act.activation(out=h_bf[:], in_=xt[:], func=AF.Square, accum_out=ss[:, 0:1]), r=[xt], w=[h_bf, ss])
    rsq(ss, ss[:, 0:1], rstd, rstd[:, 0:1], 1.0 / D, "rms")
    kb.op(DVE, lambda: dve.scalar_tensor_tensor(out=h_bf[:], in0=xt[:], scalar=rstd[:, 0:1], in1=nmw_b[:], op0=ALU.mult, op1=ALU.mult),
          r=[xt, rstd, nmw_b], w=[h_bf])
    pt = PST()

    def fn_trs():
        ins = None
        for kc in range(8):
            ins = pe.transpose(out=pt[:, kc * 128:(kc + 1) * 128], in_=h_bf[:, kc * 128:(kc + 1) * 128], identity=ident[:])
        return ins
    kb.op(PE, fn_trs, r=[h_bf, ident], w=[pt])
    kb.op(ACT, lambda: act.copy(out=hTc[:].rearrange("p a b -> p (a b)"), in_=pt[:]), r=[pt], w=[hTc])

    for (c0, dst) in [(0, r_sb), (512, k_sb), (1024, v_sb)]:
        ps = PS()
        proj_s(c0, c0 + 512, ps, hTc)
        kb.op(ACT, lambda: act.copy(out=praw[:, c0:c0 + 512], in_=ps[:]), r=[ps], w=[praw])
        kb.op(DVE, lambda: dve.tensor_tensor(out=dst[:], in0=spv[:, c0:c0 + 512], in1=praw[:, c0:c0 + 512], op=ALU.subtract), r=[spv, praw], w=[dst])
        kb.op(POOL, lambda: pool.tensor_tensor(out=dst[:], in0=dst[:], in1=mu_b[:, c0:c0 + 512], op=ALU.mult), r=[dst, mu_b], w=[dst])
        kb.op(DVE, lambda: dve.tensor_tensor(out=dst[:], in0=dst[:], in1=praw[:, c0:c0 + 512], op=ALU.add), r=[dst, praw], w=[dst])
    psx = PS()
    proj_s(1536, 1696, psx, hTc)
    kb.op(ACT, lambda: act.copy(out=praw[:, 1536:1696], in_=psx[:, 0:160]), r=[psx], w=[praw])
    xl = t1[:, 0:160]
    kb.op(DVE, lambda: dve.tensor_tensor(out=xl, in0=spv[:, 1536:1696], in1=praw[:, 1536:1696], op=ALU.subtract), r=[spv, praw], w=[t1])
    kb.op(POOL, lambda: pool.tensor_tensor(out=xl, in0=xl, in1=mul_b[:], op=ALU.mult), r=[t1, mul_b], w=[t1])
    kb.op(DVE, lambda: dve.tensor_tensor(out=xl, in0=xl, in1=praw[:, 1536:1696], op=ALU.add), r=[t1, praw], w=[t1])
    kb.dma(s_shift[:, :], praw[0:16, :], r=[praw], sembuf=praw)
    kb.op(ACT, lambda: act.activation(out=t2[:, 0:32], in_=t1[:, 0:32], func=AF.Exp, scale=2.0), r=[t1], w=[t2])
    kb.op(DVE, lambda: dve.tensor_scalar_add(out=t2[:, 0:32], in0=t2[:, 0:32], scalar1=1.0), r=[t2], w=[t2])
    kb.op(DVE, lambda: dve.reciprocal(out=t2[:, 0:32], in_=t2[:, 0:32]), r=[t2], w=[t2])
    kb.op(DVE, lambda: dve.tensor_scalar(out=qr_bf[:, 0:32], in0=t2[:, 0:32], scalar1=-2.0, scalar2=1.0, op0=ALU.mult, op1=ALU.add), r=[t2], w=[qr_bf])
    kb.op(POOL, lambda: pool.tensor_copy(out=qr_bf[:, 32:64], in_=t1[:, 32:64]), r=[t1], w=[qr_bf])
    kb.op(ACT, lambda: act.activation(out=t2[:, 64:160], in_=t1[:, 64:160], func=AF.Exp, scale=-1.0), r=[t1], w=[t2])
    kb.op(DVE, lambda: dve.tensor_scalar_add(out=t2[:, 64:160], in0=t2[:, 64:160], scalar1=1.0), r=[t2], w=[t2])
    kb.op(DVE, lambda: dve.reciprocal(out=t2[:, 64:160], in_=t2[:, 64:160]), r=[t2], w=[t2])
    kb.op(POOL, lambda: pool.tensor_copy(out=qr_bf[:, 64:160], in_=t2[:, 64:160]), r=[t2], w=[qr_bf])
    pt = PST()

    def fn_trl():
        pe.transpose(out=pt[0:32, 0:128], in_=qr_bf[:, 0:32], identity=ident[:])
        pe.transpose(out=pt[0:32, 128:256], in_=qr_bf[:, 32:64], identity=ident[:])
        return pe.transpose(out=pt[0:96, 256:384], in_=qr_bf[:, 64:160], identity=ident[:])
    kb.op(PE, fn_trl, r=[qr_bf, ident], w=[pt])
    kb.op(ACT, lambda: act.copy(out=th_bf[:], in_=pt[0:32, 0:128]), r=[pt], w=[th_bf])
    kb.op(ACT, lambda: act.copy(out=al_bf[:], in_=pt[0:32, 128:256]), r=[pt], w=[al_bf])
    kb.op(ACT, lambda: act.copy(out=sg_bf[:], in_=pt[0:96, 256:384]), r=[pt], w=[sg_bf])
    for _ in rwkv_pointwise(True, g_sb, PS):
        pass
    bonus_pre(bon2[0])
    kb.op(ACT, lambda: act.activation(out=t1[:], in_=s_sb[:], func=AF.Exp, scale=CDEC), r=[s_sb], w=[t1])
    kb.op(DVE, lambda: dve.tensor_scalar_mul(out=t2[:], in0=kkn[:], scalar1=-1.0), r=[kkn], w=[t2])
    for a_, src in enumerate([t1, t2, bv, v_sb, kmod, r_sb]):
        kb.dma(vscr[a_, :, :], src[0:16, :], r=[src], w=[vscr], sembuf=src)
    kb.dma(vecp[:], vscr.t.rearrange("a t (h d) -> (t h) a d", h=8), r=[vscr], w=[vecp], sembuf=vecp)
    Sc, Tm = t3, t4
    w_b = vecp[:, 0, :].unsqueeze(1).to_broadcast([128, 8, 64])
    a_b = vecp[:, 1, :].unsqueeze(1).to_broadcast([128, 8, 64])
    b_b = vecp[:, 2, :].unsqueeze(1).to_broadcast([128, 8, 64])
    k_b = vecp[:, 4, :].unsqueeze(1).to_broadcast([128, 8, 64])
    r_b = vecp[:, 5, :].unsqueeze(1).to_broadcast([128, 8, 64])
    chunk_bufs = [(t3, t4), (t1, t2)]
    kb.dma(t3[:], st_wkv[:, 0:512], w=[t3], sembuf=t3)
    for c_ in range(8):
        Sc, Tm = chunk_bufs[c_ % 2]
        if c_ + 1 < 8:
            nxt = chunk_bufs[(c_ + 1) % 2][0]
            kb.dma(nxt[:], st_wkv[:, (c_ + 1) * 512:(c_ + 2) * 512], w=[nxt], sembuf=nxt)
        kb.op(DVE, lambda: dve.tensor_tensor(out=v3(Tm), in0=v3(Sc), in1=a_b, op=ALU.mult), r=[Sc, vecp], w=[Tm])
        kb.op(DVE, lambda: dve.tensor_reduce(out=sa_p[:], in_=v3(Tm), axis=AX.X, op=ALU.add), r=[Tm], w=[sa_p])
        kb.op(POOL, lambda: pool.tensor_tensor(out=v3(Sc), in0=v3(Sc), in1=w_b, op=ALU.mult), r=[Sc, vecp], w=[Sc])
        kb.op(DVE, lambda: dve.tensor_tensor(out=v3(Tm), in0=bc(sa_p[:], 8, 64), in1=b_b, op=ALU.mult), r=[sa_p, vecp], w=[Tm])
        kb.op(POOL, lambda: pool.tensor_tensor(out=Sc[:], in0=Sc[:], in1=Tm[:], op=ALU.add), r=[Sc, Tm], w=[Sc])
        kb.op(DVE, lambda: dve.tensor_tensor(out=v3(Tm), in0=bc(vecp[:, 3, c_ * 8:(c_ + 1) * 8], 8, 64), in1=k_b, op=ALU.mult), r=[vecp], w=[Tm])
        kb.op(POOL, lambda: pool.tensor_tensor(out=Sc[:], in0=Sc[:], in1=Tm[:], op=ALU.add), r=[Sc, Tm], w=[Sc])
        kb.dma(s_wkv[:, c_ * 512:(c_ + 1) * 512], Sc[:], r=[Sc], sembuf=Sc)
        kb.op(DVE, lambda: dve.tensor_tensor(out=v3(Tm), in0=v3(Sc), in1=r_b, op=ALU.mult), r=[Sc, vecp], w=[Tm])
        kb.op(DVE, lambda: dve.tensor_reduce(out=y_p[:, c_ * 8:(c_ + 1) * 8], in_=v3(Tm), axis=AX.X, op=ALU.add), r=[Tm], w=[y_p])
    kb.dma(yscr.t[0].rearrange("t (h d) -> (t h) d", h=8), y_p[:], r=[y_p], w=[yscr], sembuf=y_p)
    kb.op(POOL, lambda: pool.memset(y_sb[:], 0.0), w=[y_sb])
    kb.dma(y_sb[0:16, :], yscr[0, :, :], r=[yscr], w=[y_sb], sembuf=y_sb)
    groupnorm_gate2(bon2[0], g_sb, mix_tok)
    cst = cs_t[0]
    kb.dma(cst[:, 0, :], c_cos_s[:, :], w=[cst], sembuf=cst)
    kb.dma(cst[:, 1, :], c_sin_s[:, :], w=[cst], sembuf=cst)
    pkv = PS()
    proj_s(2208, 2464, pkv, hTc)
    kb.op(ACT, lambda: act.copy(out=kv_sb[:], in_=pkv[:, 0:256]), r=[pkv], w=[kv_sb])
    k3 = kv_sb[:, 0:128].rearrange("p (h d) -> p h d", h=2)
    kf3 = kf[:].rearrange("p (h d) -> p h d", h=2)
    qk_norm_rope_s(kv_sb, k3, 2, knw_b, kf, kf3, t4[:, 0:128].rearrange("p (h d) -> p h d", h=2), cst)
    pq = PS()
    proj_s(1696, 2208, pq, hTc)
    kb.op(ACT, lambda: act.copy(out=q_sb[:], in_=pq[:]), r=[pq], w=[q_sb])
    qk_norm_rope_s(q_sb, v3(q_sb), 8, qnw_b, t5, v3(t5), v3(t4), cst)
    kb.op(POOL, lambda: pool.tensor_copy(out=qr_bf[:], in_=t5[:]), r=[t5], w=[qr_bf])
    pt = PST()

    def fn_qts():
        ins = None
        for h in range(8):
            ins = pe.transpose(out=pt[0:64, h * 128:(h + 1) * 128], in_=qr_bf[:, h * 64:(h + 1) * 64], identity=ident[:])
        return ins
    kb.op(PE, fn_qts, r=[qr_bf, ident], w=[pt])
    kb.op(ACT, lambda: act.copy(out=qT[:].rearrange("p a b -> p (a b)"), in_=pt[0:64, :]), r=[pt], w=[qT])
    for (dst_w, src_c, new_buf, new_ap) in [(s_kwin, c_k, kf, kf[0:16, :]), (s_vwin, c_v, kv_sb, kv_sb[0:16, 128:256])]:
        kb.dma(dst_w[:, 127, :], new_ap, r=[new_buf], w=[dst_w], sembuf=new_buf)
    for hf_ in range(2):
        kb.dma(Kst[:], s_kwin.t[8 * hf_:8 * hf_ + 8].rearrange("t k c -> k t c"), r=[s_kwin], w=[Kst], sembuf=Kst)
        kb.op(DVE, lambda: dve.tensor_copy(out=Kw_bf[:, 8 * hf_:8 * hf_ + 8, :], in_=Kst[:]), r=[Kst], w=[Kw_bf])
    for hf_ in range(2):
        kb.dma(Kst[:], s_vwin.t[8 * hf_:8 * hf_ + 8].rearrange("t k c -> k t c"), r=[s_vwin], w=[Kst], sembuf=Kst)
        kb.op(DVE, lambda: dve.tensor_copy(out=Vaug[:, 8 * hf_:8 * hf_ + 8, :, 0:64], in_=Kst[:].rearrange("p t (h d) -> p t h d", h=2)), r=[Kst], w=[Vaug])
    for g4 in range(4):
        pt = PST()

        def fn_kts():
            ins = None
            for j in range(8):
                t_, kvh = divmod(8 * g4 + j, 2)
                ins = pe.transpose(out=pt[0:64, j * 128:(j + 1) * 128], in_=Kw_bf[:, t_, kvh * 64:(kvh + 1) * 64], identity=ident[:])
            return ins
        kb.op(PE, fn_kts, r=[Kw_bf, ident], w=[pt])
        kb.op(ACT, lambda: act.copy(out=KT_bf[:, 8 * g4:8 * g4 + 8, :].rearrange("p a b -> p (a b)"), in_=pt[0:64, :]), r=[pt], w=[KT_bf])
    psS = PS()

    def fn_sc():
        ins = None
        for t_ in range(16):
            for kvh in range(2):
                ins = pe.matmul(psS[:, t_ * 8 + 4 * kvh:t_ * 8 + 4 * kvh + 4], lhsT=KT_bf[:, 2 * t_ + kvh, :], rhs=qT[:, 4 * kvh:4 * kvh + 4, t_], start=True, stop=True)
        return ins
    kb.op(PE, fn_sc, r=[KT_bf, qT], w=[psS])
    kb.op(ACT, lambda: act.activation(out=E_bf[:], in_=psS[:, 0:128], func=AF.Exp, scale=0.125), r=[psS], w=[E_bf])
    psO = PS()

    def fn_os():
        ins = None
        for t_ in range(16):
            for kvh in range(2):
                c0_ = t_ * 8 + 4 * kvh
                ins = pe.matmul(psO[0:65, c0_:c0_ + 4], lhsT=Vaug[:, t_, kvh, :], rhs=E_bf[:, c0_:c0_ + 4], start=True, stop=True)
        return ins
    kb.op(PE, fn_os, r=[Vaug, E_bf], w=[psO])
    kb.op(ACT, lambda: act.copy(out=OT_sb[:], in_=psO[0:65, 0:128]), r=[psO], w=[OT_sb])
    psO2 = PS()
    kb.op(PE, lambda: pe.matmul(psO2[:, 0:65], lhsT=OT_sb[:], rhs=ident_f[0:65, 0:65], start=True, stop=True), r=[OT_sb, ident_f], w=[psO2])
    kb.op(DVE, lambda: dve.tensor_tensor(out=den_p[:], in0=psO2[:, 64:65], in1=esk[:], op=ALU.add), r=[psO2, esk], w=[den_p])
    kb.op(DVE, lambda: dve.reciprocal(out=den_p[:], in_=den_p[:]), r=[den_p], w=[den_p])
    kb.op(DVE, lambda: dve.tensor_scalar_mul(out=ya_p[:], in0=psO2[:, 0:64], scalar1=den_p[:, 0:1]), r=[psO2, den_p], w=[ya_p])
    kb.dma(yscr.t[1].rearrange("t (h d) -> (t h) d", h=8), ya_p[:], r=[ya_p], w=[yscr], sembuf=ya_p)
    kb.op(POOL, lambda: pool.memset(t3[:], 0.0), w=[t3])
    kb.dma(t3[0:16, :], yscr[1, :, :], r=[yscr], w=[t3], sembuf=t3)
    kb.op(POOL, lambda: pool.tensor_copy(out=mix_tok[:, 512:1024], in_=t3[:]), r=[t3], w=[mix_tok])
    pt = PST()

    def fn_mts():
        ins = None
        for kc in range(8):
            ins = pe.transpose(out=pt[:, kc * 128:(kc + 1) * 128], in_=mix_tok[:, kc * 128:(kc + 1) * 128], identity=ident[:])
        return ins
    kb.op(PE, fn_mts, r=[mix_tok, ident], w=[pt])
    mtt = mixTt[0]
    kb.op(ACT, lambda: act.copy(out=mtt[:].rearrange("p a b -> p (a b)"), in_=pt[:]), r=[pt], w=[mtt])
    kb.dma(mix_scr[:, :, SEG:SEG + 128], mtt[:], r=[mtt], w=[mix_scr], sembuf=mtt)
    ck(50)
    kb.barrier()
    esS.close()
    stacks.remove(esS)
    esA.close()
    stacks.remove(esA)
    esC = ExitStack()
    stacks.append(esC)

    def sc(name, shape, dt=F32):
        return kb.sb(name, shape, dt, stack=esC)

    TS = 256
    NSUP = SEG // TS
    wout = sc("wout", [128, 8, D], BF16)
    wup = sc("wup", [128, 8, 4 * D], BF16)
    wdn = sc("wdn", [128, 32, D], BF16)
    stgc = [sc(f"stgc{i}", [128, D]) for i in range(2)]
    xr = [sc(f"xr{i}", [128, D]) for i in range(2)]
    x1s = sc("x1s", [128, TS // 128, D])
    mixs = sc("mixs", [128, 8, TS], BF16)
    hfb = sc("hfb", [128, D], BF16)
    hfT = sc("hfT", [128, 8, TS], BF16)
    actb = sc("actb", [128, 32, TS], BF16)
    rl = [sc(f"rl{i}", [128, TS]) for i in range(2)]
    smc = [sc(f"smc{i}", [128, 1]) for i in range(2)]
    wparts = {"wout": [], "wup": [], "wdn": []}
    stage_bufs = stgc + xr
    cast_engs = [(DVE, dve), (POOL, pool), (DVE, dve), (ACT, act)]
    nld = [0]

    def load_piece(dst, src_v, kc, c0):
        st = stage_bufs[nld[0] % 4]
        eng_, e_ = cast_engs[nld[0] % 4]
        nld[0] += 1
        part = Buf(dst.t, f"{dst.name}_{kc}_{c0}")
        wparts[dst.name].append(part)
        kb.dma(st[:], src_v[kc][:, c0:c0 + D], w=[st], sembuf=st)
        if eng_ is ACT:
            kb.op(ACT, lambda: act.copy(out=dst[:, kc, c0:c0 + D], in_=st[:]), r=[st], w=[part])
        else:
            kb.op(eng_, lambda: e_.tensor_copy(out=dst[:, kc, c0:c0 + D], in_=st[:]), r=[st], w=[part])
    wo_v = w_out.t.rearrange("(kc p) n -> kc p n", p=128)
    wu_v = w_up.t.rearrange("(kc p) n -> kc p n", p=128)
    wd_v = w_dn.t.rearrange("(kc p) n -> kc p n", p=128)
    for kc in range(8):
        load_piece(wout, wo_v, kc, 0)
    for kc in range(8):
        for c0 in range(0, 4 * D, D):
            load_piece(wup, wu_v, kc, c0)
    wdn_todo = list(range(32))
    o_ys = dout("ys", [16, D])
    jobs = [(su * TS, TS // 128, None) for su in range(NSUP)] + [(SEG, 1, "s")]
    for (col0, ntl, kind) in jobs:
        ncols = ntl * 128
        kb.dma(mixs[:, :, 0:ncols], mix_scr[:, :, col0:col0 + ncols], r=[mix_scr], w=[mixs], sembuf=mixs)
        for tt in range(ntl):
            gt = col0 // 128 + tt
            xrb = xr[gt % 2]
            if kind is None:
                kb.dma(xrb[:], xext[(MT0 + gt) * 128:(MT0 + gt + 1) * 128, :], w=[xrb], sembuf=xrb)
            else:
                kb.dma(xrb[:], xs_pad[:, :], w=[xrb], sembuf=xrb)
            for half in range(2):
                po = PS()

                def fn_wo():
                    ins = None
                    for kc in range(8):
                        ins = pe.matmul(po[:], lhsT=mixs[:, kc, tt * 128:(tt + 1) * 128], rhs=wout[:, kc, half * 512:(half + 1) * 512], start=(kc == 0), stop=(kc == 7))
                    return ins
                kb.op(PE, fn_wo, r=[mixs] + wparts['wout'], w=[po])
                kb.op(DVE, lambda: dve.tensor_tensor(out=x1s[:, tt, half * 512:(half + 1) * 512], in0=po[:], in1=xrb[:, half * 512:(half + 1) * 512], op=ALU.add),
                      r=[po, xrb], w=[x1s])
            ss, rstd = smc[0], smc[1]
            kb.op(ACT, lambda: act.activation(out=hfb[:], in_=x1s[:, tt, :], func=AF.Square, accum_out=ss[:, 0:1]), r=[x1s], w=[hfb, ss])
            rsq(ss, ss[:, 0:1], rstd, rstd[:, 0:1], 1.0 / D, "rms")
            kb.op(DVE, lambda: dve.scalar_tensor_tensor(out=hfb[:], in0=x1s[:, tt, :], scalar=rstd[:, 0:1], in1=nfw_b[:], op0=ALU.mult, op1=ALU.mult),
                  r=[x1s, rstd, nfw_b], w=[hfb])
            pt = PST()

            def fn_trc():
                ins = None
                for kc in range(8):
                    ins = pe.transpose(out=pt[:, kc * 128:(kc + 1) * 128], in_=hfb[:, kc * 128:(kc + 1) * 128], identity=ident[:])
                return ins
            kb.op(PE, fn_trc, r=[hfb, ident], w=[pt])
            kb.op(ACT, lambda: act.copy(out=hfT[:, :, tt * 128:(tt + 1) * 128], in_=pt[:].rearrange("p (a b) -> p a b", a=8)), r=[pt], w=[hfT])
        for fch in range(32):
            pu = PS()

            def fn_up():
                ins = None
                for kc in range(8):
                    ins = pe.matmul(pu[:, 0:ncols], lhsT=wup[:, kc, fch * 128:(fch + 1) * 128], rhs=hfT[:, kc, 0:ncols], start=(kc == 0), stop=(kc == 7))
                return ins
            kb.op(PE, fn_up, r=[hfT] + wparts['wup'], w=[pu])
            rlb = rl[fch % 2]
            kb.op(ACT, lambda: act.activation(out=rlb[:, 0:ncols], in_=pu[:, 0:ncols], func=AF.Relu), r=[pu], w=[rlb])
            kb.op(POOL, lambda: pool.tensor_tensor(out=actb[:, fch, 0:ncols], in0=rlb[:, 0:ncols], in1=rlb[:, 0:ncols], op=ALU.mult), r=[rlb], w=[actb])
            if wdn_todo:
                load_piece(wdn, wd_v, wdn_todo.pop(0), 0)
        for tt in range(ntl):
            gt = col0 // 128 + tt
            yo = stgc[gt % 2]
            for half in range(2):
                pd = PS()

                def fn_dn():
                    ins = None
                    for fch in range(32):
                        ins = pe.matmul(pd[:], lhsT=actb[:, fch, tt * 128:(tt + 1) * 128], rhs=wdn[:, fch, half * 512:(half + 1) * 512], start=(fch == 0), stop=(fch == 31))
                    return ins
                kb.op(PE, fn_dn, r=[actb] + wparts['wdn'], w=[pd])
                kb.op(DVE, lambda: dve.tensor_tensor(out=yo[:, half * 512:(half + 1) * 512], in0=pd[:], in1=x1s[:, tt, half * 512:(half + 1) * 512], op=ALU.add),
                      r=[pd, x1s], w=[yo])
            if kind is None:
                kb.dma(y_main[gt * 128:(gt + 1) * 128, :], yo[:], r=[yo], sembuf=yo)
            else:
                kb.dma(o_ys[:, :], yo[0:16, :], r=[yo], sembuf=yo)

    kb.finish()
    esC.close()
    kb.es.close()
    return nc, kb


def _consts():
    s = np.arange(128)[:, None]
    t = np.arange(128)[None, :]
    c = {}
    c["c_ident"] = (s == t).astype(np.float32)
    c["c_sha"] = ((t == s + 1).astype(np.float32) - (t == s).astype(np.float32))
    c["c_shb"] = ((s == 127) & (t == 0)).astype(np.float32)
    c["c_msu"] = (s < t).astype(np.float32)
    c["c_miu"] = (s <= t).astype(np.float32)
    c["c_msl"] = (s > t).astype(np.float32)
    return c


def _rope_tables(pos):
    half = 8
    inv = np.power(np.float32(500000.0), -np.arange(half, dtype=np.float32) * np.float32(2.0 / 16))
    ang = pos.astype(np.float32)[:, None] * inv[None, :]
    return np.cos(ang).astype(np.float32), np.sin(ang).astype(np.float32)


def _sample_maps(c, xsm, swkv, ssh, ckw, cvw, cos_s, sin_s):
    sl = slice(16 * c, 16 * (c + 1))
    xs_pad = np.zeros((128, D), np.float32)
    xs_pad[:16] = xsm[sl]
    sp = np.zeros((128, DS), np.float32)
    sp[:16] = ssh[sl]
    return {"xs_pad": xs_pad, "st_wkv": np.ascontiguousarray(swkv[sl]).reshape(128, 4096), "st_shift": sp,
            "cache_k": np.ascontiguousarray(ckw[sl]), "cache_v": np.ascontiguousarray(cvw[sl]),
            "c_cos_s": cos_s, "c_sin_s": sin_s}


_CACHE = {}


def kernel(**inp):
    f32 = lambda a: np.ascontiguousarray(np.asarray(a, dtype=np.float32))
    if "nc" not in _CACHE:
        _CACHE["nc"] = build()
    nc, kb = _CACHE["nc"]
    xp = f32(inp["x_prompt"])
    consts = _consts()
    shared = {
        "w_in": f32(inp["w_in"][0]), "w_out": f32(inp["w_out"][0]), "w_up": f32(inp["w_ffn_up"][0]), "w_dn": f32(inp["w_ffn_down"][0]),
        "w_decay_up": f32(inp["w_decay_up"][0]), "w_a_up": f32(inp["w_a_up"][0]), "w_g_up": f32(inp["w_g_up"][0]),
    }
    for nm in ["norm_mix_w", "mu_shift", "w0", "a0", "k_k", "k_a", "ln_x_w", "ln_x_b", "q_norm_w", "k_norm_w", "sinks", "norm_ffn_w"]:
        shared[nm] = f32(inp[nm]).reshape(1, -1)
    shared["r_k"] = f32(inp["r_k"]).reshape(1, -1)
    shared.update(consts)
    in_maps = []
    xsm = f32(inp["x_sample"]).reshape(128, D)
    swkv = f32(inp["state_wkv"]).reshape(128, 8 * 64 * 64)
    ssh = f32(inp["state_shift"]).reshape(128, DS)
    ckw = f32(inp["cache_k_win"]).reshape(128, 128, 128)
    cvw = f32(inp["cache_v_win"]).reshape(128, 128, 128)
    cos_s, sin_s = _rope_tables(np.full((128,), 16384))
    for c in range(NCORE):
        b, j = c // 4, c % 4
        xe = np.zeros((NEXT, D), np.float32)
        n = SEG * (j + 1)
        xe[NEXT - n:] = xp[b, :n]
        pos = np.arange(SEG * j - 128, SEG * (j + 1))
        cos, sin = _rope_tables(pos)
        m = dict(shared)
        m["xext"] = xe
        m["c_cos"] = cos
        m["c_sin"] = sin
        m["c_mp0"] = consts["c_msl"] * (0.0 if j == 0 else 1.0)
        m.update(_sample_maps(c, xsm, swkv, ssh, ckw, cvw, cos_s, sin_s))
        in_maps.append(m)
    res = run_bass_kernel_spmd(nc, in_maps, core_ids=list(range(NCORE)))
    R = res.results
    y_prompt = np.zeros((2, 8192, D), np.float32)
    for c in range(len(R)):
        b, j = c // 4, c % 4
        y_prompt[b, j * SEG:(j + 1) * SEG] = np.asarray(R[c]["y_main"], dtype=np.float32)
    last = [min(3, len(R) - 1), min(7, len(R) - 1)]
    wkv_p = np.stack([np.asarray(R[c]["o_wkv"], dtype=np.float32) for c in last])[None]
    sh_p = np.stack([np.asarray(R[c]["o_shift"], dtype=np.float32).reshape(1, DS) for c in last])[None]
    kw_p = np.stack([np.asarray(R[c]["o_kwin"], dtype=np.float32).reshape(128, 2, 64) for c in last])[None]
    vw_p = np.stack([np.asarray(R[c]["o_vwin"], dtype=np.float32).reshape(128, 2, 64) for c in last])[None]
    nr = len(R)
    y_s = np.concatenate([np.asarray(R[c]["ys"], dtype=np.float32) for c in range(nr)], 0).reshape(16 * nr, 1, D)
    wkv_s = np.concatenate([np.asarray(R[c]["s_wkv"], dtype=np.float32).reshape(16, 8, 64, 64) for c in range(nr)], 0)[None]
    sh_s = np.concatenate([np.asarray(R[c]["s_shift"], dtype=np.float32).reshape(16, 1, DS) for c in range(nr)], 0)[None]
    kw_s = np.concatenate([np.asarray(R[c]["s_kwin"], dtype=np.float32).reshape(16, 128, 2, 64) for c in range(nr)], 0)[None]
    vw_s = np.concatenate([np.asarray(R[c]["s_vwin"], dtype=np.float32).reshape(16, 128, 2, 64) for c in range(nr)], 0)[None]
    return (y_prompt, y_s, wkv_p, sh_p, kw_p, vw_p, wkv_s, sh_s, kw_s, vw_s)
```
